# Optimizing a Trainium2 kernel written in Bass

```python
import math
import jax, jax.numpy as jnp
from jax import lax
import numpy as np

D_MODEL = 1024
BATCH = 4
SEQ = 8192
DEPTH = 1
DEC_BATCH = 128
DEC_SEQ = 4
PAST_LEN = 16384
PAGE_SIZE = 128

SSM_WIDTH = D_MODEL // 2
SSM_GROUP = 16
N_SSM_GROUPS = SSM_WIDTH // SSM_GROUP
SSM_STATE = 64
SSM_BLOCK = 128

HEAD_DIM = 64
ATTN_WIDTH = D_MODEL - SSM_WIDTH
N_HEADS = ATTN_WIDTH // HEAD_DIM
N_KV_HEADS = 2
GQA_GROUP = N_HEADS // N_KV_HEADS
Q_WIDTH = N_HEADS * HEAD_DIM
KV_WIDTH = N_KV_HEADS * HEAD_DIM
WINDOW = 128
ROPE_THETA = 500000.0
ROPE_DIM = HEAD_DIM // 4

N_MEM = 256
MEM_HEADS = 4
MEM_HEAD_DIM = D_MODEL // MEM_HEADS

D_FF = 2816
RMS_EPS = 1e-6
IN_WIDTH = SSM_WIDTH + Q_WIDTH + 2 * KV_WIDTH
NEG_INF = -1e30

kernel_name = "hymba_s5_swa_sink_macaron_memory_step"


def rms_norm(x, g):
    xf = x.astype(jnp.float32)
    y = xf * lax.rsqrt(jnp.mean(xf * xf, axis=-1, keepdims=True) + RMS_EPS)
    return (y * g.astype(jnp.float32)).astype(x.dtype)


def swiglu(x, w_gate, w_up, w_down):
    return (jax.nn.silu(x @ w_gate) * (x @ w_up)) @ w_down


def rope_partial(x, pos):
    half = ROPE_DIM // 2
    inv = ROPE_THETA ** (-jnp.arange(half, dtype=jnp.float32) * (2.0 / ROPE_DIM))
    ang = pos.astype(jnp.float32)[:, None] * inv[None, :]
    cos = jnp.cos(ang)[:, None, :]
    sin = jnp.sin(ang)[:, None, :]
    xr = x[..., :ROPE_DIM].astype(jnp.float32)
    x1, x2 = xr[..., :half], xr[..., half:]
    rot = jnp.concatenate([x1 * cos - x2 * sin, x2 * cos + x1 * sin], axis=-1)
    return jnp.concatenate([rot.astype(x.dtype), x[..., ROPE_DIM:]], axis=-1)


def sink_softmax(sc, sink):
    sink = sink.astype(jnp.float32)
    m = jnp.maximum(jnp.max(sc, axis=-1, keepdims=True), sink)
    e = jnp.exp(sc - m)
    return e / (jnp.sum(e, axis=-1, keepdims=True) + jnp.exp(sink - m))


def s5_discretise(a_re, a_im, log_step, b_re, b_im):
    f32 = jnp.float32
    a_re, a_im = a_re.astype(f32), a_im.astype(f32)
    dt = jnp.exp(log_step.astype(f32))[:, None]
    mag = jnp.exp(a_re * dt)
    lb_re, lb_im = mag * jnp.cos(a_im * dt), mag * jnp.sin(a_im * dt)
    den = a_re * a_re + a_im * a_im
    n_re, n_im = lb_re - 1.0, lb_im
    c_re = (n_re * a_re + n_im * a_im) / den
    c_im = (n_im * a_re - n_re * a_im) / den
    b_re, b_im = b_re.astype(f32), b_im.astype(f32)
    bb_re = c_re[..., None] * b_re - c_im[..., None] * b_im
    bb_im = c_re[..., None] * b_im + c_im[..., None] * b_re
    return lb_re, lb_im, bb_re, bb_im


def _ssm_combine(e1, e2):
    a1r, a1i, b1r, b1i = e1
    a2r, a2i, b2r, b2i = e2
    return (a1r * a2r - a1i * a2i, a1r * a2i + a1i * a2r,
            a2r * b1r - a2i * b1i + b2r, a2r * b1i + a2i * b1r + b2i)


def s5_scan(u, s0_re, s0_im, lb_re, lb_im, bb_re, bb_im, c_re, c_im, d):
    n, L, _ = u.shape
    blk = math.gcd(L, SSM_BLOCK)
    nblk = L // blk
    ub = u.reshape(n, nblk, blk, N_SSM_GROUPS, SSM_GROUP).swapaxes(0, 1)
    a_re = jnp.broadcast_to(lb_re, (n, blk, N_SSM_GROUPS, SSM_STATE))
    a_im = jnp.broadcast_to(lb_im, (n, blk, N_SSM_GROUPS, SSM_STATE))

    def step(carry, u_blk):
        sr, si = carry
        bu_re = jnp.einsum('nlgh,gph->nlgp', u_blk, bb_re)
        bu_im = jnp.einsum('nlgh,gph->nlgp', u_blk, bb_im)
        bu_re = bu_re.at[:, 0].add(lb_re * sr - lb_im * si)
        bu_im = bu_im.at[:, 0].add(lb_re * si + lb_im * sr)
        _, _, hr, hi = lax.associative_scan(_ssm_combine, (a_re, a_im, bu_re, bu_im), axis=1)
        y = (jnp.einsum('nlgp,ghp->nlgh', hr, c_re)
             - jnp.einsum('nlgp,ghp->nlgh', hi, c_im)
             + d[None, None] * u_blk)
        return (hr[:, -1], hi[:, -1]), y

    (sr, si), ys = lax.scan(step, (s0_re, s0_im), ub)
    y = ys.swapaxes(0, 1).reshape(n, L, SSM_WIDTH)
    return y, sr, si


def swa_prompt(q, k, v, sinks):
    n, s = q.shape[:2]
    nb = s // WINDOW
    qb = q.reshape(n, nb, WINDOW, N_KV_HEADS, GQA_GROUP, HEAD_DIM)

    def band(t):
        tb = t.reshape(n, nb, WINDOW, N_KV_HEADS, HEAD_DIM)
        prev = jnp.concatenate([jnp.zeros_like(tb[:, :1]), tb[:, :-1]], axis=1)
        return jnp.concatenate([prev, tb], axis=2)

    kk, vv = band(k), band(v)
    qi = jnp.arange(WINDOW)[:, None] + WINDOW
    kj = jnp.arange(2 * WINDOW)[None, :]
    diff = qi - kj
    blk = jnp.arange(nb)[:, None, None]
    mask = (diff >= 0) & (diff <= WINDOW) & (blk * WINDOW + kj - WINDOW >= 0)
    sc = jnp.einsum('bnqkgd,bnskd->bnkgqs', qb, kk).astype(jnp.float32) * (HEAD_DIM ** -0.5)
    sc = jnp.where(mask[None, :, None, None], sc, NEG_INF)
    pr = sink_softmax(sc, sinks.reshape(N_KV_HEADS, GQA_GROUP, 1, 1))
    o = jnp.einsum('bnkgqs,bnskd->bnqkgd', pr.astype(vv.dtype), vv)
    w = min(WINDOW, s)
    return o.reshape(n, s, Q_WIDTH), k[:, -w:], v[:, -w:]


def swa_sample(q, k, v, win_k, win_v, sinks):
    n, t = q.shape[:2]
    wb = win_k.shape[1]
    kk = jnp.concatenate([win_k.astype(k.dtype), k], axis=1)
    vv = jnp.concatenate([win_v.astype(v.dtype), v], axis=1)
    qb = q.reshape(n, t, N_KV_HEADS, GQA_GROUP, HEAD_DIM)
    diff = (wb + jnp.arange(t))[:, None] - jnp.arange(wb + t)[None, :]
    mask = (diff >= 0) & (diff <= WINDOW)
    sc = jnp.einsum('btkgd,bskd->bkgts', qb, kk).astype(jnp.float32) * (HEAD_DIM ** -0.5)
    sc = jnp.where(mask, sc, NEG_INF)
    pr = sink_softmax(sc, sinks.reshape(N_KV_HEADS, GQA_GROUP, 1, 1))
    o = jnp.einsum('bkgts,bskd->btkgd', pr.astype(vv.dtype), vv)
    return (o.reshape(n, t, Q_WIDTH), kk[:, t:].astype(win_k.dtype), vv[:, t:].astype(win_v.dtype))


def memory_kv(mem, g, w_k, w_v):
    n = mem.shape[0]
    m = rms_norm(mem, g)
    mk = (m @ w_k).reshape(n, N_MEM, MEM_HEADS, MEM_HEAD_DIM)
    mv = (m @ w_v).reshape(n, N_MEM, MEM_HEADS, MEM_HEAD_DIM)
    return mk, mv


def memory_attend(h, mk, mv, w_q, w_o):
    n, L, _ = h.shape
    q = (h @ w_q).reshape(n, L, MEM_HEADS, MEM_HEAD_DIM)
    sc = jnp.einsum('blhd,bmhd->bhlm', q, mk.astype(q.dtype)).astype(jnp.float32) * (MEM_HEAD_DIM ** -0.5)
    pr = jax.nn.softmax(sc, axis=-1)
    o = jnp.einsum('bhlm,bmhd->blhd', pr.astype(h.dtype), mv.astype(h.dtype))
    return o.reshape(n, L, D_MODEL) @ w_o


def decoder_layer(x, pos, s0_re, s0_im, win_k, win_v, mem_k, mem_v, p):
    f32 = jnp.float32
    n, L, _ = x.shape
    h = x
    h = h + 0.5 * rms_norm(swiglu(rms_norm(h, p['ffn1_pre_g']), p['ffn1_w_gate'], p['ffn1_w_up'],
                                  p['ffn1_w_down']), p['ffn1_post_g'])
    z = rms_norm(h, p['mix_pre_g']) @ p['w_in']
    u = z[..., :SSM_WIDTH]
    o1 = SSM_WIDTH + Q_WIDTH
    q = z[..., SSM_WIDTH:o1].reshape(n, L, N_HEADS, HEAD_DIM)
    k = z[..., o1:o1 + KV_WIDTH].reshape(n, L, N_KV_HEADS, HEAD_DIM)
    v = z[..., o1 + KV_WIDTH:].reshape(n, L, N_KV_HEADS, HEAD_DIM)
    lb_re, lb_im, bb_re, bb_im = s5_discretise(p['ssm_a_re'], p['ssm_a_im'], p['ssm_log_step'],
                                               p['ssm_b_re'], p['ssm_b_im'])
    y, s_re, s_im = s5_scan(u.astype(f32), s0_re.astype(f32), s0_im.astype(f32),
                            lb_re, lb_im, bb_re, bb_im,
                            p['ssm_c_re'].astype(f32), p['ssm_c_im'].astype(f32),
                            p['ssm_d'].astype(f32).reshape(N_SSM_GROUPS, SSM_GROUP))
    g = jax.nn.gelu(y)
    y_ssm = (g * jax.nn.sigmoid(g @ p['ssm_w_glu'].astype(f32) + p['ssm_b_glu'].astype(f32))).astype(x.dtype)
    q = rope_partial(q, pos)
    k = rope_partial(k, pos)
    if win_k is None:
        o, nk, nv = swa_prompt(q, k, v, p['attn_sinks'])
    else:
        o, nk, nv = swa_sample(q, k, v, win_k, win_v, p['attn_sinks'])
    mixed = jnp.concatenate([rms_norm(y_ssm, p['ssm_out_g']), rms_norm(o, p['attn_out_g'])], axis=-1) @ p['w_out']
    h = h + rms_norm(mixed, p['mix_post_g'])
    c = memory_attend(rms_norm(h, p['xa_pre_g']), mem_k, mem_v, p['w_mem_q'], p['w_mem_o'])
    h = h + rms_norm(c, p['xa_post_g'])
    h = h + 0.5 * rms_norm(swiglu(rms_norm(h, p['ffn2_pre_g']), p['ffn2_w_gate'], p['ffn2_w_up'],
                                  p['ffn2_w_down']), p['ffn2_post_g'])
    return h, s_re.astype(s0_re.dtype), s_im.astype(s0_im.dtype), nk, nv


def setup_inputs(seed: int = 0) -> dict:
    key = jax.random.key(seed)
    keys = jax.random.split(key, 64)
    ctr = [0]
    f32 = jnp.float32

    def nk():
        ctr[0] += 1
        return keys[ctr[0]]

    def normal(shape, scale):
        return scale * jax.random.normal(nk(), shape, f32)

    def gain(m):
        return 1.0 + normal((m,), 0.01)

    win = min(WINDOW, PAST_LEN)
    G, P, H = N_SSM_GROUPS, SSM_STATE, SSM_GROUP
    d_in = D_MODEL ** -0.5
    return {
        'x_prompt': normal((BATCH, SEQ, D_MODEL), 1.0),
        'x_sample': normal((DEC_BATCH, DEC_SEQ, D_MODEL), 1.0),
        'state_ssm_re': normal((DEC_BATCH, G, P), 0.3),
        'state_ssm_im': normal((DEC_BATCH, G, P), 0.3),
        'cache_swa_k': normal((DEC_BATCH, win, N_KV_HEADS, HEAD_DIM), 1.0),
        'cache_swa_v': normal((DEC_BATCH, win, N_KV_HEADS, HEAD_DIM), 1.0),
        'cache_mem_k': normal((DEC_BATCH, N_MEM, MEM_HEADS, MEM_HEAD_DIM), 1.0),
        'cache_mem_v': normal((DEC_BATCH, N_MEM, MEM_HEADS, MEM_HEAD_DIM), 1.0),
        'mem_prompt': normal((BATCH, N_MEM, D_MODEL), 1.0),
        'ffn1_pre_g': gain(D_MODEL),
        'ffn1_w_gate': normal((D_MODEL, D_FF), d_in),
        'ffn1_w_up': normal((D_MODEL, D_FF), d_in),
        'ffn1_w_down': normal((D_FF, D_MODEL), D_FF ** -0.5),
        'ffn1_post_g': gain(D_MODEL),
        'mix_pre_g': gain(D_MODEL),
        'w_in': normal((D_MODEL, IN_WIDTH), d_in),
        'ssm_a_re': -0.5 + normal((G, P), 0.01),
        'ssm_a_im': math.pi * jnp.broadcast_to(jnp.arange(P, dtype=f32), (G, P)) + normal((G, P), 0.01),
        'ssm_log_step': jax.random.uniform(nk(), (G,), f32, math.log(1e-3), math.log(1e-1)),
        'ssm_b_re': normal((G, P, H), (2 * H) ** -0.5),
        'ssm_b_im': normal((G, P, H), (2 * H) ** -0.5),
        'ssm_c_re': normal((G, H, P), (2 * P) ** -0.5),
        'ssm_c_im': normal((G, H, P), (2 * P) ** -0.5),
        'ssm_d': normal((SSM_WIDTH,), 1.0),
        'ssm_w_glu': normal((SSM_WIDTH, SSM_WIDTH), SSM_WIDTH ** -0.5),
        'ssm_b_glu': normal((SSM_WIDTH,), 0.01),
        'attn_sinks': normal((N_HEADS,), 0.5),
        'ssm_out_g': gain(SSM_WIDTH),
        'attn_out_g': gain(Q_WIDTH),
        'w_out': normal((SSM_WIDTH + Q_WIDTH, D_MODEL), (SSM_WIDTH + Q_WIDTH) ** -0.5),
        'mix_post_g': gain(D_MODEL),
        'mem_norm_g': gain(D_MODEL),
        'w_mem_q': normal((D_MODEL, D_MODEL), d_in),
        'w_mem_k': normal((D_MODEL, D_MODEL), d_in),
        'w_mem_v': normal((D_MODEL, D_MODEL), d_in),
        'w_mem_o': normal((D_MODEL, D_MODEL), d_in),
        'xa_pre_g': gain(D_MODEL),
        'xa_post_g': gain(D_MODEL),
        'ffn2_pre_g': gain(D_MODEL),
        'ffn2_w_gate': normal((D_MODEL, D_FF), d_in),
        'ffn2_w_up': normal((D_MODEL, D_FF), d_in),
        'ffn2_w_down': normal((D_FF, D_MODEL), D_FF ** -0.5),
        'ffn2_post_g': gain(D_MODEL),
    }


def reference(x_prompt, x_sample, state_ssm_re, state_ssm_im, cache_swa_k, cache_swa_v,
              cache_mem_k, cache_mem_v, mem_prompt,
              ffn1_pre_g, ffn1_w_gate, ffn1_w_up, ffn1_w_down, ffn1_post_g,
              mix_pre_g, w_in, ssm_a_re, ssm_a_im, ssm_log_step, ssm_b_re, ssm_b_im,
              ssm_c_re, ssm_c_im, ssm_d, ssm_w_glu, ssm_b_glu, attn_sinks,
              ssm_out_g, attn_out_g, w_out, mix_post_g,
              mem_norm_g, w_mem_q, w_mem_k, w_mem_v, w_mem_o, xa_pre_g, xa_post_g,
              ffn2_pre_g, ffn2_w_gate, ffn2_w_up, ffn2_w_down, ffn2_post_g):
    p = dict(ffn1_pre_g=ffn1_pre_g, ffn1_w_gate=ffn1_w_gate, ffn1_w_up=ffn1_w_up,
             ffn1_w_down=ffn1_w_down, ffn1_post_g=ffn1_post_g,
             mix_pre_g=mix_pre_g, w_in=w_in, ssm_a_re=ssm_a_re, ssm_a_im=ssm_a_im,
             ssm_log_step=ssm_log_step, ssm_b_re=ssm_b_re, ssm_b_im=ssm_b_im,
             ssm_c_re=ssm_c_re, ssm_c_im=ssm_c_im, ssm_d=ssm_d, ssm_w_glu=ssm_w_glu,
             ssm_b_glu=ssm_b_glu, attn_sinks=attn_sinks, ssm_out_g=ssm_out_g,
             attn_out_g=attn_out_g, w_out=w_out, mix_post_g=mix_post_g,
             w_mem_q=w_mem_q, w_mem_o=w_mem_o, xa_pre_g=xa_pre_g, xa_post_g=xa_post_g,
             ffn2_pre_g=ffn2_pre_g, ffn2_w_gate=ffn2_w_gate, ffn2_w_up=ffn2_w_up,
             ffn2_w_down=ffn2_w_down, ffn2_post_g=ffn2_post_g)
    n_p, s_p, _ = x_prompt.shape
    n_s, t_s, _ = x_sample.shape
    pm_k, pm_v = memory_kv(mem_prompt, mem_norm_g, w_mem_k, w_mem_v)
    h_p = x_prompt
    h_s = x_sample
    for _ in range(DEPTH):
        z0 = jnp.zeros((n_p, N_SSM_GROUPS, SSM_STATE), x_prompt.dtype)
        h_p, p_sre, p_sim, p_wk, p_wv = decoder_layer(
            h_p, jnp.arange(s_p, dtype=jnp.int32), z0, z0, None, None, pm_k, pm_v, p)
        h_s, s_sre, s_sim, s_wk, s_wv = decoder_layer(
            h_s, PAST_LEN + jnp.arange(t_s, dtype=jnp.int32), state_ssm_re, state_ssm_im,
            cache_swa_k, cache_swa_v, cache_mem_k, cache_mem_v, p)
    return (h_p, h_s, p_sre, p_sim, p_wk, p_wv, pm_k, pm_v, s_sre, s_sim, s_wk, s_wv)
```

```python
import contextlib
import math
import numpy as np
import concourse.bass as bass
import concourse.mybir as mybir
from concourse.bass_utils import run_bass_kernel_spmd

F32 = mybir.dt.float32
BF16 = mybir.dt.bfloat16
AF = mybir.ActivationFunctionType
ALU = mybir.AluOpType

D = 1024
DFF = 2816
NCORES = 8
SEQ = 8192
HALF = SEQ // 2
EPS = 1e-6
ENGS = ("pe", "act", "dve", "pool", "sp")


class Buf:
    __slots__ = ("w", "r")

    def __init__(self):
        self.w = None
        self.r = []


class Sched:
    def __init__(self, nc):
        self.nc = nc
        self.ops = {e: [] for e in ENGS}
        self.cnt = {e: 0 for e in ENGS}
        self.seen = {e: {} for e in ENGS}
        self.dcnt = {}
        self.bufs = {}
        self.last_ev = {}
        self.alias = {}

    def _exp(self, names):
        out = []
        for n in names:
            a = self.alias.get(n)
            if a is None:
                out.append(n)
            else:
                out.extend(a)
        return out

    def buf(self, name):
        b = self.bufs.get(name)
        if b is None:
            b = Buf()
            self.bufs[name] = b
        return b

    def _deps(self, eng, reads, writes):
        reads = self._exp(reads)
        writes = self._exp(writes)
        deps = []
        for n in reads:
            b = self.buf(n)
            if b.w is not None:
                deps.append(b.w)
            if n.startswith("ps"):
                deps.extend(ev for ev in b.r if ev[0] != eng)
        for n in writes:
            b = self.buf(n)
            if b.w is not None:
                deps.append(b.w)
            deps.extend(b.r)
        seen = self.seen[eng]
        best = {}
        for (k, v) in deps:
            if k == "pe" and eng == "pe":
                continue
            if seen.get(k, 0) >= v:
                continue
            if best.get(k, 0) < v:
                best[k] = v
        for k, v in best.items():
            seen[k] = v
        return list(best.items())

    def _commit(self, ev, reads, writes):
        reads = self._exp(reads)
        writes = self._exp(writes)
        self.last_ev[ev[0]] = ev[1]
        for n in writes:
            b = self.buf(n)
            b.w = ev
            b.r = []
        for n in reads:
            b = self.buf(n)
            b.r.append(ev)
            if len(b.r) > 48:
                best = {}
                for k, v in b.r:
                    if best.get(k, 0) < v:
                        best[k] = v
                b.r = list(best.items())

    def op(self, eng, fn, reads=(), writes=()):
        waits = self._deps(eng, reads, writes)
        self.cnt[eng] += 1
        ev = (eng, self.cnt[eng])
        self.ops[eng].append((waits, fn, (eng, 1)))
        self._commit(ev, reads, writes)
        return ev

    def dma(self, q, out, in_, key, reads=(), writes=(), **kw):
        waits = self._deps(q, reads, writes)
        n = self.dcnt.get(key, 0) + 1
        self.dcnt[key] = n
        ev = ("d:" + key, 16 * n)

        def fn(e, out=out, in_=in_, kw=kw):
            return e.dma_start(out=out, in_=in_, **kw)

        self.ops[q].append((waits, fn, ("d:" + key, 16)))
        self._commit(ev, reads, writes)
        return ev

    def barrier(self):
        for e in ENGS:
            waits = []
            for k, v in self.last_ev.items():
                if k == e and e == "pe":
                    continue
                if self.seen[e].get(k, 0) < v:
                    self.seen[e][k] = v
                    waits.append((k, v))
            if waits:
                self.ops[e].append((waits, None, None))

    def emit(self, stack):
        nc = self.nc
        keys = set(["pe", "act", "dve", "pool"])
        for k in self.dcnt:
            keys.add("d:" + k)
        sems = {}
        for k in sorted(keys):
            sems[k] = stack.enter_context(nc.semaphore("s_" + k.replace(":", "_")))
        block = stack.enter_context(nc.Block())
        ops = self.ops

        def run(e, lst):
            for waits, fn, inc in lst:
                for (k, v) in waits:
                    e.wait_ge(sems[k], v)
                if fn is not None:
                    ins = fn(e)
                    if inc is not None:
                        ins.then_inc(sems[inc[0]], inc[1])

        @block.tensor
        def _(e):
            run(e, ops["pe"])

        @block.scalar
        def _(e):
            run(e, ops["act"])

        @block.vector
        def _(e):
            run(e, ops["dve"])

        @block.gpsimd
        def _(e):
            run(e, ops["pool"])

        @block.sync
        def _(e):
            run(e, ops["sp"])
        return len(sems)


class K:
    def __init__(self, cfg):
        self.cfg = cfg
        nd = cfg.get("num_devices")
        self.nc = bass.Bass("TRN2", target_bir_lowering=False, num_devices=nd) if nd else bass.Bass("TRN2", target_bir_lowering=False)
        self.S = Sched(self.nc)
        self.st = contextlib.ExitStack()
        self.din = {}
        self.dout = {}
        self.uid = 0

    def inp(self, name, shape, dt=F32):
        t = self.nc.dram_tensor(name, list(shape), dt, kind="ExternalInput").ap()
        self.din[name] = t
        return t

    def outp(self, name, shape, dt=F32):
        t = self.nc.dram_tensor(name, list(shape), dt, kind="ExternalOutput").ap()
        self.dout[name] = t
        return t

    def scratch(self, name, shape, dt):
        return self.nc.dram_tensor(name, list(shape), dt, kind="Internal").ap()

    def sb(self, name, shape, dt):
        return self.st.enter_context(self.nc.sbuf_tensor(name, list(shape), dt))

    def ps(self, name, shape, dt=F32):
        return self.st.enter_context(self.nc.psum_tensor(name, list(shape), dt))


SLOT = 3072
NSLOT = 3
STG = 1024
TWO_PI = 2.0 * math.pi

GV = {"ffn1_pre": 0, "ffn1_post": 8, "mix_pre": 16, "mix_post": 24, "xa_pre": 32, "xa_post": 40,
      "ffn2_pre": 48, "ffn2_post": 56, "mem_norm": 64, "out_g": 72, "ssm_d": 84, "b_glu": 88, "sinks": 92}
GVN = 104


class WSpec:
    def __init__(self, name, vmts, KT, mpp, gcols=None):
        self.name = name
        self.vmts = vmts
        self.KT = KT
        self.mpp = mpp
        self.gcols = gcols
        self.npan = (len(vmts) + mpp - 1) // mpp
        self.cast_done = set()
        self.scr = None


def build(cfg):
    kb = K(cfg)
    nc, S = kb.nc, kb.S
    NB = cfg["NB"]
    NTOK = NB * 512
    NCK = NTOK // 16
    stage = cfg.get("stage", "full")
    use_cc = cfg.get("collective", True)
    NCC = cfg.get("ncc", NCORES)
    do_sample = cfg.get("sample", True)
    dbg = cfg.get("dbg", False) or cfg.get("stage", "full").startswith("dbg")
    I32 = mybir.dt.int32

    xp = kb.inp("xp", [128 + NTOK, D])
    xs = kb.inp("xs", [64, D])
    xprev = kb.inp("xprev", [NTOK, D])
    ident_d = kb.inp("ident", [128, 128])
    gv = kb.inp("gvecs", [128, GVN])
    masks_d = kb.inp("masks", [128, 5, 128])
    ropeC_d = kb.inp("ropeC", [128, 128 + NTOK])
    ropeS_d = kb.inp("ropeS", [128, 128 + NTOK])
    ropeCs_d = kb.inp("ropeCs", [128, 64])
    ropeSs_d = kb.inp("ropeSs", [128, 64])
    sel_d = kb.inp("sel", [128, 8])
    w = {}
    for nm, shp in [("ffn1_w_gate", [D, DFF]), ("ffn1_w_up", [D, DFF]), ("ffn1_w_down", [DFF, D]),
                    ("ffn2_w_gate", [D, DFF]), ("ffn2_w_up", [D, DFF]), ("ffn2_w_down", [DFF, D]),
                    ("w_in_p", [D, 1920]), ("ssm_w_glu", [512, 512]), ("w_out", [D, D]),
                    ("w_mem_q", [D, D]), ("w_mem_k", [D, D]), ("w_mem_v", [D, D]), ("w_mem_o", [D, D]),
                    ("ssm_pack", [128, 16, 67]),
                    ("st_re", [16, 32, 64]), ("st_im", [16, 32, 64]),
                    ("swa_k", [16, 128, 128]), ("swa_v", [16, 128, 128]),
                    ("memk", [16, 256, D]), ("memv", [16, 256, D]), ("mem_p", [256, D])]:
        w[nm] = kb.inp(nm, shp)
    y_out = kb.outp("y_p", [NTOK, D])
    ys_out = kb.outp("y_s", [64, D])
    o_psre = kb.outp("p_sre", [32, 64])
    o_psim = kb.outp("p_sim", [32, 64])
    o_pwk = kb.outp("p_wk", [128, 128])
    o_pwv = kb.outp("p_wv", [128, 128])
    o_pmk = kb.outp("pm_k", [256, D])
    o_pmv = kb.outp("pm_v", [256, D])
    o_ssre = kb.outp("s_sre", [16, 32, 64])
    o_ssim = kb.outp("s_sim", [16, 32, 64])
    o_swk = kb.outp("s_wk", [16, 128, 128])
    o_swv = kb.outp("s_wv", [16, 128, 128])
    dbg_o = kb.outp("dbg", [128, 8, 512]) if dbg else None
    h1_scr = kb.scratch("h1_scr", [NB, 128, 4096], F32)
    q_scr = kb.scratch("q_scr", [NB, 128, 2048], BF16)
    k_scr = kb.scratch("k_scr", [128, 128 + NTOK], BF16)
    v_scr = kb.scratch("v_scr", [1 + NB * 4, 128, 128], BF16)
    u_scr = kb.scratch("u_scr", [128, 4, NTOK], BF16)
    uprev_scr = kb.scratch("uprev_scr", [128, 4, NTOK], BF16)
    hb_scr = kb.scratch("hb_scr", [128, 16, 2, NCK], BF16)
    cc_in = kb.scratch("cc_in", [128, 32], F32)
    cc_out = kb.scratch("cc_out", [NCC * 128, 32], F32)

    A = kb.sb("A", [128, 4096], F32)
    B = kb.sb("B", [128, 4096], F32)
    C = kb.sb("C", [128, 11264], BF16)
    Dd = kb.sb("Dd", [128, 4096], BF16)
    R = kb.sb("R", [128, 24576], BF16)
    wslot = [kb.sb(f"wslot{i}", [128, SLOT], BF16) for i in range(NSLOT)]
    wstages = [kb.sb(f"wstage{i}", [128, STG], F32) for i in range(2)]
    xtok = [kb.sb(f"xtok{i}", [128, D], F32) for i in range(2)]
    ident = kb.sb("ident_sb", [128, 128], F32)
    ident_bf = kb.sb("ident_bf", [128, 128], BF16)
    ones_bf = kb.sb("ones_bf", [128, 128], BF16)
    gvs = kb.sb("gvs", [128, GVN], F32)
    cst = kb.sb("cst", [128, 8], F32)
    masks = kb.sb("masks_b", [128, 5, 128], BF16)
    sqb = [kb.sb(f"sqb{i}", [128, 512], BF16) for i in range(4)]
    sg = [kb.sb(f"sg{i}", [128, 512], BF16) for i in range(2)]
    rstd = kb.sb("rstd", [128, 512], F32)
    rc = kb.sb("rc", [128, 512], F32)
    rs = kb.sb("rs", [128, 512], F32)
    kq = kb.sb("kq", [128, 2, 512], F32)
    krot_b = kb.sb("krot_b", [128, 512], BF16)
    u_bf = kb.sb("u_bf", [128, 4, 512], BF16)
    qrot = kb.sb("qrot", [128, 4, 512], BF16)
    v_bf = kb.sb("v_bf", [128, 5, 128], BF16)
    v_f = kb.sb("v_f", [128, 128], F32)
    kblk = kb.sb("kblk", [128, 640], BF16)
    pt = kb.sb("ptile", [128, 4, 512], BF16)
    hbk = kb.sb("hbk", [128, 16, 2, 32], BF16)
    sinkexp = kb.sb("sinkexp", [64, 8], F32)
    att_t = kb.sb("att_t", [64, 512], F32)
    mkT = kb.sb("mkT", [128, 8, 256], BF16)
    mv_b = kb.sb("mv_b", [128, 2, D], BF16)
    sp = kb.sb("ssm_p", [128, 16, 24], F32)
    PWr = kb.sb("PWr", [128, 16, 17], F32)
    PWi = kb.sb("PWi", [128, 16, 17], F32)
    BBr = kb.sb("BBr", [128, 16, 16], F32)
    BBi = kb.sb("BBi", [128, 16, 16], F32)
    CRt = kb.sb("CRt", [128, 16, 16], F32)
    CIt = kb.sb("CIt", [128, 16, 16], F32)
    sti = kb.sb("sti", [128, 16], I32)
    AS = kb.sb("AS", [128, 3, 9, 16], F32)
    sin_f = kb.sb("sin_f", [128, 2, 16], F32)
    selt = kb.sb("selt", [128, 8], F32)
    h1s = kb.sb("h1s", [128, 8, 64], F32)
    hs_b = kb.sb("hs_b", [128, 16, 2, 16], BF16)
    us_b = kb.sb("us_b", [128, 4, 64], BF16)
    qs_b = kb.sb("qs_b", [128, 4, 64], BF16)
    ks_b = kb.sb("ks_b", [128, 64], BF16)
    vs_b = kb.sb("vs_b", [64, 128], BF16)
    PS = [kb.ps(f"ps{i}", [128, 512], F32) for i in range(8)]

    SPI = {"ar": 0, "ai": 1, "ls": 2, "dt": 3, "ang": 4, "mag": 5, "sn": 6, "cs": 7, "lbr": 8, "lbi": 9,
           "cre": 10, "cim": 11, "t0": 12, "t1": 13, "t2": 14, "t3": 15, "nlbi": 16}

    def spc(nm):
        return sp[:, :, SPI[nm]]

    hT = A[:].rearrange("p (k n) -> p k n", n=512)
    cT = B[:].rearrange("p (k n) -> p k n", n=512)
    Cb = C[:]
    Cf = C[:].bitcast(F32)
    hmid = Cb.rearrange("p (k n) -> p k n", n=512)
    xn = Dd[:].rearrange("p (k n) -> p k n", n=512)

    S.dma("sp", ident[:], ident_d, "c_ident", writes=["ident"])
    S.dma("sp", gvs[:], gv, "c_gv", writes=["gvs"])
    S.dma("sp", xtok[0][:, 0:640].rearrange("p (a b) -> p a b", b=128), masks_d, "c_masks", writes=["xtok0"])
    S.dma("sp", selt[:], sel_d, "c_sel", writes=["selt"])
    S.op("pool", lambda e: e.memset(ones_bf[:], 1.0), writes=["ones"])
    S.op("pool", lambda e: e.memset(cst[:, 0:1], EPS), writes=["cst"])
    S.op("pool", lambda e: e.memset(cst[:, 1:2], math.log(0.5)), writes=["cst"])
    S.op("pool", lambda e: e.memset(cst[:, 2:3], 0.0), writes=["cst"])
    S.op("pool", lambda e: e.memset(cst[:, 3:4], 1.0), writes=["cst"])
    S.op("dve", lambda e: e.tensor_copy(masks[:], xtok[0][:, 0:640].rearrange("p (a b) -> p a b", b=128)), reads=["xtok0"], writes=["masks"])
    S.op("dve", lambda e: e.tensor_copy(ident_bf[:], ident[:]), reads=["ident"], writes=["ident_bf"])
    S.op("act", lambda e: e.activation(sinkexp[:], gvs[0:64, GV["sinks"]:GV["sinks"] + 8], AF.Exp), reads=["gvs"], writes=["sinkexp"])

    wstate = {"slot": 0, "stg": 0, "pending": []}

    def flush_pending(keep=0):
        while len(wstate["pending"]) > keep:
            wstate["pending"].pop(0)()


    def std_vmts(src, KT, ncols):
        vm = []
        for mt in range(ncols // 128):
            pieces = []
            k0 = 0
            while k0 < KT:
                kn = min(8, KT - k0)
                pieces.append((src[k0 * 128:(k0 + kn) * 128, mt * 128:(mt + 1) * 128].rearrange("(kt p) m -> p kt m", p=128), k0, kn, 128))
                k0 += kn
            vm.append(pieces)
        return vm

    specs = {}

    def add_spec(name, vmts, KT, mpp, gcols=None):
        sp_ = WSpec(name, vmts, KT, mpp, gcols)
        sp_.scr = kb.scratch("scr_" + name, [sp_.npan, 128, SLOT], BF16)
        specs[name] = sp_
        return sp_

    for pfx, gk in (("ffn1", "ffn1_pre"), ("ffn2", "ffn2_pre")):
        g_, u_ = std_vmts(w[pfx + "_w_gate"], 8, DFF), std_vmts(w[pfx + "_w_up"], 8, DFF)
        vm = []
        for mt in range(22):
            vm.append(g_[mt])
            vm.append(u_[mt])
        add_spec(pfx + "_gu", vm, 8, 2, gvs[:, GV[gk]:GV[gk] + 8])
        add_spec(pfx + "_dn", std_vmts(w[pfx + "_w_down"], 22, D), 22, 1)
    add_spec("w_in", std_vmts(w["w_in_p"], 8, 1920), 8, 3, gvs[:, GV["mix_pre"]:GV["mix_pre"] + 8])
    add_spec("w_glu", std_vmts(w["ssm_w_glu"], 4, 512), 4, 4)
    vm = []
    for mt in range(8):
        vm.append([(w["w_out"][0:512, mt * 128:(mt + 1) * 128].rearrange("(kt p) m -> p kt m", p=128), 0, 4, 128),
                   (w["w_out"][512:1024, mt * 128:(mt + 1) * 128].rearrange("(kt p) m -> p kt m", p=64), 4, 8, 64)])
    add_spec("w_out", vm, 12, 2, gvs[:, GV["out_g"]:GV["out_g"] + 12])
    add_spec("w_mem_q", std_vmts(w["w_mem_q"], 8, D), 8, 3, gvs[:, GV["xa_pre"]:GV["xa_pre"] + 8])
    add_spec("w_mem_k", std_vmts(w["w_mem_k"], 8, D), 8, 3, gvs[:, GV["mem_norm"]:GV["mem_norm"] + 8])
    add_spec("w_mem_v", std_vmts(w["w_mem_v"], 8, D), 8, 3, gvs[:, GV["mem_norm"]:GV["mem_norm"] + 8])
    add_spec("w_mem_o", std_vmts(w["w_mem_o"], 8, D), 8, 3)

    def get_panel(spec, pn):
        si = wstate["slot"]
        wstate["slot"] = (si + 1) % NSLOT
        slot = wslot[si]
        sname = f"wslot{si}"
        nv = min(spec.mpp, len(spec.vmts) - pn * spec.mpp)
        KT = spec.KT
        used = nv * KT * 128
        scrname = "scr_" + spec.name + str(pn)
        if pn in spec.cast_done:
            flush_pending()
            S.dma("sp", slot[:, 0:used], spec.scr[pn, :, 0:used], sname, reads=[scrname], writes=[sname])
        else:
            spec.cast_done.add(pn)
            flush_pending(keep=1)
            for ml in range(nv):
                vmt = spec.vmts[pn * spec.mpp + ml]
                for (src, klo, kn, prows) in vmt:
                    gi = wstate["stg"]
                    wstate["stg"] = 1 - gi
                    wstage, wsn = wstages[gi], f"wstage{gi}"
                    S.dma("sp", wstage[0:prows, 0:kn * 128].rearrange("p (k m) -> p k m", m=128), src,
                          wsn, writes=[wsn])
                    dst = slot[0:prows, (ml * KT + klo) * 128:(ml * KT + klo + kn) * 128]
                    ceng = "pool" if (gi == 0 or wstate.get("pool_only")) else "dve"
                    if spec.gcols is not None:
                        gc = spec.gcols[0:prows, klo:klo + kn]
                        S.op(ceng, lambda e, dst=dst, gc=gc, kn=kn, prows=prows, wstage=wstage: e.tensor_tensor(
                            dst.rearrange("p (k m) -> p k m", m=128), wstage[0:prows, 0:kn * 128].rearrange("p (k m) -> p k m", m=128),
                            gc.unsqueeze(2).to_broadcast([prows, kn, 128]), ALU.mult),
                            reads=[wsn, "gvs"], writes=[sname])
                    else:
                        S.op(ceng, lambda e, dst=dst, kn=kn, prows=prows, wstage=wstage: e.tensor_copy(dst, wstage[0:prows, 0:kn * 128]),
                             reads=[wsn], writes=[sname])
            wstate["pending"].append(lambda spec=spec, pn=pn, slot=slot, used=used, si=si, sname=sname, scrname=scrname:
                                     S.dma("sp", spec.scr[pn, :, 0:used], slot[:, 0:used], "scrw%d" % si, reads=[sname], writes=[scrname]))
        return slot, sname

    def precast_phase2_weights():
        wstate["pool_only"] = True
        for nm in ("w_glu", "w_out", "w_mem_q", "w_mem_o", "ffn2_gu", "ffn2_dn"):
            sp_ = specs[nm]
            for pn in range(sp_.npan):
                if pn not in sp_.cast_done:
                    get_panel(sp_, pn)
        flush_pending()
        wstate["pool_only"] = False

    psrr = {"i": 0}

    def next_ps():
        i = psrr["i"]
        psrr["i"] = (i + 1) % 6
        return PS[i], f"ps{i}"

    def mm(out, lhsT, rhs, start, stop, reads, writes, **kw):
        S.op("pe", lambda e: e.matmul(out, lhsT, rhs, start=start, stop=stop, **kw), reads=reads, writes=writes)

    def rms_stats(tiles, srcname, N, P=128, half=False, nfeat=None):
        KT = len(tiles)
        ptile, pname = PS[7], "ps7"
        SQ_ENG = ("dve", "act", "pool", "act", "act", "act", "dve", "act")
        for kt, t in enumerate(tiles):
            sq = sqb[kt % 4]
            sqn = f"sqb{kt % 4}"
            srcname_ = srcname if isinstance(srcname, str) else srcname[kt]
            eng_ = SQ_ENG[kt % 8]
            if eng_ == "act":
                S.op("act", lambda e, t=t, sq=sq: e.activation(sq[0:P, 0:N], t, AF.Square), reads=[srcname_], writes=[sqn])
            else:
                S.op(eng_, lambda e, t=t, sq=sq: e.tensor_tensor(sq[0:P, 0:N], t, t, ALU.mult), reads=[srcname_], writes=[sqn])
            mm(ptile[:, 0:N], ones_bf[0:P, :], sq[0:P, 0:N], kt == 0, kt == KT - 1, [sqn, "ones"], [pname])
        nf = nfeat if nfeat is not None else KT * P
        S.op("act", lambda e: e.activation(rstd[:, 0:N], ptile[:, 0:N], AF.Ln, bias=cst[:, 0:1], scale=1.0 / nf),
             reads=[pname, "cst"], writes=["rstd"])
        bc = cst[:, 1:2] if half else cst[:, 2:3]
        S.op("act", lambda e: e.activation(rstd[:, 0:N], rstd[:, 0:N], AF.Exp, bias=bc, scale=-0.5),
             reads=["rstd", "cst"], writes=["rstd"])

    def norm_to_bf(dst3, dstname, src3, srcname, KT, N, P=128):
        h_ = (KT * 5) // 8 if KT >= 8 else (KT * 3) // 4
        S.op("dve", lambda e: e.tensor_tensor(dst3[:, 0:h_, :], src3[:, 0:h_, :], rstd[0:P, 0:N].unsqueeze(1).to_broadcast([P, h_, N]), ALU.mult),
             reads=[srcname, "rstd"], writes=[dstname + "_a"])
        S.op("pool", lambda e: e.tensor_tensor(dst3[:, h_:KT, :], src3[:, h_:KT, :], rstd[0:P, 0:N].unsqueeze(1).to_broadcast([P, KT - h_, N]), ALU.mult),
             reads=[srcname, "rstd"], writes=[dstname + "_b"])

    def post_residual(hname, gcol0, N, half):
        rms_stats([cT[:, kt, 0:N] for kt in range(8)], [f"cT{kt}" for kt in range(8)], N, half=half)
        S.op("dve", lambda e: e.tensor_tensor(cT[:, 0:4, 0:N], cT[:, 0:4, 0:N], rstd[:, 0:N].unsqueeze(1).to_broadcast([128, 4, N]), ALU.mult),
             reads=["cT_a", "rstd"], writes=["cT_a"])
        S.op("pool", lambda e: e.tensor_tensor(cT[:, 4:8, 0:N], cT[:, 4:8, 0:N], rstd[:, 0:N].unsqueeze(1).to_broadcast([128, 4, N]), ALU.mult),
             reads=["cT_b", "rstd"], writes=["cT_b"])
        for kt in range(8):
            S.op("dve", lambda e, kt=kt: e.scalar_tensor_tensor(hT[:, kt, 0:N], cT[:, kt, 0:N], gvs[:, gcol0 + kt:gcol0 + kt + 1],
                                                                hT[:, kt, 0:N], ALU.mult, ALU.add),
                 reads=[f"cT{kt}", "gvs", f"hT{kt}"], writes=[f"hT{kt}"])

    def linear(spec, rhs_fn, KT, N, evac, vm_sel=None, rnames=("xn",)):
        for pn in range(spec.npan):
            nv = min(spec.mpp, len(spec.vmts) - pn * spec.mpp)
            if vm_sel is not None and not any(vm_sel(pn * spec.mpp + ml) for ml in range(nv)):
                continue
            slot, sname = get_panel(spec, pn)
            for ml in range(nv):
                vmt = pn * spec.mpp + ml
                if vm_sel is not None and not vm_sel(vmt):
                    continue
                r = evac(vmt, None, None, slot, sname, ml)
                if r:
                    continue
                pp, ppn = next_ps()
                for kt in range(KT):
                    rap, kk = rhs_fn(kt)
                    mm(pp[:, 0:N], slot[0:kk, (ml * spec.KT + kt) * 128:(ml * spec.KT + kt + 1) * 128], rap,
                       kt == 0, kt == KT - 1, [sname] + list(rnames), [ppn])
                evac(vmt, pp, ppn, slot, sname, ml)

    def ffn(pfx, gpost, N):
        gu, dnp = specs[pfx + "_gu"], specs[pfx + "_dn"]
        rms_stats([hT[:, kt, 0:N] for kt in range(8)], [f"hT{kt}" for kt in range(8)], N)
        norm_to_bf(xn[:, :, 0:N], "xn", hT[:, :, 0:N], "hT", 8, N)
        state = {}

        def ev_gu(vmt, pp, ppn, slot, sname, ml):
            if pp is None:
                return False
            mt = vmt // 2
            if vmt % 2 == 0:
                sgi = mt % 2
                S.op("act", lambda e: e.activation(sg[sgi][:, 0:N], pp[:, 0:N], AF.Silu), reads=[ppn], writes=[f"sg{sgi}"])
            else:
                sgi = mt % 2
                S.op("dve", lambda e: e.tensor_tensor(hmid[:, mt, 0:N], sg[sgi][:, 0:N], pp[:, 0:N], ALU.mult),
                     reads=[ppn, f"sg{sgi}"], writes=["hmid"])
            return False

        linear(gu, lambda kt: (xn[:, kt, 0:N], 128), 8, N, ev_gu)

        def ev_dn(vmt, pp, ppn, slot, sname, ml):
            if pp is None:
                return False
            S.op("act", lambda e: e.activation(cT[:, vmt, 0:N], pp[:, 0:N], AF.Copy), reads=[ppn], writes=[f"cT{vmt}"])
            return False

        linear(dnp, lambda kt: (hmid[:, kt, 0:N], 128), 22, N, ev_dn, rnames=("hmid",))
        post_residual("hT", gpost, N, True)

    xt_i = {"i": 0}

    xpre = {}

    def x_load(src_rows, s, W):
        xi = xt_i["i"]
        xt_i["i"] = 1 - xi
        S.dma("sp", xtok[xi][0:W, :], src_rows[s * W:(s + 1) * W, :], f"xtok{xi}", writes=[f"xtok{xi}"])
        return xi

    def load_block_T(src_rows, nsub, W=128, key=None, nxt=None):
        pre = xpre.pop(key, []) if key is not None else []
        for s in range(nsub):
            xi = pre[s] if s < len(pre) else x_load(src_rows, s, W)
            xt, xtn = xtok[xi], f"xtok{xi}"
            for kq_ in range(2):
                pp, ppn = next_ps()
                for k4 in range(4):
                    kt = kq_ * 4 + k4
                    S.op("pe", lambda e, pp=pp, k4=k4, kt=kt, xt=xt: e.transpose(pp[:, k4 * 128:k4 * 128 + W], xt[0:W, kt * 128:(kt + 1) * 128], ident[0:W, 0:W]),
                         reads=[xtn, "ident"], writes=[ppn])
                S.op("act", lambda e, pp=pp, kq_=kq_, s=s: e.activation(
                    hT[:, kq_ * 4:(kq_ + 1) * 4, s * W:(s + 1) * W], pp[:].rearrange("p (k n) -> p k n", n=128)[:, :, 0:W], AF.Copy),
                    reads=[ppn], writes=["hT"])
        if nxt is not None:
            xpre[nxt[3]] = [x_load(nxt[0], s, nxt[2]) for s in range(min(2, nxt[1]))]

    def store_block_T(dst_rows, nsub, W=128):
        for s in range(nsub):
            xi = xt_i["i"]
            xt_i["i"] = 1 - xi
            xt, xtn = xtok[xi], f"xtok{xi}"
            for kq_ in range(2):
                pp, ppn = next_ps()
                for k4 in range(4):
                    kt = kq_ * 4 + k4
                    S.op("pe", lambda e, pp=pp, k4=k4, kt=kt, s=s: e.transpose(pp[0:W, k4 * 128:(k4 + 1) * 128], hT[:, kt, s * W:(s + 1) * W], ident[:]),
                         reads=["hT", "ident"], writes=[ppn])
                S.op("act", lambda e, pp=pp, kq_=kq_, xt=xt: e.activation(xt[0:W, kq_ * 512:(kq_ + 1) * 512], pp[0:W, :], AF.Copy),
                     reads=[ppn], writes=[xtn])
            S.dma("sp", dst_rows[s * W:(s + 1) * W, :], xt[0:W, :], "ystore%d" % xi, reads=[xtn], writes=["yout"])

    def in_proj(N, nsub, W, ropec, ropes, need_q, tok0=None, sample=False):
        rms_stats([hT[:, kt, 0:N] for kt in range(8)], [f"hT{kt}" for kt in range(8)], N)
        norm_to_bf(xn[:, :, 0:N], "xn", hT[:, :, 0:N], "hT", 8, N)
        S.dma("sp", rc[:, 0:N], ropec, "rope", writes=["rc"])
        S.dma("sp", rs[:, 0:N], ropes, "rope", writes=["rs"])

        def ev(vmt, pp, ppn, slot, sname, ml):
            if vmt == 14:
                if pp is not None:
                    return False
                ppv, ppvn = next_ps()
                for s in range(nsub):
                    for kt in range(8):
                        mm(ppv[0:W, s * 128:(s + 1) * 128], xn[:, kt, s * W:(s + 1) * W], slot[:, (ml * 8 + kt) * 128:(ml * 8 + kt + 1) * 128],
                           kt == 0, kt == 7, [sname, "xn"], [ppvn])
                S.op("act", lambda e: e.activation(v_bf[0:W, 1:1 + nsub, :], ppv[0:W, 0:nsub * 128].rearrange("p (s m) -> p s m", m=128), AF.Copy),
                     reads=[ppvn], writes=["v_bf"])
                S.op("act", lambda e: e.activation(v_f[0:W, :], ppv[0:W, (nsub - 1) * 128:nsub * 128], AF.Copy), reads=[ppvn], writes=["v_f"])
                return True
            if pp is None:
                return False
            if vmt < 4:
                S.op("act", lambda e: e.activation(u_bf[:, vmt, 0:N], pp[:, 0:N], AF.Copy), reads=[ppn], writes=["u_bf"])
            elif vmt < 8:
                S.op("act", lambda e: e.activation(cT[:, vmt - 4, 0:N], pp[:, 0:N], AF.Copy), reads=[ppn], writes=["cT"])
            elif vmt == 8:
                S.op("act", lambda e: e.activation(kq[:, 0, 0:N], pp[:, 0:N], AF.Copy), reads=[ppn], writes=["kq"])
            elif vmt < 13:
                S.op("act", lambda e: e.activation(cT[:, vmt - 5, 0:N], pp[:, 0:N], AF.Copy), reads=[ppn], writes=["cT"])
            else:
                S.op("act", lambda e: e.activation(kq[:, 1, 0:N], pp[:, 0:N], AF.Copy), reads=[ppn], writes=["kq"])
            return False

        sel = None if need_q else (lambda v: v in (8, 13, 14))
        linear(specs["w_in"], lambda kt: (xn[:, kt, 0:N], 128), 8, N, ev, vm_sel=sel)
        if need_q:
            S.op("dve", lambda e: e.tensor_tensor(cT[:, 0:4, 0:N], cT[:, 0:4, 0:N], rc[:, 0:N].unsqueeze(1).to_broadcast([128, 4, N]), ALU.mult),
                 reads=["cT", "rc"], writes=["cT"])
            S.op("pool", lambda e: e.tensor_tensor(cT[:, 4:8, 0:N], cT[:, 4:8, 0:N], rs[:, 0:N].unsqueeze(1).to_broadcast([128, 4, N]), ALU.mult),
                 reads=["cT", "rs"], writes=["cT"])
            S.op("dve", lambda e: e.tensor_tensor(qrot[:, :, 0:N], cT[:, 0:4, 0:N], cT[:, 4:8, 0:N], ALU.add), reads=["cT"], writes=["qrot"])
        S.op("dve", lambda e: e.tensor_tensor(kq[:, 0, 0:N], kq[:, 0, 0:N], rc[:, 0:N], ALU.mult), reads=["kq", "rc"], writes=["kq"])
        S.op("pool", lambda e: e.tensor_tensor(kq[:, 1, 0:N], kq[:, 1, 0:N], rs[:, 0:N], ALU.mult), reads=["kq", "rs"], writes=["kq"])
        S.op("dve", lambda e: e.tensor_tensor(kq[:, 0, 0:N], kq[:, 0, 0:N], kq[:, 1, 0:N], ALU.add), reads=["kq"], writes=["kq"])
        S.op("act", lambda e: e.activation(krot_b[:, 0:N], kq[:, 0, 0:N], AF.Copy), reads=["kq"], writes=["krot_b"])

    def t2(eng, out, a, b, op, reads, writes):
        S.op(eng, lambda e: e.tensor_tensor(out, a, b, op), reads=reads, writes=writes)

    def cmul(eng, o_re, o_im, a_re, a_im, b_re, b_im, tmp, names_r, names_w, shape):
        ta, tb = tmp
        t2(eng, ta, a_re, b_re, ALU.mult, names_r, ["cm_ta"])
        t2(eng, tb, a_im, b_im, ALU.mult, names_r, ["cm_tb"])
        t2(eng, o_re, ta, tb, ALU.subtract, ["cm_ta", "cm_tb"], names_w)
        t2(eng, ta, a_re, b_im, ALU.mult, names_r + ["cm_ta"], ["cm_ta"])
        t2(eng, tb, a_im, b_re, ALU.mult, names_r + ["cm_tb"], ["cm_tb"])
        t2(eng, o_im, ta, tb, ALU.add, ["cm_ta", "cm_tb"], names_w)

    def reduce_angle(dst, src, add):
        S.op("dve", lambda e: e.tensor_scalar(spc("t0"), src, add, None, ALU.add), reads=["sp"], writes=["sp"])
        S.op("dve", lambda e: e.tensor_scalar(spc("t1"), spc("t0"), 1.0 / TWO_PI, None, ALU.mult), reads=["sp"], writes=["sp"])
        S.op("dve", lambda e: e.tensor_copy(sti[:], spc("t1")), reads=["sp"], writes=["sti"])
        S.op("dve", lambda e: e.tensor_copy(spc("t1"), sti[:]), reads=["sti"], writes=["sp"])
        S.op("dve", lambda e: e.scalar_tensor_tensor(dst, spc("t1"), -TWO_PI, spc("t0"), ALU.mult, ALU.add), reads=["sp"], writes=["sp"])

    Xre = Cb[:, 0:2048].rearrange("p (j m) -> p j m", m=128)
    Xim = Cb[:, 2048:4096].rearrange("p (j m) -> p j m", m=128)
    Cpr = Cb[:, 4096:6144].rearrange("p (j m) -> p j m", m=128)
    Cpi = Cb[:, 6144:8192].rearrange("p (j m) -> p j m", m=128)
    Ctmp = [Cf[:, 4096 + i * 256:4096 + (i + 1) * 256] for i in range(4)]

    WE = R[:, 0:16384].rearrange("p (i c d m) -> p i c d m", i=4, c=2, d=16)
    UP = [Dd[:, i * 2048:(i + 1) * 2048] for i in range(2)]
    WO = R[:, 0:16384].rearrange("p (j c t m) -> p j c t m", j=16, c=2, t=16)
    KD = R[:, 16384:24576].rearrange("p (i d m) -> p i d m", i=4, d=16)

    def scat_ap(t, half, base_elems, rowlen):
        return bass.AP(t[:].tensor, half * 64 * rowlen + base_elems + half * 16, [[rowlen, 64], [512, 4], [160, 4], [1, 16]])

    def ssm_prep_params():
        pk = A[:, 0:1072].rearrange("p (j f) -> p j f", f=67)
        S.dma("sp", pk, w["ssm_pack"], "ssm_ld", writes=["hT"])
        for nm, col in (("ar", 0), ("ai", 1), ("ls", 2)):
            S.op("dve", lambda e, nm=nm, col=col: e.tensor_copy(spc(nm), pk[:, :, col]), reads=["hT"], writes=["sp"])
        for dstt, c0, nm in ((BBr, 3, "BBr"), (BBi, 19, "BBi"), (CRt, 35, "CRt"), (CIt, 51, "CIt")):
            S.op("dve", lambda e, dstt=dstt, c0=c0: e.tensor_copy(dstt[:], pk[:, :, c0:c0 + 16]), reads=["hT"], writes=[nm])
        S.op("act", lambda e: e.activation(spc("dt"), spc("ls"), AF.Exp), reads=["sp"], writes=["sp"])
        t2("dve", spc("ang"), spc("ai"), spc("dt"), ALU.mult, ["sp"], ["sp"])
        t2("dve", spc("t2"), spc("ar"), spc("dt"), ALU.mult, ["sp"], ["sp"])
        S.op("act", lambda e: e.activation(spc("mag"), spc("t2"), AF.Exp), reads=["sp"], writes=["sp"])
        reduce_angle(spc("t3"), spc("ang"), 0.0)
        S.op("act", lambda e: e.activation(spc("sn"), spc("t3"), AF.Sin), reads=["sp"], writes=["sp"])
        reduce_angle(spc("t3"), spc("ang"), math.pi / 2)
        S.op("act", lambda e: e.activation(spc("cs"), spc("t3"), AF.Sin), reads=["sp"], writes=["sp"])
        t2("dve", spc("lbr"), spc("mag"), spc("cs"), ALU.mult, ["sp"], ["sp"])
        t2("dve", spc("lbi"), spc("mag"), spc("sn"), ALU.mult, ["sp"], ["sp"])
        S.op("dve", lambda e: e.tensor_scalar(spc("t0"), spc("lbr"), -1.0, None, ALU.add), reads=["sp"], writes=["sp"])
        t2("dve", spc("t1"), spc("ar"), spc("ar"), ALU.mult, ["sp"], ["sp"])
        t2("dve", spc("t2"), spc("ai"), spc("ai"), ALU.mult, ["sp"], ["sp"])
        t2("dve", spc("t1"), spc("t1"), spc("t2"), ALU.add, ["sp"], ["sp"])
        S.op("dve", lambda e: e.reciprocal(spc("t1"), spc("t1")), reads=["sp"], writes=["sp"])
        t2("dve", spc("t2"), spc("t0"), spc("ar"), ALU.mult, ["sp"], ["sp"])
        t2("dve", spc("t3"), spc("lbi"), spc("ai"), ALU.mult, ["sp"], ["sp"])
        t2("dve", spc("t2"), spc("t2"), spc("t3"), ALU.add, ["sp"], ["sp"])
        t2("dve", spc("cre"), spc("t2"), spc("t1"), ALU.mult, ["sp"], ["sp"])
        t2("dve", spc("t2"), spc("lbi"), spc("ar"), ALU.mult, ["sp"], ["sp"])
        t2("dve", spc("t3"), spc("t0"), spc("ai"), ALU.mult, ["sp"], ["sp"])
        t2("dve", spc("t2"), spc("t2"), spc("t3"), ALU.subtract, ["sp"], ["sp"])
        t2("dve", spc("cim"), spc("t2"), spc("t1"), ALU.mult, ["sp"], ["sp"])
        cb = lambda nm: spc(nm).unsqueeze(2).to_broadcast([128, 16, 16])
        T = [Ctmp[i].rearrange("p (j h) -> p j h", h=16) for i in range(4)]
        t2("dve", T[0], BBr[:], cb("cre"), ALU.mult, ["BBr", "sp"], ["ct0"])
        t2("dve", T[1], BBi[:], cb("cim"), ALU.mult, ["BBi", "sp"], ["ct1"])
        t2("dve", T[2], BBi[:], cb("cre"), ALU.mult, ["BBi", "sp"], ["ct2"])
        t2("dve", T[3], BBr[:], cb("cim"), ALU.mult, ["BBr", "sp"], ["ct3"])
        t2("dve", BBr[:], T[0], T[1], ALU.subtract, ["ct0", "ct1"], ["BBr"])
        t2("dve", BBi[:], T[2], T[3], ALU.add, ["ct2", "ct3"], ["BBi"])
        S.op("dve", lambda e: e.memset(PWr[:, :, 0:1], 1.0), writes=["PW"])
        S.op("dve", lambda e: e.memset(PWi[:, :, 0:1], 0.0), writes=["PW"])
        S.op("dve", lambda e: e.tensor_copy(PWr[:, :, 1], spc("lbr")), reads=["sp"], writes=["PW"])
        S.op("dve", lambda e: e.tensor_copy(PWi[:, :, 1], spc("lbi")), reads=["sp"], writes=["PW"])
        n = 1
        while n < 16:
            tmpv = [Ctmp[i][:, 0:16 * n].rearrange("p (j d) -> p j d", d=n) for i in range(2)]
            br = PWr[:, :, n:n + 1].to_broadcast([128, 16, n])
            bi = PWi[:, :, n:n + 1].to_broadcast([128, 16, n])
            cmul("dve", PWr[:, :, n + 1:2 * n + 1], PWi[:, :, n + 1:2 * n + 1], PWr[:, :, 1:n + 1], PWi[:, :, 1:n + 1], br, bi,
                 tmpv, ["PW"], ["PW"], None)
            n *= 2
        S.op("dve", lambda e: e.tensor_copy(AS[:, 0, 0, :], PWr[:, :, 16]), reads=["PW"], writes=["AS"])
        S.op("dve", lambda e: e.tensor_copy(AS[:, 1, 0, :], PWi[:, :, 16]), reads=["PW"], writes=["AS"])
        S.op("dve", lambda e: e.tensor_scalar(AS[:, 2, 0, :], AS[:, 1, 0, :], -1.0, None, ALU.mult), reads=["AS"], writes=["AS"])
        for k in range(8):
            tv = [Ctmp[i][:, 0:16] for i in range(2)]
            cmul("dve", AS[:, 0, k + 1, :], AS[:, 1, k + 1, :], AS[:, 0, k, :], AS[:, 1, k, :], AS[:, 0, k, :], AS[:, 1, k, :],
                 tv, ["AS"], ["AS"], None)
            S.op("dve", lambda e, k=k: e.tensor_scalar(AS[:, 2, k + 1, :], AS[:, 1, k + 1, :], -1.0, None, ALU.mult), reads=["AS"], writes=["AS"])

    def build_xpad(d):
        T = [Ctmp[i].rearrange("p (j h) -> p j h", h=16) for i in range(4)]
        pr = PWr[:, :, d:d + 1].to_broadcast([128, 16, 16])
        pi_ = PWi[:, :, d:d + 1].to_broadcast([128, 16, 16])
        t2("dve", T[0], BBr[:], pr, ALU.mult, ["BBr", "PW"], ["ct0"])
        t2("dve", T[1], BBi[:], pi_, ALU.mult, ["BBi", "PW"], ["ct1"])
        t2("dve", T[2], BBi[:], pr, ALU.mult, ["BBi", "PW"], ["ct2"])
        t2("dve", T[3], BBr[:], pi_, ALU.mult, ["BBr", "PW"], ["ct3"])
        for g2 in range(2):
            hs = slice(g2 * 64, (g2 + 1) * 64)
            v4 = lambda t: t[hs].rearrange("p (a b) h -> p a b h", b=4)
            S.op("dve", lambda e, g2=g2, hs=hs: e.tensor_tensor(scat_ap(C, g2, 0, 11264), T[0][hs].rearrange("p (a b) h -> p a b h", b=4),
                                                                T[1][hs].rearrange("p (a b) h -> p a b h", b=4), ALU.subtract),
                 reads=["ct0", "ct1"], writes=["Xre"])
            S.op("dve", lambda e, g2=g2, hs=hs: e.tensor_tensor(scat_ap(C, g2, 2048, 11264), T[2][hs].rearrange("p (a b) h -> p a b h", b=4),
                                                                T[3][hs].rearrange("p (a b) h -> p a b h", b=4), ALU.add),
                 reads=["ct2", "ct3"], writes=["Xim"])

    def zero_pads():
        S.op("pool", lambda e: e.memset(Cb[:, 0:8192], 0.0), writes=["Xre", "Xim", "Cpr", "Cpi"])

    def build_cpad():
        for g2 in range(2):
            hs = slice(g2 * 64, (g2 + 1) * 64)
            S.op("dve", lambda e, g2=g2, hs=hs: e.tensor_copy(scat_ap(C, g2, 4096, 11264), CRt[hs].rearrange("p (a b) h -> p a b h", b=4)),
                 reads=["CRt"], writes=["Cpr"])
            S.op("dve", lambda e, g2=g2, hs=hs: e.tensor_scalar(scat_ap(C, g2, 6144, 11264), CIt[hs].rearrange("p (a b) h -> p a b h", b=4),
                                                                -1.0, None, ALU.mult),
                 reads=["CIt"], writes=["Cpi"])

    def build_WE():
        zero_pads()
        build_cpad()
        for d in range(16):
            build_xpad(d)
            for comp, X, xn_ in ((0, Xre, "Xre"), (1, Xim, "Xim")):
                pp, ppn = next_ps()
                for i in range(4):
                    for jj in range(4):
                        mm(pp[:, i * 128:(i + 1) * 128], X[:, 4 * i + jj, :], ident_bf[:], jj == 0, jj == 3, [xn_, "ident_bf"], [ppn])
                S.op("act", lambda e, pp=pp, comp=comp, d=d: e.activation(WE[:, :, comp, d, :], pp[:].rearrange("p (i m) -> p i m", m=128), AF.Copy),
                     reads=[ppn], writes=["R_we"])
            pp, ppn = next_ps()
            for i in range(4):
                n_ = 0
                for jj in range(4):
                    for X, Cp, xn_, cn_ in ((Xre, Cpr, "Xre", "Cpr"), (Xim, Cpi, "Xim", "Cpi")):
                        mm(pp[:, i * 128:(i + 1) * 128], X[:, 4 * i + jj, :], Cp[:, 4 * i + jj, :], n_ == 0, n_ == 7, [xn_, cn_], [ppn])
                        n_ += 1
            if d == 0:
                for i in range(4):
                    S.op("dve", lambda e, pp=pp, i=i: e.scalar_tensor_tensor(KD[:, i, 0, :], ident[:], gvs[:, GV["ssm_d"] + i:GV["ssm_d"] + i + 1],
                                                                             pp[:, i * 128:(i + 1) * 128], ALU.mult, ALU.add),
                         reads=[ppn, "ident", "gvs"], writes=["R_kd"])
            else:
                S.op("act", lambda e, pp=pp, d=d: e.activation(KD[:, :, d, :], pp[:].rearrange("p (i m) -> p i m", m=128), AF.Copy),
                     reads=[ppn], writes=["R_kd"])

    def build_KD():
        pass

    def build_WO():
        S.op("pool", lambda e: e.memset(R[:, 0:16384], 0.0), writes=["R_wo"])
        for jg in range(4):
            js = slice(jg * 4, (jg + 1) * 4)
            T = [Cf[:, i * 1024:(i + 1) * 1024].rearrange("p (j t h) -> p j t h", j=4, t=16) for i in range(4)]
            cr = CRt[:, js, :].unsqueeze(2).to_broadcast([128, 4, 16, 16])
            ci = CIt[:, js, :].unsqueeze(2).to_broadcast([128, 4, 16, 16])
            pr = PWr[:, js, 1:17].unsqueeze(3).to_broadcast([128, 4, 16, 16])
            pi_ = PWi[:, js, 1:17].unsqueeze(3).to_broadcast([128, 4, 16, 16])
            t2("dve", T[0], cr, pr, ALU.mult, ["CRt", "PW"], ["wt0"])
            t2("pool", T[1], ci, pi_, ALU.mult, ["CIt", "PW"], ["wt1"])
            t2("dve", T[2], cr, pi_, ALU.mult, ["CRt", "PW"], ["wt2"])
            t2("pool", T[3], ci, pr, ALU.mult, ["CIt", "PW"], ["wt3"])
            S.op("dve", lambda e, T=T: e.scalar_tensor_tensor(T[2], T[2], -1.0, T[3], ALU.mult, ALU.subtract), reads=["wt2", "wt3"], writes=["wt2"])
            for g2 in range(2):
                hs = slice(g2 * 64, (g2 + 1) * 64)
                base = g2 * 64 * 24576 + jg * 4 * 1024 + g2 * 16
                o_re = bass.AP(R[:].tensor, base, [[24576, 64], [1024, 4], [32, 16], [1, 16]])
                o_im = bass.AP(R[:].tensor, base + 512, [[24576, 64], [1024, 4], [32, 16], [1, 16]])
                S.op("dve", lambda e, o_re=o_re, T=T, hs=hs: e.tensor_tensor(o_re, T[0][hs], T[1][hs], ALU.subtract), reads=["wt0", "wt1"], writes=["R_wo"])
                S.op("dve", lambda e, o_im=o_im, T=T, hs=hs: e.tensor_copy(o_im, T[2][hs]), reads=["wt2"], writes=["R_wo"])

    Zre = A[:].rearrange("p (j c) -> p j c", j=16)
    Zim = B[:].rearrange("p (j c) -> p j c", j=16)
    Zp = [Cf[:, 0:2048].rearrange("p (j c) -> p j c", j=8), Cf[:, 2048:4096].rearrange("p (j c) -> p j c", j=8)]

    def compute_E(src_scr, srcname):
        nck = NCK
        hn = nck // 2
        ht = NTOK // 2
        cnt = 0
        for i in range(4):
            for hf in range(2):
                up = UP[cnt % 2]
                upn = "xn_a" if cnt % 2 == 0 else "xn_b"
                cnt += 1
                S.dma("sp", up[:, 0:ht], src_scr[:, i, hf * ht:(hf + 1) * ht], "UP%d" % (cnt % 2), reads=[srcname], writes=[upn])
                u3 = up[:, 0:ht].rearrange("p (c t) -> p c t", t=16)
                for jj in range(4):
                    j = 4 * i + jj
                    for comp, Z in ((0, Zre), (1, Zim)):
                        pp, ppn = next_ps()
                        for tau in range(16):
                            mm(pp[:, 0:hn], WE[jj * 32:(jj + 1) * 32, i, comp, 15 - tau, :], u3[jj * 32:(jj + 1) * 32, :, tau],
                               tau == 0, tau == 15, ["R_we", upn], [ppn], tile_position=(32 * jj, 0))
                        S.op("act", lambda e, pp=pp, Z=Z, j=j, hf=hf: e.activation(Z[:, j, hf * hn:(hf + 1) * hn], pp[:, 0:hn], AF.Copy),
                             reads=[ppn], writes=[f"Z{j}" + ("r" if comp == 0 else "i")])

    def prev_state():
        compute_E(uprev_scr, "uprev_scr")
        Zq = [Cf[:, 0:2048].rearrange("p (j c) -> p j c", j=16), Cf[:, 2048:4096].rearrange("p (j c) -> p j c", j=16)]
        n = NCK
        k = 0
        cur = 0
        while n > 1:
            h = n // 2
            for j in range(16):
                if cur == 0:
                    sre, sim, sn = Zre[:, j, 0:n], Zim[:, j, 0:n], f"Z{j}"
                    dre, dim, dn = Zq[0][:, j, 0:h], Zq[1][:, j, 0:h], f"Zq{j}"
                else:
                    sre, sim, sn = Zq[0][:, j, 0:n], Zq[1][:, j, 0:n], f"Zq{j}"
                    dre, dim, dn = Zre[:, j, 0:h], Zim[:, j, 0:h], f"Z{j}"
                ev_re, od_re = sre.rearrange("p (c two) -> p c two", two=2)[:, :, 0], sre.rearrange("p (c two) -> p c two", two=2)[:, :, 1]
                ev_im, od_im = sim.rearrange("p (c two) -> p c two", two=2)[:, :, 0], sim.rearrange("p (c two) -> p c two", two=2)[:, :, 1]
                are, aim, naim = AS[:, 0, k, j:j + 1], AS[:, 1, k, j:j + 1], AS[:, 2, k, j:j + 1]
                S.op("dve", lambda e, dre=dre, ev_re=ev_re, od_re=od_re, are=are: e.scalar_tensor_tensor(dre, ev_re, are, od_re, ALU.mult, ALU.add),
                     reads=[sn + "r", "AS"], writes=[dn + "r"])
                S.op("dve", lambda e, dim=dim, ev_im=ev_im, od_im=od_im, are=are: e.scalar_tensor_tensor(dim, ev_im, are, od_im, ALU.mult, ALU.add),
                     reads=[sn + "i", "AS"], writes=[dn + "i"])
                S.op("dve", lambda e, dre=dre, ev_im=ev_im, naim=naim: e.scalar_tensor_tensor(dre, ev_im, naim, dre, ALU.mult, ALU.add),
                     reads=[sn + "i", "AS", dn + "r"], writes=[dn + "r"])
                S.op("dve", lambda e, dim=dim, ev_re=ev_re, aim=aim: e.scalar_tensor_tensor(dim, ev_re, aim, dim, ALU.mult, ALU.add),
                     reads=[sn + "r", "AS", dn + "i"], writes=[dn + "i"])
            cur = 1 - cur
            n = h
            k += 1
        src = (Zre, Zim, "Z") if cur == 0 else (Zq[0], Zq[1], "Zq")
        allr = [f"{src[2]}{j}r" for j in range(16)] + [f"{src[2]}{j}i" for j in range(16)]
        S.op("dve", lambda e: e.tensor_copy(sin_f[:, 0, :], src[0][:, :, 0]), reads=allr, writes=["sin_f"])
        S.op("dve", lambda e: e.tensor_copy(sin_f[:, 1, :], src[1][:, :, 0]), reads=allr, writes=["sin_f"])

    def phase15():
        nck = NCK
        compute_E(u_scr, "u_scr")
        for half in range(2):
            k = 0
            s_ = 1
            cur = 0
            while s_ < nck:
                for j8 in range(8):
                    j = half * 8 + j8
                    zz = (Zre[:, j, 0:nck], Zim[:, j, 0:nck], f"Z{j}")
                    zp = (Zp[0][:, j8, 0:nck], Zp[1][:, j8, 0:nck], f"Zp{j8}")
                    (ore, oim, on_), (nre, nim, nn_) = (zz, zp) if cur == 0 else (zp, zz)
                    are = AS[:, 0, k, j:j + 1]
                    aim = AS[:, 1, k, j:j + 1]
                    naim = AS[:, 2, k, j:j + 1]
                    s = s_
                    S.op("dve", lambda e, nre=nre, ore=ore, are=are, s=s: e.scalar_tensor_tensor(nre[:, s:], ore[:, 0:nck - s], are, ore[:, s:], ALU.mult, ALU.add),
                         reads=[on_ + "r", "AS"], writes=[nn_ + "r"])
                    S.op("dve", lambda e, nim=nim, oim=oim, are=are, s=s: e.scalar_tensor_tensor(nim[:, s:], oim[:, 0:nck - s], are, oim[:, s:], ALU.mult, ALU.add),
                         reads=[on_ + "i", "AS"], writes=[nn_ + "i"])
                    S.op("dve", lambda e, nre=nre, oim=oim, naim=naim, s=s: e.scalar_tensor_tensor(nre[:, s:], oim[:, 0:nck - s], naim, nre[:, s:], ALU.mult, ALU.add),
                         reads=[on_ + "i", "AS", nn_ + "r"], writes=[nn_ + "r"])
                    S.op("dve", lambda e, nim=nim, ore=ore, aim=aim, s=s: e.scalar_tensor_tensor(nim[:, s:], ore[:, 0:nck - s], aim, nim[:, s:], ALU.mult, ALU.add),
                         reads=[on_ + "r", "AS", nn_ + "i"], writes=[nn_ + "i"])
                    S.op("pool", lambda e, nre=nre, ore=ore, s=s: e.tensor_copy(nre[:, 0:s], ore[:, 0:s]), reads=[on_ + "r", nn_ + "r"], writes=[nn_ + "r"])
                    S.op("pool", lambda e, nim=nim, oim=oim, s=s: e.tensor_copy(nim[:, 0:s], oim[:, 0:s]), reads=[on_ + "i", nn_ + "i"], writes=[nn_ + "i"])
                cur = 1 - cur
                s_ *= 2
                k += 1
            if cur == 1:
                for j8 in range(8):
                    j = half * 8 + j8
                    S.op("pool", lambda e, j=j, j8=j8: e.tensor_copy(Zre[:, j, 0:nck], Zp[0][:, j8, 0:nck]), reads=[f"Zp{j8}r"], writes=[f"Z{j}r"])
                    S.op("pool", lambda e, j=j, j8=j8: e.tensor_copy(Zim[:, j, 0:nck], Zp[1][:, j8, 0:nck]), reads=[f"Zp{j8}i"], writes=[f"Z{j}i"])

    sin_dev = kb.inp("sin_dev", [128, 32])
    fin_t = kb.sb("fin_t", [128, 32], F32)

    def custom_dma_like(q, fn, key, reads, writes):
        waits = S._deps(q, reads, writes)
        n = S.dcnt.get(key, 0) + 1
        S.dcnt[key] = n
        ev = ("d:" + key, 16 * n)
        S.ops[q].append((waits, fn, ("d:" + key, 16)))
        S._commit(ev, reads, writes)

    def exchange_and_correct():
        nck = NCK
        zall = [f"Z{j}{c}" for j in range(16) for c in "ri"]
        S.op("dve", lambda e: e.tensor_copy(fin_t[:, 0:16], Zre[:, :, nck - 1]), reads=zall, writes=["fin_t"])
        S.op("dve", lambda e: e.tensor_copy(fin_t[:, 16:32], Zim[:, :, nck - 1]), reads=zall, writes=["fin_t"])
        for half in range(2):
            for j8 in range(8):
                j = half * 8 + j8
                fr, fi = Zp[0][:, j8, 0:nck], Zp[1][:, j8, 0:nck]
                fn_ = f"Zp{j8}"
                are, aim, naim = AS[:, 0, 0, j:j + 1], AS[:, 1, 0, j:j + 1], AS[:, 2, 0, j:j + 1]
                sr, si = sin_f[:, 0, j:j + 1], sin_f[:, 1, j:j + 1]
                S.op("dve", lambda e, fr=fr, sr=sr, are=are: e.tensor_scalar(fr[:, 0:1], sr, are, None, ALU.mult), reads=["sin_f", "AS"], writes=[fn_ + "r"])
                S.op("dve", lambda e, fr=fr, si=si, naim=naim: e.scalar_tensor_tensor(fr[:, 0:1], si, naim, fr[:, 0:1], ALU.mult, ALU.add),
                     reads=["sin_f", "AS", fn_ + "r"], writes=[fn_ + "r"])
                S.op("dve", lambda e, fi=fi, si=si, are=are: e.tensor_scalar(fi[:, 0:1], si, are, None, ALU.mult), reads=["sin_f", "AS"], writes=[fn_ + "i"])
                S.op("dve", lambda e, fi=fi, sr=sr, aim=aim: e.scalar_tensor_tensor(fi[:, 0:1], sr, aim, fi[:, 0:1], ALU.mult, ALU.add),
                     reads=["sin_f", "AS", fn_ + "i"], writes=[fn_ + "i"])
                n = 1
                k = 0
                while n < nck:
                    m = min(n, nck - n)
                    are, aim, naim = AS[:, 0, k, j:j + 1], AS[:, 1, k, j:j + 1], AS[:, 2, k, j:j + 1]
                    S.op("dve", lambda e, fr=fr, are=are, n=n, m=m: e.tensor_scalar(fr[:, n:n + m], fr[:, 0:m], are, None, ALU.mult),
                         reads=[fn_ + "r", "AS"], writes=[fn_ + "r"])
                    S.op("dve", lambda e, fi=fi, are=are, n=n, m=m: e.tensor_scalar(fi[:, n:n + m], fi[:, 0:m], are, None, ALU.mult),
                         reads=[fn_ + "i", "AS"], writes=[fn_ + "i"])
                    S.op("dve", lambda e, fr=fr, fi=fi, naim=naim, n=n, m=m: e.scalar_tensor_tensor(fr[:, n:n + m], fi[:, 0:m], naim, fr[:, n:n + m], ALU.mult, ALU.add),
                         reads=[fn_ + "r", fn_ + "i", "AS"], writes=[fn_ + "r"])
                    S.op("dve", lambda e, fr=fr, fi=fi, aim=aim, n=n, m=m: e.scalar_tensor_tensor(fi[:, n:n + m], fr[:, 0:m], aim, fi[:, n:n + m], ALU.mult, ALU.add),
                         reads=[fn_ + "r", fn_ + "i", "AS"], writes=[fn_ + "i"])
                    n *= 2
                    k += 1
                S.op("pool", lambda e, j=j, fr=fr: e.tensor_tensor(Zre[:, j, 0:nck], Zre[:, j, 0:nck], fr, ALU.add), reads=[fn_ + "r", f"Z{j}r"], writes=[f"Z{j}r"])
                S.op("pool", lambda e, j=j, fi=fi: e.tensor_tensor(Zim[:, j, 0:nck], Zim[:, j, 0:nck], fi, ALU.add), reads=[fn_ + "i", f"Z{j}i"], writes=[f"Z{j}i"])
        S.op("dve", lambda e: e.tensor_copy(fin_t[:, 0:16], Zre[:, :, nck - 1]), reads=zall, writes=["fin_t"])
        S.op("dve", lambda e: e.tensor_copy(fin_t[:, 16:32], Zim[:, :, nck - 1]), reads=zall, writes=["fin_t"])
        ppf, ppfn = next_ps()
        S.op("pe", lambda e: e.transpose(ppf[0:32, 0:128], fin_t[:, 0:32], ident[:]), reads=["fin_t", "ident"], writes=[ppfn])
        S.op("act", lambda e: e.activation(xtok[0][0:32, 0:128], ppf[0:32, 0:128], AF.Copy), reads=[ppfn], writes=["xtok0"])
        S.dma("sp", o_psre.rearrange("(j g) p -> j (g p)", g=2), xtok[0][0:16, 0:128], "ystore0", reads=["xtok0"], writes=["o_ps"])
        S.dma("sp", o_psim.rearrange("(j g) p -> j (g p)", g=2), xtok[0][16:32, 0:128], "ystore0", reads=["xtok0"], writes=["o_ps"])
        hst = Cb[:, 0:32 * nck].rearrange("p (j a c) -> p j a c", j=16, a=2)
        zp_all = [f"Zp{j8}{c}" for j8 in range(8) for c in "ri"]
        if nck > 1:
            S.op("dve", lambda e: e.tensor_copy(hst[:, :, 0, 1:nck], Zre[:, :, 0:nck - 1]), reads=zall, writes=["hst"] + zp_all)
            S.op("pool", lambda e: e.tensor_copy(hst[:, :, 1, 1:nck], Zim[:, :, 0:nck - 1]), reads=zall, writes=["hst"] + zp_all)
        S.op("dve", lambda e: e.tensor_copy(hst[:, :, 0, 0], sin_f[:, 0, :]), reads=["sin_f"], writes=["hst"] + zp_all)
        S.op("dve", lambda e: e.tensor_copy(hst[:, :, 1, 0], sin_f[:, 1, :]), reads=["sin_f"], writes=["hst"] + zp_all)
        S.dma("sp", hb_scr, hst, "hb_st", reads=["hst"], writes=["hb_scr"])

    def memory_kv():
        import os
        MK = int(os.environ.get("MK", "9"))
        load_block_T(w["mem_p"], 2)
        if MK < 2:
            return
        rms_stats([hT[:, kt, 0:256] for kt in range(8)], "hT", 256)
        norm_to_bf(xn[:, :, 0:256], "xn", hT[:, :, 0:256], "hT", 8, 256)
        if MK < 3:
            return
        for which, spec, outd in ((0, specs["w_mem_k"], o_pmk), (1, specs["w_mem_v"], o_pmv)):
            if which == 1 and os.environ.get("MKV", "1") == "0":
                continue
            def ev(vmt, pp, ppn, slot, sname, ml, which=which):
                if pp is not None:
                    S.op("act", lambda e: e.activation(mkT[:, vmt, :], pp[:, 0:256], AF.Copy), reads=[ppn], writes=["mkT"])
                    return False
                ppt, pptn = next_ps()
                for s in range(2):
                    for kt in range(8):
                        mm(ppt[:, s * 128:(s + 1) * 128], xn[:, kt, s * 128:(s + 1) * 128], slot[:, (ml * 8 + kt) * 128:(ml * 8 + kt + 1) * 128],
                           kt == 0, kt == 7, [sname, "xn"], [pptn])
                for s in range(2):
                    S.op("act", lambda e, s=s: e.activation(xtok[s][:, vmt * 128:(vmt + 1) * 128], ppt[:, s * 128:(s + 1) * 128], AF.Copy),
                         reads=[pptn], writes=[f"xtok{s}"])
                if which == 1:
                    for s in range(2):
                        S.op("pool", lambda e, s=s: e.tensor_copy(mv_b[:, s, vmt * 128:(vmt + 1) * 128], xtok[s][:, vmt * 128:(vmt + 1) * 128]),
                             reads=[f"xtok{s}"], writes=["mv_b"])
                return which == 1
            linear(spec, lambda kt: (xn[:, kt, 0:256], 128), 8, 256, ev)
            if MK < 4:
                return
            for s in range(2):
                S.dma("sp", outd[s * 128:(s + 1) * 128, :], xtok[s][:, :], "o_pm%d" % s, reads=[f"xtok{s}"], writes=["o_pm"])

    def p1_block(b):
        halo = b < 0
        N, nsub = (128, 1) if halo else (512, 4)
        row0 = 0 if halo else 128 + b * 512
        nxt = None
        if halo:
            nxt = (xprev[0:512, :], 4, 128, "prev0")
        elif b < NB - 1:
            nxt = (xp[128 + (b + 1) * 512:128 + (b + 2) * 512, :], 4, 128, f"own{b + 1}")
        load_block_T(xp[row0:row0 + N, :], nsub, key=None if halo else f"own{b}", nxt=nxt)
        ffn("ffn1", GV["ffn1_post"], N)
        if not halo:
            S.dma("sp", h1_scr[b], A[:], "st_h1", reads=["hT"], writes=["h1_scr"])
        in_proj(N, nsub, 128, ropeC_d[:, row0:row0 + N], ropeS_d[:, row0:row0 + N], need_q=not halo)
        S.dma("sp", k_scr[:, row0:row0 + N], krot_b[:, 0:N], "st_k", reads=["krot_b"], writes=["k_scr"])
        vs0 = 0 if halo else 1 + 4 * b
        S.dma("sp", v_scr[vs0:vs0 + nsub].rearrange("s p m -> p s m"), v_bf[:, 1:1 + nsub, :], "st_v", reads=["v_bf"], writes=["v_scr"])
        if not halo:
            S.dma("sp", q_scr[b], qrot[:].rearrange("p t n -> p (t n)"), "st_q", reads=["qrot"], writes=["q_scr"])
            S.dma("sp", u_scr[:, :, b * 512:(b + 1) * 512], u_bf[:], "st_u", reads=["u_bf"], writes=["u_scr"])
            if b == NB - 1:
                pp, ppn = next_ps()
                S.op("pe", lambda e: e.transpose(pp[:, 0:128], kq[:, 0, 384:512], ident[:]), reads=["kq", "ident"], writes=[ppn])
                S.op("act", lambda e: e.activation(xtok[0][:, 0:128], pp[:, 0:128], AF.Copy), reads=[ppn], writes=["xtok0"])
                S.dma("sp", o_pwk, xtok[0][:, 0:128], "ystore0", reads=["xtok0"], writes=["o_pwk"])
                S.dma("sp", o_pwv, v_f[:], "o_vf", reads=["v_f"], writes=["o_pwv"])

    def p1_prev_block(pb):
        N = 512
        nxt = (xprev[(pb + 1) * 512:(pb + 2) * 512, :], 4, 128, f"prev{pb + 1}") if pb < NB - 1 else (xp[128:128 + 512, :], 4, 128, "own0")
        load_block_T(xprev[pb * 512:(pb + 1) * 512, :], 4, key=f"prev{pb}", nxt=nxt)
        ffn("ffn1", GV["ffn1_post"], N)
        rms_stats([hT[:, kt, 0:N] for kt in range(8)], [f"hT{kt}" for kt in range(8)], N)
        norm_to_bf(xn[:, :, 0:N], "xn", hT[:, :, 0:N], "hT", 8, N)

        def ev(vmt, pp, ppn, slot, sname, ml):
            if pp is None:
                return False
            S.op("act", lambda e: e.activation(u_bf[:, vmt, 0:N], pp[:, 0:N], AF.Copy), reads=[ppn], writes=["u_bf"])
            return False

        linear(specs["w_in"], lambda kt: (xn[:, kt, 0:N], 128), 8, N, ev, vm_sel=lambda v: v < 4)
        S.dma("sp", uprev_scr[:, :, pb * 512:(pb + 1) * 512], u_bf[:], "st_u", reads=["u_bf"], writes=["uprev_scr"])

    gf = Cf[:, 0:2048].rearrange("p (k n) -> p k n", n=512)
    gb = Cb[:, 4096:6144].rearrange("p (k n) -> p k n", n=512)
    ysn = Cb[:, 6144:8192].rearrange("p (k n) -> p k n", n=512)
    qmT = Cb[:, 0:4096].rearrange("p (k n) -> p k n", n=512)
    Pm = Cb[:, 4096:8192].rearrange("p (k n) -> p k n", n=512)
    o_all = B[0:64, :].rearrange("p (k n) -> p k n", n=512)

    def ssm_out(N, u3_fn, y3_fn, h_fn, ntau, nlag):
        for i in range(4):
            pp, ppn = next_ps()
            y3 = y3_fn(pp)
            for d in range(nlag):
                mm(y3[:, :, d:ntau], KD[:, i, d, :], u3_fn(i)[:, :, 0:ntau - d], d == 0, False, ["R_kd", "u_bf"], [ppn])
            cnt = 0
            for jj in range(4):
                j = 4 * i + jj
                for comp in range(2):
                    for tau in range(ntau):
                        cnt += 1
                        mm(y3[jj * 32:(jj + 1) * 32, :, tau], WO[:, j, comp, tau, :], h_fn(j, comp), False, cnt == 8 * ntau,
                           ["R_wo", "hbk"], [ppn], tile_position=(0, 32 * jj))
            S.op("act", lambda e, pp=pp, i=i: e.activation(gf[:, i, 0:N], pp[:, 0:N], AF.Gelu_apprx_tanh), reads=[ppn], writes=["gf"])
        S.op("pool", lambda e: e.tensor_copy(gb[:, :, 0:N], gf[:, :, 0:N]), reads=["gf"], writes=["gb"])

        def ev(vmt, pp, ppn, slot, sname, ml):
            if pp is None:
                return False
            S.op("act", lambda e: e.activation(rc[:, 0:N], pp[:, 0:N], AF.Sigmoid, bias=gvs[:, GV["b_glu"] + vmt:GV["b_glu"] + vmt + 1]),
                 reads=[ppn, "gvs"], writes=["rc"])
            S.op("dve", lambda e: e.tensor_tensor(gf[:, vmt, 0:N], gf[:, vmt, 0:N], rc[:, 0:N], ALU.mult), reads=["gf", "rc", "gb"], writes=["gf"])
            return False

        linear(specs["w_glu"], lambda kt: (gb[:, kt, 0:N], 128), 4, N, ev, rnames=("gb",))
        rms_stats([gf[:, i, 0:N] for i in range(4)], "gf", N)
        norm_to_bf(ysn[:, :, 0:N], "ysn", gf[:, :, 0:N], "gf", 4, N)

    def attn_norm_outproj(N):
        rms_stats([o_all[0:64, h, 0:N] for h in range(8)], "cT", N, P=64, nfeat=512)
        norm_to_bf(xn[0:64, :, 0:N], "xn", o_all[0:64, :, 0:N], "cT", 8, N, P=64)

        def ev(vmt, pp, ppn, slot, sname, ml):
            if pp is None:
                return False
            S.op("act", lambda e: e.activation(cT[:, vmt, 0:N], pp[:, 0:N], AF.Copy), reads=[ppn], writes=[f"cT{vmt}"])
            return False

        linear(specs["w_out"], lambda kt: (ysn[:, kt, 0:N], 128) if kt < 4 else (xn[0:64, kt - 4, 0:N], 64), 12, N, ev, rnames=("xn", "ysn"))
        post_residual("hT", GV["mix_post"], N, False)

    def cross_attn(N, score_fn):
        rms_stats([hT[:, kt, 0:N] for kt in range(8)], [f"hT{kt}" for kt in range(8)], N)
        norm_to_bf(xn[:, :, 0:N], "xn", hT[:, :, 0:N], "hT", 8, N)

        def evq(vmt, pp, ppn, slot, sname, ml):
            if pp is None:
                return False
            S.op("act", lambda e: e.activation(qmT[:, vmt, 0:N], pp[:, 0:N], AF.Copy), reads=[ppn], writes=["qmT"])
            return False

        linear(specs["w_mem_q"], lambda kt: (xn[:, kt, 0:N], 128), 8, N, evq)
        score_fn()

        def evo(vmt, pp, ppn, slot, sname, ml):
            if pp is None:
                return False
            S.op("act", lambda e: e.activation(cT[:, vmt, 0:N], pp[:, 0:N], AF.Copy), reads=[ppn], writes=[f"cT{vmt}"])
            return False

        linear(specs["w_mem_o"], lambda kt: (xn[:, kt, 0:N], 128), 8, N, evo)
        post_residual("hT", GV["xa_post"], N, False)

    def prompt_mem_scores():
        N = 512
        for hm in range(4):
            for mt in range(2):
                pp, ppn = next_ps()
                for half in range(2):
                    mm(pp[:, 0:N], mkT[:, hm * 2 + half, mt * 128:(mt + 1) * 128], qmT[:, hm * 2 + half, 0:N], half == 0, half == 1,
                       ["mkT", "qmT"], [ppn])
                S.op("act", lambda e, pp=pp, hm=hm, mt=mt: e.activation(Pm[:, hm * 2 + mt, 0:N], pp[:, 0:N], AF.Exp, scale=1.0 / 16.0),
                     reads=[ppn], writes=["Pm"])
            pd, pdn = next_ps()
            for mt in range(2):
                mm(pd[:, 0:N], ones_bf[:, :], Pm[:, hm * 2 + mt, 0:N], mt == 0, mt == 1, ["ones", "Pm"], [pdn])
            S.op("dve", lambda e, pd=pd: e.reciprocal(rc[:, 0:N], pd[:, 0:N]), reads=[pdn], writes=["rc"])
            for half in range(2):
                po, pon = next_ps()
                for mt in range(2):
                    mm(po[:, 0:N], mv_b[:, mt, (hm * 2 + half) * 128:(hm * 2 + half + 1) * 128], Pm[:, hm * 2 + mt, 0:N], mt == 0, mt == 1,
                       ["mv_b", "Pm"], [pon])
                S.op("dve", lambda e, po=po, hm=hm, half=half: e.tensor_tensor(xn[:, hm * 2 + half, 0:N], po[:, 0:N], rc[:, 0:N], ALU.mult),
                     reads=[pon, "rc"], writes=["xn"])

    def prompt_attention(b):
        for s in range(4):
            for hh in range(2):
                hs = slice(hh * 64, (hh + 1) * 64)
                pa, pan = next_ps()
                pb, pbn = next_ps()
                for t in range(4):
                    mm(pa[:, t * 128:(t + 1) * 128], kblk[hs, 128 + s * 128:128 + (s + 1) * 128], qrot[hs, t, s * 128:(s + 1) * 128], True, True,
                       ["kblk", "qrot"], [pan])
                    mm(pb[:, t * 128:(t + 1) * 128], kblk[hs, s * 128:(s + 1) * 128], qrot[hs, t, s * 128:(s + 1) * 128], True, True,
                       ["kblk", "qrot"], [pbn])
                S.op("act", lambda e, pa=pa, hh=hh: e.activation(pt[:, hh * 2, :], pa[:, :], AF.Exp, scale=0.125), reads=[pan], writes=[f"pt{hh}c"])
                S.op("act", lambda e, pb=pb, hh=hh: e.activation(pt[:, hh * 2 + 1, :], pb[:, :], AF.Exp, scale=0.125), reads=[pbn], writes=[f"pt{hh}p"])
                mprev = 2 if (b == 0 and s == 0) else 1
                S.op("pool", lambda e, hh=hh: e.tensor_tensor(pt[:, hh * 2, :].rearrange("p (t q) -> p t q", q=128), pt[:, hh * 2, :].rearrange("p (t q) -> p t q", q=128),
                                                              masks[:, 0, :].unsqueeze(1).to_broadcast([128, 4, 128]), ALU.mult),
                     reads=[f"pt{hh}c", "masks"], writes=[f"pt{hh}c"])
                S.op("pool", lambda e, hh=hh, mprev=mprev: e.tensor_tensor(pt[:, hh * 2 + 1, :].rearrange("p (t q) -> p t q", q=128), pt[:, hh * 2 + 1, :].rearrange("p (t q) -> p t q", q=128),
                                                                           masks[:, mprev, :].unsqueeze(1).to_broadcast([128, 4, 128]), ALU.mult),
                     reads=[f"pt{hh}p", "masks"], writes=[f"pt{hh}p"])
                po, pon = next_ps()
                pd, pdn = next_ps()
                mm(po[0:64, :], v_bf[:, 1 + s, hs], pt[:, hh * 2, :], True, False, ["v_bf", f"pt{hh}c"], [pon])
                mm(po[0:64, :], v_bf[:, s, hs], pt[:, hh * 2 + 1, :], False, True, ["v_bf", f"pt{hh}p"], [pon])
                mm(pd[0:64, :], ones_bf[:, 0:64], pt[:, hh * 2, :], True, False, ["ones", f"pt{hh}c"], [pdn])
                mm(pd[0:64, :], ones_bf[:, 0:64], pt[:, hh * 2 + 1, :], False, True, ["ones", f"pt{hh}p"], [pdn])
                S.op("dve", lambda e, pd=pd, hh=hh: e.tensor_tensor(att_t[0:64, :].rearrange("p (t q) -> p t q", q=128), pd[0:64, :].rearrange("p (t q) -> p t q", q=128),
                                                                    sinkexp[0:64, hh * 4:(hh + 1) * 4].unsqueeze(2).to_broadcast([64, 4, 128]), ALU.add),
                     reads=[pdn, "sinkexp"], writes=["att_t"])
                S.op("dve", lambda e: e.reciprocal(att_t[0:64, :], att_t[0:64, :]), reads=["att_t"], writes=["att_t"])
                S.op("dve", lambda e, po=po, hh=hh, s=s: e.tensor_tensor(o_all[0:64, hh * 4:(hh + 1) * 4, s * 128:(s + 1) * 128], po[0:64, :].rearrange("p (t q) -> p t q", q=128),
                                                                         att_t[0:64, :].rearrange("p (t q) -> p t q", q=128), ALU.mult),
                     reads=[pon, "att_t"], writes=["cT"])

    def p2_loads(b):
        S.dma("sp", qrot[:].rearrange("p t n -> p (t n)"), q_scr[b], "ld_q", reads=["q_scr"], writes=["qrot"])
        S.dma("sp", kblk[:, :], k_scr[:, b * 512:b * 512 + 640], "ld_k", reads=["k_scr"], writes=["kblk"])
        S.dma("sp", v_bf[:, :, :], v_scr[4 * b:4 * b + 5].rearrange("s p m -> p s m"), "ld_v", reads=["v_scr"], writes=["v_bf"])
        S.dma("sp", u_bf[:], u_scr[:, :, b * 512:(b + 1) * 512], "ld_u", reads=["u_scr"], writes=["u_bf"])
        S.dma("sp", hbk[:], hb_scr[:, :, :, b * 32:(b + 1) * 32], "ld_hb", reads=["hb_scr"], writes=["hbk"])

    def p2_block(b):
        N = 512
        if b == 0:
            p2_loads(0)
        S.dma("sp", A[:], h1_scr[b], "ld_h1", reads=["h1_scr"], writes=["hT"])
        ssm_out(N, lambda i: u_bf[:, i, :].rearrange("p (c t) -> p c t", t=16), lambda pp: pp[:].rearrange("p (c t) -> p c t", t=16),
                lambda j, comp: hbk[:, j, comp, :], 16, 16)
        if stage == "dbg_ssm":
            S.dma("sp", dbg_o[:, 0:4, :], gf[:, :, :], "dbg", reads=["gf"], writes=["dbg"])
            return
        prompt_attention(b)
        if stage == "dbg_attn":
            S.dma("sp", dbg_o[0:64, :, :], o_all[:, :, :], "dbg", reads=["cT"], writes=["dbg"])
            return
        attn_norm_outproj(N)
        if b + 1 < NB and not stage.startswith("dbg"):
            p2_loads(b + 1)
        if stage == "dbg_mix":
            S.dma("sp", dbg_o[:, :, :], hT[:, :, :], "dbg", reads=["hT"], writes=["dbg"])
            return
        cross_attn(N, prompt_mem_scores)
        if stage == "dbg_xa":
            S.dma("sp", dbg_o[:, :, :], hT[:, :, :], "dbg", reads=["hT"], writes=["dbg"])
            return
        ffn("ffn2", GV["ffn2_post"], N)
        if stage == "dbg_ffn2":
            S.dma("sp", dbg_o[:, :, :], hT[:, :, :], "dbg", reads=["hT"], writes=["dbg"])
        store_block_T(y_out[b * 512:(b + 1) * 512, :], 4)


    def sample_part_a():
        import os
        SA = int(os.environ.get("SA", "9"))
        N = 64
        load_block_T(xs, 1, W=64)
        if SA < 2:
            return
        ffn("ffn1", GV["ffn1_post"], N)
        if SA < 3:
            return
        S.op("pool", lambda e: e.tensor_copy(h1s[:], hT[:, :, 0:64]), reads=["hT"], writes=["h1s"])
        in_proj(N, 1, 64, ropeCs_d, ropeSs_d, need_q=True)
        if SA < 4:
            return
        S.op("pool", lambda e: e.tensor_copy(us_b[:], u_bf[:, :, 0:64]), reads=["u_bf"], writes=["us_b"])
        S.op("pool", lambda e: e.tensor_copy(qs_b[:], qrot[:, :, 0:64]), reads=["qrot"], writes=["qs_b"])
        S.op("pool", lambda e: e.tensor_copy(ks_b[:], krot_b[:, 0:64]), reads=["krot_b"], writes=["ks_b"])
        S.op("pool", lambda e: e.tensor_copy(vs_b[:], v_bf[0:64, 1, :]), reads=["v_bf"], writes=["vs_b"])
        if SA < 5:
            return
        S.dma("sp", o_swk[:, 0:124, :], w["swa_k"][:, 4:128, :], "o_sw", writes=["o_swk"])
        S.dma("sp", o_swv[:, 0:124, :], w["swa_v"][:, 4:128, :], "o_sw", writes=["o_swv"])
        if SA < 6:
            return
        pp, ppn = next_ps()
        S.op("pe", lambda e: e.transpose(pp[0:64, 0:128], kq[:, 0, 0:64], ident[:]), reads=["kq", "ident"], writes=[ppn])
        S.op("act", lambda e: e.activation(xtok[0][0:64, 0:128], pp[0:64, 0:128], AF.Copy), reads=[ppn], writes=["xtok0"])
        for b_ in range(16):
            S.dma("sp", o_swk[b_, 124:128, :], xtok[0][4 * b_:4 * b_ + 4, 0:128], "ystore0", reads=["xtok0"], writes=["o_swk"])
            S.dma("sp", o_swv[b_, 124:128, :], v_f[4 * b_:4 * b_ + 4, :], "o_vf", reads=["v_f"], writes=["o_swv"])

    def sample_state():
        hs_f = rc[:, :].rearrange("p (c j b) -> p c j b", c=2, j=16)
        fin = rs[:, :].rearrange("p (c j b) -> p c j b", c=2, j=16)
        for comp, src_ in ((0, w["st_re"]), (1, w["st_im"])):
            xt, xtn = xtok[comp], f"xtok{comp}"
            S.dma("sp", xt[0:16, 0:1024], src_.rearrange("b g p -> b (g p)")[:, 0:1024], xtn, writes=[xtn])
            pp, ppn = next_ps()
            for j in range(8):
                S.op("pe", lambda e, pp=pp, j=j, xt=xt: e.transpose(pp[:, j * 16:(j + 1) * 16], xt[0:16, j * 128:(j + 1) * 128], ident[0:16, 0:16]),
                     reads=[xtn, "ident"], writes=[ppn])
            S.op("act", lambda e, pp=pp, comp=comp: e.activation(hs_f[:, comp, 0:8, :], pp[:, 0:128].rearrange("p (j b) -> p j b", b=16), AF.Copy),
                 reads=[ppn], writes=["rc"])
            S.dma("sp", xt[0:16, 0:1024], src_.rearrange("b g p -> b (g p)")[:, 1024:2048], xtn, reads=[ppn], writes=[xtn])
            pp, ppn = next_ps()
            for j in range(8):
                S.op("pe", lambda e, pp=pp, j=j, xt=xt: e.transpose(pp[:, j * 16:(j + 1) * 16], xt[0:16, j * 128:(j + 1) * 128], ident[0:16, 0:16]),
                     reads=[xtn, "ident"], writes=[ppn])
            S.op("act", lambda e, pp=pp, comp=comp: e.activation(hs_f[:, comp, 8:16, :], pp[:, 0:128].rearrange("p (j b) -> p j b", b=16), AF.Copy),
                 reads=[ppn], writes=["rc"])
        import os
        SS = int(os.environ.get("SS", "9"))
        if SS < 2:
            return
        S.op("pool", lambda e: e.tensor_copy(hs_b[:].rearrange("p j c b -> p c j b"), hs_f), reads=["rc"], writes=["hs_b"])
        if SS < 3:
            return
        pes = [next_ps() for _ in range(4)]
        u4 = us_b[:].rearrange("p i (b t) -> p i b t", t=4)
        for i in range(4):
            for jj in range(4):
                pe_, pen = pes[jj]
                for comp in range(2):
                    col = (comp * 4 + i) * 16
                    for tau in range(4):
                        mm(pe_[:, col:col + 16], WE[jj * 32:(jj + 1) * 32, i, comp, 3 - tau, :], u4[jj * 32:(jj + 1) * 32, i, :, tau],
                           tau == 0, tau == 3, ["R_we", "us_b"], [pen], tile_position=(32 * jj, 0))
        if SS < 4:
            return
        p4r = PWr[:, :, 4:5].to_broadcast([128, 16, 16])
        p4i = PWi[:, :, 4:5].to_broadcast([128, 16, 16])
        T = [Ctmp[i].rearrange("p (j b) -> p j b", b=16) for i in range(4)]
        t2("dve", T[0], hs_f[:, 0], p4r, ALU.mult, ["rc", "PW"], ["ct0"])
        t2("dve", T[1], hs_f[:, 1], p4i, ALU.mult, ["rc", "PW"], ["ct1"])
        t2("dve", T[2], hs_f[:, 0], p4i, ALU.mult, ["rc", "PW"], ["ct2"])
        t2("dve", T[3], hs_f[:, 1], p4r, ALU.mult, ["rc", "PW"], ["ct3"])
        t2("dve", T[0], T[0], T[1], ALU.subtract, ["ct0", "ct1"], ["ct0"])
        t2("dve", T[2], T[2], T[3], ALU.add, ["ct2", "ct3"], ["ct2"])
        for jj in range(4):
            pe_, pen = pes[jj]
            pe3 = pe_[:, 0:128].rearrange("p (c i b) -> p c i b", c=2, i=4)
            for comp, Tc, tn in ((0, T[0], "ct0"), (1, T[2], "ct2")):
                S.op("dve", lambda e, jj=jj, comp=comp, Tc=Tc, pe3=pe3: e.tensor_tensor(
                    fin[:, comp].rearrange("p (i q) b -> p i q b", q=4)[:, :, jj, :], Tc.rearrange("p (i q) b -> p i q b", q=4)[:, :, jj, :],
                    pe3[:, comp], ALU.add), reads=[tn, pen], writes=["rs"])
        if SS < 5:
            return
        for comp, dst_ in ((0, o_ssre), (1, o_ssim)):
            xt, xtn = xtok[comp], f"xtok{comp}"
            for jq in range(4):
                pp, ppn = next_ps()
                for j4 in range(4):
                    j = jq * 4 + j4
                    S.op("pe", lambda e, pp=pp, j4=j4, j=j, comp=comp: e.transpose(pp[0:16, j4 * 128:(j4 + 1) * 128], fin[:, comp, j, :], ident[:]),
                         reads=["rs", "ident"], writes=[ppn])
                S.op("act", lambda e, pp=pp, jq=jq, xt=xt: e.activation(xt[0:16, (jq % 2) * 512:(jq % 2) * 512 + 512], pp[0:16, :], AF.Copy),
                     reads=[ppn], writes=[xtn])
                if jq % 2 == 1:
                    S.dma("sp", dst_.rearrange("b g p -> b (g p)")[:, (jq // 2) * 1024:(jq // 2 + 1) * 1024], xt[0:16, 0:1024], "ystore%d" % comp,
                          reads=[xtn], writes=["o_ss"])

    def sample_attention():
        kcT = pt[:].rearrange("p a n -> p (a n)").rearrange("p (b k) -> p b k", k=128)
        vc = kq[:].rearrange("p a n -> p (a n)").bitcast(BF16).rearrange("p (b c) -> p b c", c=128)
        vnew = Cb[0:4, 8192:10240].rearrange("p (b c) -> p b c", c=128)
        for hf in range(2):
            xt, xtn = xtok[hf], f"xtok{hf}"
            S.dma("sp", xt[:, :].rearrange("p (b c) -> p b c", c=128), w["swa_k"][8 * hf:8 * hf + 8].rearrange("b p c -> p b c"), xtn, writes=[xtn])
            for bq in range(2):
                pp, ppn = next_ps()
                for b4 in range(4):
                    b8 = bq * 4 + b4
                    S.op("pe", lambda e, pp=pp, b4=b4, b8=b8, xt=xt: e.transpose(pp[:, b4 * 128:(b4 + 1) * 128], xt[:, b8 * 128:(b8 + 1) * 128], ident[:]),
                         reads=[xtn, "ident"], writes=[ppn])
                S.op("act", lambda e, pp=pp, hf=hf, bq=bq: e.activation(kcT[:, hf * 8 + bq * 4:hf * 8 + bq * 4 + 4, :], pp[:].rearrange("p (b k) -> p b k", k=128), AF.Copy),
                     reads=[ppn], writes=["pt0c"])
        for hf in range(2):
            xt, xtn = xtok[hf], f"xtok{hf}"
            S.dma("sp", xt[:, :].rearrange("p (b c) -> p b c", c=128), w["swa_v"][8 * hf:8 * hf + 8].rearrange("b p c -> p b c"), xtn, writes=[xtn])
            S.op("pool", lambda e, hf=hf, xt=xt: e.tensor_copy(vc[:, hf * 8:(hf + 1) * 8, :], xt[:, :].rearrange("p (b c) -> p b c", c=128)),
                 reads=[xtn], writes=["kq"])
        for b_ in range(16):
            S.dma("sp", vnew[0:4, b_, :], vs_b[4 * b_:4 * b_ + 4, :], "vnew", reads=["vs_b"], writes=["C2"])
        psc, pscn = next_ps()
        psn, psnn = next_ps()
        for b_ in range(16):
            for hh in range(2):
                hs = slice(hh * 64, (hh + 1) * 64)
                for t in range(4):
                    col = ((b_ * 2 + hh) * 4 + t) * 4
                    mm(psc[:, col:col + 4], kcT[hs, b_, :], qs_b[hs, t, 4 * b_:4 * b_ + 4], True, True, ["pt0c", "qs_b"], [pscn])
                    mm(psn[0:4, col:col + 4], ks_b[hs, 4 * b_:4 * b_ + 4], qs_b[hs, t, 4 * b_:4 * b_ + 4], True, True, ["ks_b", "qs_b"], [psnn])
        Pc, Pn = sg[0], sg[1]
        S.op("act", lambda e: e.activation(Pc[:, :], psc[:, :], AF.Exp, scale=0.125), reads=[pscn], writes=["sg0"])
        S.op("act", lambda e: e.activation(Pn[0:4, :], psn[0:4, :], AF.Exp, scale=0.125), reads=[psnn], writes=["sg1"])
        S.op("pool", lambda e: e.tensor_tensor(Pc[:, :].rearrange("p (a q) -> p a q", q=4), Pc[:, :].rearrange("p (a q) -> p a q", q=4),
                                               masks[:, 3, 0:4].unsqueeze(1).to_broadcast([128, 128, 4]), ALU.mult), reads=["sg0", "masks"], writes=["sg0"])
        S.op("pool", lambda e: e.tensor_tensor(Pn[0:4, :].rearrange("p (a q) -> p a q", q=4), Pn[0:4, :].rearrange("p (a q) -> p a q", q=4),
                                               masks[0:4, 4, 0:4].unsqueeze(1).to_broadcast([4, 128, 4]), ALU.mult), reads=["sg1", "masks"], writes=["sg1"])
        po, pon = next_ps()
        pd, pdn = next_ps()
        for b_ in range(16):
            for hh in range(2):
                hs = slice(hh * 64, (hh + 1) * 64)
                col = (b_ * 2 + hh) * 16
                mm(po[0:64, col:col + 16], vc[:, b_, hs], Pc[:, col:col + 16], True, False, ["kq", "sg0"], [pon])
                mm(po[0:64, col:col + 16], vnew[0:4, b_, hs], Pn[0:4, col:col + 16], False, True, ["C2", "sg1"], [pon])
                mm(pd[0:64, col:col + 16], ones_bf[:, 0:64], Pc[:, col:col + 16], True, False, ["ones", "sg0"], [pdn])
                mm(pd[0:64, col:col + 16], ones_bf[0:4, 0:64], Pn[0:4, col:col + 16], False, True, ["ones", "sg1"], [pdn])
        S.op("dve", lambda e: e.tensor_tensor(att_t[0:64, :].rearrange("p (b h q) -> p b h q", h=8, q=4), pd[0:64, :].rearrange("p (b h q) -> p b h q", h=8, q=4),
                                              sinkexp[0:64, :].unsqueeze(1).unsqueeze(3).to_broadcast([64, 16, 8, 4]), ALU.add),
             reads=[pdn, "sinkexp"], writes=["att_t"])
        S.op("dve", lambda e: e.reciprocal(att_t[0:64, :], att_t[0:64, :]), reads=["att_t"], writes=["att_t"])
        S.op("dve", lambda e: e.tensor_tensor(o_all[0:64, :, 0:64].rearrange("p h (b q) -> p b h q", q=4), po[0:64, :].rearrange("p (b h q) -> p b h q", h=8, q=4),
                                              att_t[0:64, :].rearrange("p (b h q) -> p b h q", h=8, q=4), ALU.mult),
             reads=[pon, "att_t"], writes=["cT"])

    def sample_mem_scores():
        KTb = mkT
        Vb = mv_b
        vst = Cf[:, 4096:5120]
        Ps0, Ps1 = sg[0], sg[1]
        pC, pCn = PS[6], "ps6"
        pD, pDn = PS[7], "ps7"
        for b_ in range(16):
            for mt in range(2):
                xt, xtn = xtok[mt], f"xtok{mt}"
                S.dma("sp", xt[:, :], w["memk"][b_, mt * 128:(mt + 1) * 128, :], xtn, writes=[xtn])
                for kq_ in range(2):
                    pp, ppn = next_ps()
                    for k4 in range(4):
                        kt = kq_ * 4 + k4
                        S.op("pe", lambda e, pp=pp, k4=k4, kt=kt, xt=xt: e.transpose(pp[:, k4 * 128:(k4 + 1) * 128], xt[:, kt * 128:(kt + 1) * 128], ident[:]),
                             reads=[xtn, "ident"], writes=[ppn])
                    S.op("act", lambda e, pp=pp, kq_=kq_, mt=mt: e.activation(KTb[:, kq_ * 4:(kq_ + 1) * 4, mt * 128:(mt + 1) * 128],
                                                                             pp[:].rearrange("p (k n) -> p k n", n=128), AF.Copy),
                         reads=[ppn], writes=["mkT"])
                S.dma("sp", vst, w["memv"][b_, mt * 128:(mt + 1) * 128, :], "vst", writes=["C2"])
                S.op("pool", lambda e, mt=mt: e.tensor_copy(Vb[:, mt, :], vst), reads=["C2"], writes=["mv_b"])
            psS = []
            for mt in range(2):
                pp, ppn = next_ps()
                psS.append((pp, ppn))
                for hm in range(4):
                    for half in range(2):
                        mm(pp[:, hm * 4:(hm + 1) * 4], KTb[:, hm * 2 + half, mt * 128:(mt + 1) * 128], qmT[:, hm * 2 + half, 4 * b_:4 * b_ + 4],
                           half == 0, half == 1, ["mkT", "qmT"], [ppn])
            for mt, Pt, pn_ in ((0, Ps0, "sg0"), (1, Ps1, "sg1")):
                pp, ppn = psS[mt]
                S.op("act", lambda e, pp=pp, Pt=Pt: e.activation(Pt[:, 0:16], pp[:, 0:16], AF.Exp, scale=1.0 / 16.0), reads=[ppn], writes=[pn_])
            for hm in range(4):
                for half in range(2):
                    col = ((b_ * 4 + hm) * 2 + half) * 4
                    for mt, Pt, pn_ in ((0, Ps0, "sg0"), (1, Ps1, "sg1")):
                        mm(pC[:, col:col + 4], Vb[:, mt, (hm * 2 + half) * 128:(hm * 2 + half + 1) * 128], Pt[:, hm * 4:(hm + 1) * 4],
                           mt == 0, mt == 1, ["mv_b", pn_], [pCn])
                cold = (b_ * 4 + hm) * 4
                for mt, Pt, pn_ in ((0, Ps0, "sg0"), (1, Ps1, "sg1")):
                    mm(pD[:, cold:cold + 4], ones_bf[:, :], Pt[:, hm * 4:(hm + 1) * 4], mt == 0, mt == 1, ["ones", pn_], [pDn])
        S.op("dve", lambda e: e.reciprocal(rc[:, 0:256], pD[:, 0:256]), reads=[pDn], writes=["rc"])
        for half in range(2):
            S.op("dve", lambda e, half=half: e.tensor_tensor(
                xn[:, :, 0:64].rearrange("p (hm hf) (b q) -> p hf b hm q", hf=2, q=4)[:, half],
                pC[:, :].rearrange("p (b hm hf q) -> p hf b hm q", hm=4, hf=2, q=4)[:, half],
                rc[:, 0:256].rearrange("p (b hm q) -> p b hm q", hm=4, q=4), ALU.mult),
                reads=[pCn, "rc"], writes=["xn"])

    def sample_part_b():
        N = 64
        S.barrier()
        S.op("pool", lambda e: e.tensor_copy(hT[:, :, 0:64], h1s[:]), reads=["h1s"], writes=["hT"])
        S.op("pool", lambda e: e.tensor_copy(u_bf[:, :, 0:64], us_b[:]), reads=["us_b"], writes=["u_bf"])
        S.op("pool", lambda e: e.tensor_copy(hbk[:, :, :, 0:16], hs_b[:]), reads=["hs_b"], writes=["hbk"])
        ssm_out(N, lambda i: u_bf[:, i, 0:64].rearrange("p (b t) -> p b t", t=4), lambda pp: pp[:, 0:64].rearrange("p (b t) -> p b t", t=4),
                lambda j, comp: hbk[:, j, comp, 0:16], 4, 4)
        sample_attention()
        attn_norm_outproj(N)
        cross_attn(N, sample_mem_scores)
        ffn("ffn2", GV["ffn2_post"], N)
        store_block_T(ys_out, 1, W=64)

    kb._ns = dict(locals())
    return kb


def assemble(cfg):
    kb = build(cfg)
    ns = kb._ns
    S = kb.S
    NB = cfg["NB"]
    S.alias.update({"hmid": ["C0", "C1", "C2"], "gf": ["C0"], "gb": ["C1"], "ysn": ["C1"], "qmT": ["C0"], "Pm": ["C1"],
                    "Xre": ["C0"], "Xim": ["C0"], "Cpr": ["C1"], "Cpi": ["C1"], "cm_ta": ["C2"], "cm_tb": ["C2"],
                    "ct0": ["C2"], "ct1": ["C2"], "ct2": ["C2"], "ct3": ["C2"],
                    "wt0": ["C0"], "wt1": ["C0"], "wt2": ["C1"], "wt3": ["C1"], "hst": ["C0", "C1"]})
    S.alias.update({"xn": ["xn_a", "xn_b"], "ysn_a": ["C1"], "ysn_b": ["C1"],
                    "cT": [f"cT{k}" for k in range(8)], "cT_a": [f"cT{k}" for k in range(4)], "cT_b": [f"cT{k}" for k in range(4, 8)],
                    "hT": [f"hT{k}" for k in range(8)]})
    for j in range(16):
        S.alias[f"Zq{j}r"] = ["C0"]
        S.alias[f"Zq{j}i"] = ["C1"]
    for j8 in range(8):
        S.alias[f"Zp{j8}r"] = ["C0"]
        S.alias[f"Zp{j8}i"] = ["C1"]
    stage = cfg.get("stage", "full")
    stop = cfg.get("stop", 99)
    steps = [lambda: ns["ssm_prep_params"](), lambda: ns["build_WE"](), lambda: ns["memory_kv"](), lambda: ns["p1_block"](-1),
             lambda: [ns["p1_prev_block"](b) for b in range(NB)] + [ns["p1_block"](b) for b in range(NB)],
             lambda: (ns["sample_part_a"]() if cfg.get("sample", True) else None),
             lambda: (S.barrier(), ns["prev_state"](), S.barrier(), ns["phase15"]()),
             lambda: (ns["sample_state"]() if cfg.get("sample", True) else None),
             lambda: (ns["precast_phase2_weights"]() if cfg.get("precast", False) else None, ns["exchange_and_correct"](), S.barrier()),
             lambda: ns["build_KD"](), lambda: (ns["build_WO"](), S.barrier()),
             lambda: [ns["p2_block"](b) for b in range(NB)],
             lambda: (ns["sample_part_b"]() if cfg.get("sample", True) else None)]
    for i, st_ in enumerate(steps):
        if i >= stop:
            break
        st_()
        ns["flush_pending"]()
    S.barrier()
    kb.nsem = S.emit(kb.st)
    kb.st.close()
    return kb


def _rope_tables(pos):
    half = 8
    inv = (500000.0 ** (-np.arange(half, dtype=np.float32) * (2.0 / 16))).astype(np.float32)
    ang = pos.astype(np.float32)[None, :] * inv[:, None]
    cos, sin = np.cos(ang).astype(np.float32), np.sin(ang).astype(np.float32)
    T = pos.shape[0]
    Cc = np.ones((64, T), np.float32)
    Ss = np.zeros((64, T), np.float32)
    Cc[0:8] = cos
    Cc[8:16] = cos
    Ss[0:8] = -sin
    Ss[8:16] = sin
    return np.concatenate([Cc, Cc], 0), np.concatenate([Ss, Ss], 0)


def _perm_w_in(w_in):
    u = w_in[:, 0:512]
    q = w_in[:, 512:1024].reshape(1024, 8, 64)
    k = w_in[:, 1024:1152].reshape(1024, 2, 64)
    v = w_in[:, 1152:1280]
    swap = np.arange(64)
    swap[0:8] = np.arange(8, 16)
    swap[8:16] = np.arange(0, 8)
    qt = np.stack([np.concatenate([q[:, t, :], q[:, 4 + t, :]], axis=1) for t in range(4)], axis=1).reshape(1024, 512)
    qs = q[:, :, swap]
    qst = np.stack([np.concatenate([qs[:, t, :], qs[:, 4 + t, :]], axis=1) for t in range(4)], axis=1).reshape(1024, 512)
    ks = k[:, :, swap]
    return np.ascontiguousarray(np.concatenate([u, qt, k.reshape(1024, 128), qst, ks.reshape(1024, 128), v], axis=1))


def _gvecs(inp):
    gv = np.zeros((128, GVN), np.float32)
    for nm, key in [("ffn1_pre_g", "ffn1_pre"), ("ffn1_post_g", "ffn1_post"), ("mix_pre_g", "mix_pre"), ("mix_post_g", "mix_post"),
                    ("xa_pre_g", "xa_pre"), ("xa_post_g", "xa_post"), ("ffn2_pre_g", "ffn2_pre"), ("ffn2_post_g", "ffn2_post"),
                    ("mem_norm_g", "mem_norm")]:
        gv[:, GV[key]:GV[key] + 8] = inp[nm].reshape(8, 128).T
    gv[:, GV["out_g"]:GV["out_g"] + 4] = inp["ssm_out_g"].reshape(4, 128).T
    gv[0:64, GV["out_g"] + 4:GV["out_g"] + 12] = inp["attn_out_g"].reshape(8, 64).T
    gv[:, GV["ssm_d"]:GV["ssm_d"] + 4] = inp["ssm_d"].reshape(4, 128).T
    gv[:, GV["b_glu"]:GV["b_glu"] + 4] = inp["ssm_b_glu"].reshape(4, 128).T
    gv[:, GV["sinks"]:GV["sinks"] + 8] = np.broadcast_to(inp["attn_sinks"].reshape(1, 8), (128, 8))
    return gv


def _masks(has_prev):
    m = np.zeros((128, 5, 128), np.float32)
    k = np.arange(128)[:, None]
    q = np.arange(128)[None, :]
    m[:, 0, :] = (k <= q)
    m[:, 1, :] = (k >= q)
    m[:, 2, :] = (k >= q) * (1.0 if has_prev else 0.0)
    m[:, 3, 0:4] = (np.arange(128)[:, None] >= np.arange(4)[None, :])
    m[0:4, 4, 0:4] = (np.arange(4)[:, None] <= np.arange(4)[None, :])
    return m


_WNAMES = ["ffn1_w_gate", "ffn1_w_up", "ffn1_w_down", "ffn2_w_gate", "ffn2_w_up", "ffn2_w_down", "ssm_w_glu", "w_out",
           "w_mem_q", "w_mem_k", "w_mem_v", "w_mem_o"]


def _ssm_pack(inp):
    pk = np.zeros((128, 16, 67), np.float32)
    def gp(a):
        a = a.reshape((16, 2, 64) + a.shape[2:])
        return np.moveaxis(a, 0, 2).reshape((128, 16) + a.shape[3:])
    pk[:, :, 0] = gp(inp["ssm_a_re"])
    pk[:, :, 1] = gp(inp["ssm_a_im"])
    pk[:, :, 2] = gp(np.broadcast_to(inp["ssm_log_step"][:, None], (32, 64)))
    pk[:, :, 3:19] = gp(inp["ssm_b_re"])
    pk[:, :, 19:35] = gp(inp["ssm_b_im"])
    pk[:, :, 35:51] = gp(np.transpose(inp["ssm_c_re"], (0, 2, 1)))
    pk[:, :, 51:67] = gp(np.transpose(inp["ssm_c_im"], (0, 2, 1)))
    return pk


def make_in_maps(inp, NB, ncores, core_list=None):
    ntok = NB * 512
    shared = {k: np.ascontiguousarray(inp[k], dtype=np.float32) for k in _WNAMES}
    shared["w_in_p"] = _perm_w_in(np.asarray(inp["w_in"], np.float32))
    shared["ident"] = np.eye(128, dtype=np.float32)
    shared["gvecs"] = _gvecs(inp)
    shared["ssm_pack"] = _ssm_pack(inp)
    pos_s = 16384.0 + np.tile(np.arange(4, dtype=np.float32), 16)
    shared["ropeCs"], shared["ropeSs"] = _rope_tables(pos_s)
    shared["sin_dev"] = np.zeros((128, 32), np.float32)
    maps = []
    for c in (core_list if core_list is not None else range(ncores)):
        b, half = c // 2, c % 2
        m = dict(shared)
        start = half * ntok
        xp = np.zeros((128 + ntok, 1024), np.float32)
        if half == 1:
            xp[0:128] = inp["x_prompt"][b, start - 128:start]
        xp[128:] = inp["x_prompt"][b, start:start + ntok]
        m["xp"] = xp
        m["xprev"] = np.ascontiguousarray(inp["x_prompt"][b, 0:ntok]) if half == 1 else np.zeros((ntok, 1024), np.float32)
        pos = np.arange(start - 128, start + ntok, dtype=np.float32)
        m["ropeC"], m["ropeS"] = _rope_tables(pos)
        m["masks"] = _masks(half == 1)
        sel = np.zeros((128, 8), np.float32)
        if half == 1:
            sel[:, c - 1] = 1.0
        m["sel"] = sel
        sb = slice(16 * c, 16 * c + 16)
        m["xs"] = np.ascontiguousarray(inp["x_sample"][sb].reshape(64, 1024))
        m["st_re"] = np.ascontiguousarray(inp["state_ssm_re"][sb])
        m["st_im"] = np.ascontiguousarray(inp["state_ssm_im"][sb])
        m["swa_k"] = np.ascontiguousarray(inp["cache_swa_k"][sb].reshape(16, 128, 128))
        m["swa_v"] = np.ascontiguousarray(inp["cache_swa_v"][sb].reshape(16, 128, 128))
        m["memk"] = np.ascontiguousarray(inp["cache_mem_k"][sb].reshape(16, 256, 1024))
        m["memv"] = np.ascontiguousarray(inp["cache_mem_v"][sb].reshape(16, 256, 1024))
        m["mem_p"] = np.ascontiguousarray(inp["mem_prompt"][b])
        maps.append(m)
    return maps


_CACHE = {}


def kernel(**inputs):
    inp = {k: np.asarray(v) for k, v in inputs.items()}
    NB = 8
    if "kb" not in _CACHE:
        _CACHE["kb"] = assemble({"NB": NB})
    kb = _CACHE["kb"]
    maps = make_in_maps(inp, NB, NCORES)
    res = run_bass_kernel_spmd(kb.nc, maps, core_ids=list(range(NCORES))).results
    f32 = np.float32
    y_p = np.stack([np.concatenate([res[2 * b]["y_p"], res[2 * b + 1]["y_p"]], axis=0) for b in range(4)]).astype(f32)
    y_s = np.concatenate([r["y_s"].reshape(16, 4, 1024) for r in res], axis=0).astype(f32)
    p_sre = np.stack([res[2 * b + 1]["p_sre"] for b in range(4)]).astype(f32)
    p_sim = np.stack([res[2 * b + 1]["p_sim"] for b in range(4)]).astype(f32)
    p_wk = np.stack([res[2 * b + 1]["p_wk"].reshape(128, 2, 64) for b in range(4)]).astype(f32)
    p_wv = np.stack([res[2 * b + 1]["p_wv"].reshape(128, 2, 64) for b in range(4)]).astype(f32)
    pm_k = np.stack([res[2 * b]["pm_k"].reshape(256, 4, 256) for b in range(4)]).astype(f32)
    pm_v = np.stack([res[2 * b]["pm_v"].reshape(256, 4, 256) for b in range(4)]).astype(f32)
    s_sre = np.concatenate([r["s_sre"] for r in res], axis=0).astype(f32)
    s_sim = np.concatenate([r["s_sim"] for r in res], axis=0).astype(f32)
    s_wk = np.concatenate([r["s_wk"].reshape(16, 128, 2, 64) for r in res], axis=0).astype(f32)
    s_wv = np.concatenate([r["s_wv"].reshape(16, 128, 2, 64) for r in res], axis=0).astype(f32)
    return (y_p, y_s, p_sre, p_sim, p_wk, p_wv, pm_k, pm_v, s_sre, s_sim, s_wk, s_wv)
```

```python
import contextlib
import math
import numpy as np
import concourse.bass as bass
import concourse.mybir as mybir
from concourse.bass_utils import run_bass_kernel_spmd

F32 = mybir.dt.float32
BF16 = mybir.dt.bfloat16
AF = mybir.ActivationFunctionType
ALU = mybir.AluOpType

D = 1024
DFF = 2816
NCORES = 8
SEQ = 8192
HALF = SEQ // 2
EPS = 1e-6
ENGS = ("pe", "act", "dve", "pool", "sp")


class Buf:
    __slots__ = ("w", "r")

    def __init__(self):
        self.w = None
        self.r = []


class Sched:
    def __init__(self, nc):
        self.nc = nc
        self.ops = {e: [] for e in ENGS}
        self.cnt = {e: 0 for e in ENGS}
        self.seen = {e: {} for e in ENGS}
        self.dcnt = {}
        self.bufs = {}
        self.last_ev = {}
        self.alias = {}

    def _exp(self, names):
        out = []
        for n in names:
            a = self.alias.get(n)
            if a is None:
                out.append(n)
            else:
                out.extend(a)
        return out

    def buf(self, name):
        b = self.bufs.get(name)
        if b is None:
            b = Buf()
            self.bufs[name] = b
        return b

    def _deps(self, eng, reads, writes):
        reads = self._exp(reads)
        writes = self._exp(writes)
        deps = []
        for n in reads:
            b = self.buf(n)
            if b.w is not None:
                deps.append(b.w)
            if n.startswith("ps"):
                deps.extend(ev for ev in b.r if ev[0] != eng)
        for n in writes:
            b = self.buf(n)
            if b.w is not None:
                deps.append(b.w)
            deps.extend(b.r)
        seen = self.seen[eng]
        best = {}
        for (k, v) in deps:
            if k == "pe" and eng == "pe":
                continue
            if seen.get(k, 0) >= v:
                continue
            if best.get(k, 0) < v:
                best[k] = v
        for k, v in best.items():
            seen[k] = v
        return list(best.items())

    def _commit(self, ev, reads, writes):
        reads = self._exp(reads)
        writes = self._exp(writes)
        self.last_ev[ev[0]] = ev[1]
        for n in writes:
            b = self.buf(n)
            b.w = ev
            b.r = []
        for n in reads:
            b = self.buf(n)
            b.r.append(ev)
            if len(b.r) > 48:
                best = {}
                for k, v in b.r:
                    if best.get(k, 0) < v:
                        best[k] = v
                b.r = list(best.items())

    def op(self, eng, fn, reads=(), writes=()):
        waits = self._deps(eng, reads, writes)
        self.cnt[eng] += 1
        ev = (eng, self.cnt[eng])
        self.ops[eng].append((waits, fn, (eng, 1)))
        self._commit(ev, reads, writes)
        return ev

    def dma(self, q, out, in_, key, reads=(), writes=(), **kw):
        waits = self._deps(q, reads, writes)
        n = self.dcnt.get(key, 0) + 1
        self.dcnt[key] = n
        ev = ("d:" + key, 16 * n)

        def fn(e, out=out, in_=in_, kw=kw):
            return e.dma_start(out=out, in_=in_, **kw)

        self.ops[q].append((waits, fn, ("d:" + key, 16)))
        self._commit(ev, reads, writes)
        return ev

    def barrier(self):
        for e in ENGS:
            waits = []
            for k, v in self.last_ev.items():
                if k == e and e == "pe":
                    continue
                if self.seen[e].get(k, 0) < v:
                    self.seen[e][k] = v
                    waits.append((k, v))
            if waits:
                self.ops[e].append((waits, None, None))

    def emit(self, stack):
        nc = self.nc
        keys = set(["pe", "act", "dve", "pool"])
        for k in self.dcnt:
            keys.add("d:" + k)
        sems = {}
        for k in sorted(keys):
            sems[k] = stack.enter_context(nc.semaphore("s_" + k.replace(":", "_")))
        block = stack.enter_context(nc.Block())
        ops = self.ops

        def run(e, lst):
            for waits, fn, inc in lst:
                for (k, v) in waits:
                    e.wait_ge(sems[k], v)
                if fn is not None:
                    ins = fn(e)
                    if inc is not None:
                        ins.then_inc(sems[inc[0]], inc[1])

        @block.tensor
        def _(e):
            run(e, ops["pe"])

        @block.scalar
        def _(e):
            run(e, ops["act"])

        @block.vector
        def _(e):
            run(e, ops["dve"])

        @block.gpsimd
        def _(e):
            run(e, ops["pool"])

        @block.sync
        def _(e):
            run(e, ops["sp"])
        return len(sems)


class K:
    def __init__(self, cfg):
        self.cfg = cfg
        nd = cfg.get("num_devices")
        self.nc = bass.Bass("TRN2", target_bir_lowering=False, num_devices=nd) if nd else bass.Bass("TRN2", target_bir_lowering=False)
        self.S = Sched(self.nc)
        self.st = contextlib.ExitStack()
        self.din = {}
        self.dout = {}
        self.uid = 0

    def inp(self, name, shape, dt=F32):
        t = self.nc.dram_tensor(name, list(shape), dt, kind="ExternalInput").ap()
        self.din[name] = t
        return t

    def outp(self, name, shape, dt=F32):
        t = self.nc.dram_tensor(name, list(shape), dt, kind="ExternalOutput").ap()
        self.dout[name] = t
        return t

    def scratch(self, name, shape, dt):
        return self.nc.dram_tensor(name, list(shape), dt, kind="Internal").ap()

    def sb(self, name, shape, dt):
        return self.st.enter_context(self.nc.sbuf_tensor(name, list(shape), dt))

    def ps(self, name, shape, dt=F32):
        return self.st.enter_context(self.nc.psum_tensor(name, list(shape), dt))


SLOT = 3072
NSLOT = 3
STG = 1024
TWO_PI = 2.0 * math.pi

GV = {"ffn1_pre": 0, "ffn1_post": 8, "mix_pre": 16, "mix_post": 24, "xa_pre": 32, "xa_post": 40,
      "ffn2_pre": 48, "ffn2_post": 56, "mem_norm": 64, "out_g": 72, "ssm_d": 84, "b_glu": 88, "sinks": 92}
GVN = 104


class WSpec:
    def __init__(self, name, vmts, KT, mpp, gcols=None):
        self.name = name
        self.vmts = vmts
        self.KT = KT
        self.mpp = mpp
        self.gcols = gcols
        self.npan = (len(vmts) + mpp - 1) // mpp
        self.cast_done = set()
        self.scr = None


def build(cfg):
    kb = K(cfg)
    nc, S = kb.nc, kb.S
    NB = cfg["NB"]
    NTOK = NB * 512
    NCK = NTOK // 16
    stage = cfg.get("stage", "full")
    use_cc = cfg.get("collective", True)
    NCC = cfg.get("ncc", NCORES)
    do_sample = cfg.get("sample", True)
    dbg = cfg.get("dbg", False) or cfg.get("stage", "full").startswith("dbg")
    I32 = mybir.dt.int32

    xp = kb.inp("xp", [128 + NTOK, D])
    xs = kb.inp("xs", [64, D])
    xprev = kb.inp("xprev", [NTOK, D])
    ident_d = kb.inp("ident", [128, 128])
    gv = kb.inp("gvecs", [128, GVN])
    masks_d = kb.inp("masks", [128, 5, 128])
    ropeC_d = kb.inp("ropeC", [128, 128 + NTOK])
    ropeS_d = kb.inp("ropeS", [128, 128 + NTOK])
    ropeCs_d = kb.inp("ropeCs", [128, 64])
    ropeSs_d = kb.inp("ropeSs", [128, 64])
    sel_d = kb.inp("sel", [128, 8])
    w = {}
    for nm, shp in [("ffn1_w_gate", [D, DFF]), ("ffn1_w_up", [D, DFF]), ("ffn1_w_down", [DFF, D]),
                    ("ffn2_w_gate", [D, DFF]), ("ffn2_w_up", [D, DFF]), ("ffn2_w_down", [DFF, D]),
                    ("w_in_p", [D, 1920]), ("ssm_w_glu", [512, 512]), ("w_out", [D, D]),
                    ("w_mem_q", [D, D]), ("w_mem_k", [D, D]), ("w_mem_v", [D, D]), ("w_mem_o", [D, D]),
                    ("ssm_pack", [128, 16, 67]),
                    ("st_re", [16, 32, 64]), ("st_im", [16, 32, 64]),
                    ("swa_k", [16, 128, 128]), ("swa_v", [16, 128, 128]),
                    ("memk", [16, 256, D]), ("memv", [16, 256, D]), ("mem_p", [256, D])]:
        w[nm] = kb.inp(nm, shp)
    y_out = kb.outp("y_p", [NTOK, D])
    ys_out = kb.outp("y_s", [64, D])
    o_psre = kb.outp("p_sre", [32, 64])
    o_psim = kb.outp("p_sim", [32, 64])
    o_pwk = kb.outp("p_wk", [128, 128])
    o_pwv = kb.outp("p_wv", [128, 128])
    o_pmk = kb.outp("pm_k", [256, D])
    o_pmv = kb.outp("pm_v", [256, D])
    o_ssre = kb.outp("s_sre", [16, 32, 64])
    o_ssim = kb.outp("s_sim", [16, 32, 64])
    o_swk = kb.outp("s_wk", [16, 128, 128])
    o_swv = kb.outp("s_wv", [16, 128, 128])
    dbg_o = kb.outp("dbg", [128, 8, 512]) if dbg else None
    h1_scr = kb.scratch("h1_scr", [NB, 128, 4096], F32)
    q_scr = kb.scratch("q_scr", [NB, 128, 2048], BF16)
    k_scr = kb.scratch("k_scr", [128, 128 + NTOK], BF16)
    v_scr = kb.scratch("v_scr", [1 + NB * 4, 128, 128], BF16)
    u_scr = kb.scratch("u_scr", [128, 4, NTOK], BF16)
    uprev_scr = kb.scratch("uprev_scr", [128, 4, NTOK], BF16)
    hb_scr = kb.scratch("hb_scr", [128, 16, 2, NCK], BF16)
    cc_in = kb.scratch("cc_in", [128, 32], F32)
    cc_out = kb.scratch("cc_out", [NCC * 128, 32], F32)

    A = kb.sb("A", [128, 4096], F32)
    B = kb.sb("B", [128, 4096], F32)
    C = kb.sb("C", [128, 11264], BF16)
    Dd = kb.sb("Dd", [128, 4096], BF16)
    R = kb.sb("R", [128, 24576], BF16)
    wslot = [kb.sb(f"wslot{i}", [128, SLOT], BF16) for i in range(NSLOT)]
    wstages = [kb.sb(f"wstage{i}", [128, STG], F32) for i in range(2)]
    xtok = [kb.sb(f"xtok{i}", [128, D], F32) for i in range(2)]
    ident = kb.sb("ident_sb", [128, 128], F32)
    ident_bf = kb.sb("ident_bf", [128, 128], BF16)
    ones_bf = kb.sb("ones_bf", [128, 128], BF16)
    gvs = kb.sb("gvs", [128, GVN], F32)
    cst = kb.sb("cst", [128, 8], F32)
    masks = kb.sb("masks_b", [128, 5, 128], BF16)
    sqb = [kb.sb(f"sqb{i}", [128, 512], BF16) for i in range(4)]
    sg = [kb.sb(f"sg{i}", [128, 512], BF16) for i in range(2)]
    rstd = kb.sb("rstd", [128, 512], F32)
    rc = kb.sb("rc", [128, 512], F32)
    rs = kb.sb("rs", [128, 512], F32)
    kq = kb.sb("kq", [128, 2, 512], F32)
    krot_b = kb.sb("krot_b", [128, 512], BF16)
    u_bf = kb.sb("u_bf", [128, 4, 512], BF16)
    qrot = kb.sb("qrot", [128, 4, 512], BF16)
    v_bf = kb.sb("v_bf", [128, 5, 128], BF16)
    v_f = kb.sb("v_f", [128, 128], F32)
    kblk = kb.sb("kblk", [128, 640], BF16)
    pt = kb.sb("ptile", [128, 4, 512], BF16)
    hbk = kb.sb("hbk", [128, 16, 2, 32], BF16)
    sinkexp = kb.sb("sinkexp", [64, 8], F32)
    att_t = kb.sb("att_t", [64, 512], F32)
    mkT = kb.sb("mkT", [128, 8, 256], BF16)
    mv_b = kb.sb("mv_b", [128, 2, D], BF16)
    sp = kb.sb("ssm_p", [128, 16, 24], F32)
    PWr = kb.sb("PWr", [128, 16, 17], F32)
    PWi = kb.sb("PWi", [128, 16, 17], F32)
    BBr = kb.sb("BBr", [128, 16, 16], F32)
    BBi = kb.sb("BBi", [128, 16, 16], F32)
    CRt = kb.sb("CRt", [128, 16, 16], F32)
    CIt = kb.sb("CIt", [128, 16, 16], F32)
    sti = kb.sb("sti", [128, 16], I32)
    AS = kb.sb("AS", [128, 3, 9, 16], F32)
    sin_f = kb.sb("sin_f", [128, 2, 16], F32)
    selt = kb.sb("selt", [128, 8], F32)
    h1s = kb.sb("h1s", [128, 8, 64], F32)
    hs_b = kb.sb("hs_b", [128, 16, 2, 16], BF16)
    us_b = kb.sb("us_b", [128, 4, 64], BF16)
    qs_b = kb.sb("qs_b", [128, 4, 64], BF16)
    ks_b = kb.sb("ks_b", [128, 64], BF16)
    vs_b = kb.sb("vs_b", [64, 128], BF16)
    PS = [kb.ps(f"ps{i}", [128, 512], F32) for i in range(8)]

    SPI = {"ar": 0, "ai": 1, "ls": 2, "dt": 3, "ang": 4, "mag": 5, "sn": 6, "cs": 7, "lbr": 8, "lbi": 9,
           "cre": 10, "cim": 11, "t0": 12, "t1": 13, "t2": 14, "t3": 15, "nlbi": 16}

    def spc(nm):
        return sp[:, :, SPI[nm]]

    hT = A[:].rearrange("p (k n) -> p k n", n=512)
    cT = B[:].rearrange("p (k n) -> p k n", n=512)
    Cb = C[:]
    Cf = C[:].bitcast(F32)
    hmid = Cb.rearrange("p (k n) -> p k n", n=512)
    xn = Dd[:].rearrange("p (k n) -> p k n", n=512)

    S.dma("sp", ident[:], ident_d, "c_ident", writes=["ident"])
    S.dma("sp", gvs[:], gv, "c_gv", writes=["gvs"])
    S.dma("sp", xtok[0][:, 0:640].rearrange("p (a b) -> p a b", b=128), masks_d, "c_masks", writes=["xtok0"])
    S.dma("sp", selt[:], sel_d, "c_sel", writes=["selt"])
    S.op("pool", lambda e: e.memset(ones_bf[:], 1.0), writes=["ones"])
    S.op("pool", lambda e: e.memset(cst[:, 0:1], EPS), writes=["cst"])
    S.op("pool", lambda e: e.memset(cst[:, 1:2], math.log(0.5)), writes=["cst"])
    S.op("pool", lambda e: e.memset(cst[:, 2:3], 0.0), writes=["cst"])
    S.op("pool", lambda e: e.memset(cst[:, 3:4], 1.0), writes=["cst"])
    S.op("dve", lambda e: e.tensor_copy(masks[:], xtok[0][:, 0:640].rearrange("p (a b) -> p a b", b=128)), reads=["xtok0"], writes=["masks"])
    S.op("dve", lambda e: e.tensor_copy(ident_bf[:], ident[:]), reads=["ident"], writes=["ident_bf"])
    S.op("act", lambda e: e.activation(sinkexp[:], gvs[0:64, GV["sinks"]:GV["sinks"] + 8], AF.Exp), reads=["gvs"], writes=["sinkexp"])

    wstate = {"slot": 0, "stg": 0, "pending": []}

    def flush_pending(keep=0):
        while len(wstate["pending"]) > keep:
            wstate["pending"].pop(0)()


    def std_vmts(src, KT, ncols):
        vm = []
        for mt in range(ncols // 128):
            pieces = []
            k0 = 0
            while k0 < KT:
                kn = min(8, KT - k0)
                pieces.append((src[k0 * 128:(k0 + kn) * 128, mt * 128:(mt + 1) * 128].rearrange("(kt p) m -> p kt m", p=128), k0, kn, 128))
                k0 += kn
            vm.append(pieces)
        return vm

    specs = {}

    def add_spec(name, vmts, KT, mpp, gcols=None):
        sp_ = WSpec(name, vmts, KT, mpp, gcols)
        sp_.scr = kb.scratch("scr_" + name, [sp_.npan, 128, SLOT], BF16)
        specs[name] = sp_
        return sp_

    for pfx, gk in (("ffn1", "ffn1_pre"), ("ffn2", "ffn2_pre")):
        g_, u_ = std_vmts(w[pfx + "_w_gate"], 8, DFF), std_vmts(w[pfx + "_w_up"], 8, DFF)
        vm = []
        for mt in range(22):
            vm.append(g_[mt])
            vm.append(u_[mt])
        add_spec(pfx + "_gu", vm, 8, 2, gvs[:, GV[gk]:GV[gk] + 8])
        add_spec(pfx + "_dn", std_vmts(w[pfx + "_w_down"], 22, D), 22, 1)
    add_spec("w_in", std_vmts(w["w_in_p"], 8, 1920), 8, 3, gvs[:, GV["mix_pre"]:GV["mix_pre"] + 8])
    add_spec("w_glu", std_vmts(w["ssm_w_glu"], 4, 512), 4, 4)
    vm = []
    for mt in range(8):
        vm.append([(w["w_out"][0:512, mt * 128:(mt + 1) * 128].rearrange("(kt p) m -> p kt m", p=128), 0, 4, 128),
                   (w["w_out"][512:1024, mt * 128:(mt + 1) * 128].rearrange("(kt p) m -> p kt m", p=64), 4, 8, 64)])
    add_spec("w_out", vm, 12, 2, gvs[:, GV["out_g"]:GV["out_g"] + 12])
    add_spec("w_mem_q", std_vmts(w["w_mem_q"], 8, D), 8, 3, gvs[:, GV["xa_pre"]:GV["xa_pre"] + 8])
    add_spec("w_mem_k", std_vmts(w["w_mem_k"], 8, D), 8, 3, gvs[:, GV["mem_norm"]:GV["mem_norm"] + 8])
    add_spec("w_mem_v", std_vmts(w["w_mem_v"], 8, D), 8, 3, gvs[:, GV["mem_norm"]:GV["mem_norm"] + 8])
    add_spec("w_mem_o", std_vmts(w["w_mem_o"], 8, D), 8, 3)

    def get_panel(spec, pn):
        si = wstate["slot"]
        wstate["slot"] = (si + 1) % NSLOT
        slot = wslot[si]
        sname = f"wslot{si}"
        nv = min(spec.mpp, len(spec.vmts) - pn * spec.mpp)
        KT = spec.KT
        used = nv * KT * 128
        scrname = "scr_" + spec.name + str(pn)
        if pn in spec.cast_done:
            flush_pending()
            S.dma("sp", slot[:, 0:used], spec.scr[pn, :, 0:used], sname, reads=[scrname], writes=[sname])
        else:
            spec.cast_done.add(pn)
            flush_pending(keep=1)
            for ml in range(nv):
                vmt = spec.vmts[pn * spec.mpp + ml]
                for (src, klo, kn, prows) in vmt:
                    gi = wstate["stg"]
                    wstate["stg"] = 1 - gi
                    wstage, wsn = wstages[gi], f"wstage{gi}"
                    S.dma("sp", wstage[0:prows, 0:kn * 128].rearrange("p (k m) -> p k m", m=128), src,
                          wsn, writes=[wsn])
                    dst = slot[0:prows, (ml * KT + klo) * 128:(ml * KT + klo + kn) * 128]
                    ceng = "pool" if (gi == 0 or wstate.get("pool_only")) else "dve"
                    if spec.gcols is not None:
                        gc = spec.gcols[0:prows, klo:klo + kn]
                        S.op(ceng, lambda e, dst=dst, gc=gc, kn=kn, prows=prows, wstage=wstage: e.tensor_tensor(
                            dst.rearrange("p (k m) -> p k m", m=128), wstage[0:prows, 0:kn * 128].rearrange("p (k m) -> p k m", m=128),
                            gc.unsqueeze(2).to_broadcast([prows, kn, 128]), ALU.mult),
                            reads=[wsn, "gvs"], writes=[sname])
                    else:
                        S.op(ceng, lambda e, dst=dst, kn=kn, prows=prows, wstage=wstage: e.tensor_copy(dst, wstage[0:prows, 0:kn * 128]),
                             reads=[wsn], writes=[sname])
            wstate["pending"].append(lambda spec=spec, pn=pn, slot=slot, used=used, si=si, sname=sname, scrname=scrname:
                                     S.dma("sp", spec.scr[pn, :, 0:used], slot[:, 0:used], "scrw%d" % si, reads=[sname], writes=[scrname]))
        return slot, sname

    def precast_phase2_weights():
        wstate["pool_only"] = True
        for nm in ("w_glu", "w_out", "w_mem_q", "w_mem_o", "ffn2_gu", "ffn2_dn"):
            sp_ = specs[nm]
            for pn in range(sp_.npan):
                if pn not in sp_.cast_done:
                    get_panel(sp_, pn)
        flush_pending()
        wstate["pool_only"] = False

    psrr = {"i": 0}

    def next_ps():
        i = psrr["i"]
        psrr["i"] = (i + 1) % 6
        return PS[i], f"ps{i}"

    def mm(out, lhsT, rhs, start, stop, reads, writes, **kw):
        S.op("pe", lambda e: e.matmul(out, lhsT, rhs, start=start, stop=stop, **kw), reads=reads, writes=writes)

    def rms_stats(tiles, srcname, N, P=128, half=False, nfeat=None):
        KT = len(tiles)
        ptile, pname = PS[7], "ps7"
        SQ_ENG = ("dve", "act", "pool", "act", "act", "act", "dve", "act")
        for kt, t in enumerate(tiles):
            sq = sqb[kt % 4]
            sqn = f"sqb{kt % 4}"
            srcname_ = srcname if isinstance(srcname, str) else srcname[kt]
            eng_ = SQ_ENG[kt % 8]
            if eng_ == "act":
                S.op("act", lambda e, t=t, sq=sq: e.activation(sq[0:P, 0:N], t, AF.Square), reads=[srcname_], writes=[sqn])
            else:
                S.op(eng_, lambda e, t=t, sq=sq: e.tensor_tensor(sq[0:P, 0:N], t, t, ALU.mult), reads=[srcname_], writes=[sqn])
            mm(ptile[:, 0:N], ones_bf[0:P, :], sq[0:P, 0:N], kt == 0, kt == KT - 1, [sqn, "ones"], [pname])
        nf = nfeat if nfeat is not None else KT * P
        S.op("act", lambda e: e.activation(rstd[:, 0:N], ptile[:, 0:N], AF.Ln, bias=cst[:, 0:1], scale=1.0 / nf),
             reads=[pname, "cst"], writes=["rstd"])
        bc = cst[:, 1:2] if half else cst[:, 2:3]
        S.op("act", lambda e: e.activation(rstd[:, 0:N], rstd[:, 0:N], AF.Exp, bias=bc, scale=-0.5),
             reads=["rstd", "cst"], writes=["rstd"])

    def norm_to_bf(dst3, dstname, src3, srcname, KT, N, P=128):
        h_ = (KT * 5) // 8 if KT >= 8 else (KT * 3) // 4
        S.op("dve", lambda e: e.tensor_tensor(dst3[:, 0:h_, :], src3[:, 0:h_, :], rstd[0:P, 0:N].unsqueeze(1).to_broadcast([P, h_, N]), ALU.mult),
             reads=[srcname, "rstd"], writes=[dstname + "_a"])
        S.op("pool", lambda e: e.tensor_tensor(dst3[:, h_:KT, :], src3[:, h_:KT, :], rstd[0:P, 0:N].unsqueeze(1).to_broadcast([P, KT - h_, N]), ALU.mult),
             reads=[srcname, "rstd"], writes=[dstname + "_b"])

    def post_residual(hname, gcol0, N, half):
        rms_stats([cT[:, kt, 0:N] for kt in range(8)], [f"cT{kt}" for kt in range(8)], N, half=half)
        S.op("dve", lambda e: e.tensor_tensor(cT[:, 0:6, 0:N], cT[:, 0:6, 0:N], rstd[:, 0:N].unsqueeze(1).to_broadcast([128, 6, N]), ALU.mult),
             reads=["cT_a", "rstd"], writes=["cT_a"])
        S.op("pool", lambda e: e.tensor_tensor(cT[:, 6:8, 0:N], cT[:, 6:8, 0:N], rstd[:, 0:N].unsqueeze(1).to_broadcast([128, 2, N]), ALU.mult),
             reads=["cT_b", "rstd"], writes=["cT_b"])
        for kt in range(8):
            S.op("dve", lambda e, kt=kt: e.scalar_tensor_tensor(hT[:, kt, 0:N], cT[:, kt, 0:N], gvs[:, gcol0 + kt:gcol0 + kt + 1],
                                                                hT[:, kt, 0:N], ALU.mult, ALU.add),
                 reads=[f"cT{kt}", "gvs", f"hT{kt}"], writes=[f"hT{kt}"])

    def linear(spec, rhs_fn, KT, N, evac, vm_sel=None, rnames=("xn",)):
        for pn in range(spec.npan):
            nv = min(spec.mpp, len(spec.vmts) - pn * spec.mpp)
            if vm_sel is not None and not any(vm_sel(pn * spec.mpp + ml) for ml in range(nv)):
                continue
            slot, sname = get_panel(spec, pn)
            for ml in range(nv):
                vmt = pn * spec.mpp + ml
                if vm_sel is not None and not vm_sel(vmt):
                    continue
                r = evac(vmt, None, None, slot, sname, ml)
                if r:
                    continue
                pp, ppn = next_ps()
                for kt in range(KT):
                    rap, kk = rhs_fn(kt)
                    mm(pp[:, 0:N], slot[0:kk, (ml * spec.KT + kt) * 128:(ml * spec.KT + kt + 1) * 128], rap,
                       kt == 0, kt == KT - 1, [sname] + list(rnames), [ppn])
                evac(vmt, pp, ppn, slot, sname, ml)

    def ffn(pfx, gpost, N):
        gu, dnp = specs[pfx + "_gu"], specs[pfx + "_dn"]
        rms_stats([hT[:, kt, 0:N] for kt in range(8)], [f"hT{kt}" for kt in range(8)], N)
        norm_to_bf(xn[:, :, 0:N], "xn", hT[:, :, 0:N], "hT", 8, N)
        state = {}

        def ev_gu(vmt, pp, ppn, slot, sname, ml):
            if pp is None:
                return False
            mt = vmt // 2
            if vmt % 2 == 0:
                sgi = mt % 2
                S.op("act", lambda e: e.activation(sg[sgi][:, 0:N], pp[:, 0:N], AF.Silu), reads=[ppn], writes=[f"sg{sgi}"])
            else:
                sgi = mt % 2
                S.op("dve", lambda e: e.tensor_tensor(hmid[:, mt, 0:N], sg[sgi][:, 0:N], pp[:, 0:N], ALU.mult),
                     reads=[ppn, f"sg{sgi}"], writes=["hmid"])
            return False

        linear(gu, lambda kt: (xn[:, kt, 0:N], 128), 8, N, ev_gu)

        def ev_dn(vmt, pp, ppn, slot, sname, ml):
            if pp is None:
                return False
            S.op("act", lambda e: e.activation(cT[:, vmt, 0:N], pp[:, 0:N], AF.Copy), reads=[ppn], writes=[f"cT{vmt}"])
            return False

        linear(dnp, lambda kt: (hmid[:, kt, 0:N], 128), 22, N, ev_dn, rnames=("hmid",))
        post_residual("hT", gpost, N, True)

    xt_i = {"i": 0}

    xpre = {}

    def x_load(src_rows, s, W):
        xi = xt_i["i"]
        xt_i["i"] = 1 - xi
        S.dma("sp", xtok[xi][0:W, :], src_rows[s * W:(s + 1) * W, :], f"xtok{xi}", writes=[f"xtok{xi}"])
        return xi

    def load_block_T(src_rows, nsub, W=128, key=None, nxt=None):
        pre = xpre.pop(key, []) if key is not None else []
        for s in range(nsub):
            xi = pre[s] if s < len(pre) else x_load(src_rows, s, W)
            xt, xtn = xtok[xi], f"xtok{xi}"
            for kq_ in range(2):
                pp, ppn = next_ps()
                for k4 in range(4):
                    kt = kq_ * 4 + k4
                    S.op("pe", lambda e, pp=pp, k4=k4, kt=kt, xt=xt: e.transpose(pp[:, k4 * 128:k4 * 128 + W], xt[0:W, kt * 128:(kt + 1) * 128], ident[0:W, 0:W]),
                         reads=[xtn, "ident"], writes=[ppn])
                S.op("act", lambda e, pp=pp, kq_=kq_, s=s: e.activation(
                    hT[:, kq_ * 4:(kq_ + 1) * 4, s * W:(s + 1) * W], pp[:].rearrange("p (k n) -> p k n", n=128)[:, :, 0:W], AF.Copy),
                    reads=[ppn], writes=["hT"])
        if nxt is not None:
            xpre[nxt[3]] = [x_load(nxt[0], s, nxt[2]) for s in range(min(2, nxt[1]))]

    def store_block_T(dst_rows, nsub, W=128):
        for s in range(nsub):
            xi = xt_i["i"]
            xt_i["i"] = 1 - xi
            xt, xtn = xtok[xi], f"xtok{xi}"
            for kq_ in range(2):
                pp, ppn = next_ps()
                for k4 in range(4):
                    kt = kq_ * 4 + k4
                    S.op("pe", lambda e, pp=pp, k4=k4, kt=kt, s=s: e.transpose(pp[0:W, k4 * 128:(k4 + 1) * 128], hT[:, kt, s * W:(s + 1) * W], ident[:]),
                         reads=["hT", "ident"], writes=[ppn])
                S.op("act", lambda e, pp=pp, kq_=kq_, xt=xt: e.activation(xt[0:W, kq_ * 512:(kq_ + 1) * 512], pp[0:W, :], AF.Copy),
                     reads=[ppn], writes=[xtn])
            S.dma("sp", dst_rows[s * W:(s + 1) * W, :], xt[0:W, :], "ystore%d" % xi, reads=[xtn], writes=["yout"])

    def in_proj(N, nsub, W, ropec, ropes, need_q, tok0=None, sample=False):
        rms_stats([hT[:, kt, 0:N] for kt in range(8)], [f"hT{kt}" for kt in range(8)], N)
        norm_to_bf(xn[:, :, 0:N], "xn", hT[:, :, 0:N], "hT", 8, N)
        S.dma("sp", rc[:, 0:N], ropec, "rope", writes=["rc"])
        S.dma("sp", rs[:, 0:N], ropes, "rope", writes=["rs"])

        def ev(vmt, pp, ppn, slot, sname, ml):
            if vmt == 14:
                if pp is not None:
                    return False
                ppv, ppvn = next_ps()
                for s in range(nsub):
                    for kt in range(8):
                        mm(ppv[0:W, s * 128:(s + 1) * 128], xn[:, kt, s * W:(s + 1) * W], slot[:, (ml * 8 + kt) * 128:(ml * 8 + kt + 1) * 128],
                           kt == 0, kt == 7, [sname, "xn"], [ppvn])
                S.op("act", lambda e: e.activation(v_bf[0:W, 1:1 + nsub, :], ppv[0:W, 0:nsub * 128].rearrange("p (s m) -> p s m", m=128), AF.Copy),
                     reads=[ppvn], writes=["v_bf"])
                S.op("act", lambda e: e.activation(v_f[0:W, :], ppv[0:W, (nsub - 1) * 128:nsub * 128], AF.Copy), reads=[ppvn], writes=["v_f"])
                return True
            if pp is None:
                return False
            if vmt < 4:
                S.op("act", lambda e: e.activation(u_bf[:, vmt, 0:N], pp[:, 0:N], AF.Copy), reads=[ppn], writes=["u_bf"])
            elif vmt < 8:
                S.op("act", lambda e: e.activation(cT[:, vmt - 4, 0:N], pp[:, 0:N], AF.Copy), reads=[ppn], writes=["cT"])
            elif vmt == 8:
                S.op("act", lambda e: e.activation(kq[:, 0, 0:N], pp[:, 0:N], AF.Copy), reads=[ppn], writes=["kq"])
            elif vmt < 13:
                S.op("act", lambda e: e.activation(cT[:, vmt - 5, 0:N], pp[:, 0:N], AF.Copy), reads=[ppn], writes=["cT"])
            else:
                S.op("act", lambda e: e.activation(kq[:, 1, 0:N], pp[:, 0:N], AF.Copy), reads=[ppn], writes=["kq"])
            return False

        sel = None if need_q else (lambda v: v in (8, 13, 14))
        linear(specs["w_in"], lambda kt: (xn[:, kt, 0:N], 128), 8, N, ev, vm_sel=sel)
        if need_q:
            S.op("dve", lambda e: e.tensor_tensor(cT[:, 0:4, 0:N], cT[:, 0:4, 0:N], rc[:, 0:N].unsqueeze(1).to_broadcast([128, 4, N]), ALU.mult),
                 reads=["cT0", "cT1", "cT2", "cT3", "rc"], writes=["cT0", "cT1", "cT2", "cT3"])
            S.op("pool", lambda e: e.tensor_tensor(cT[:, 4:6, 0:N], cT[:, 4:6, 0:N], rs[:, 0:N].unsqueeze(1).to_broadcast([128, 2, N]), ALU.mult),
                 reads=["cT4", "cT5", "rs"], writes=["cT4", "cT5"])
            S.op("dve", lambda e: e.tensor_tensor(cT[:, 6:8, 0:N], cT[:, 6:8, 0:N], rs[:, 0:N].unsqueeze(1).to_broadcast([128, 2, N]), ALU.mult),
                 reads=["cT6", "cT7", "rs"], writes=["cT6", "cT7"])
            S.op("dve", lambda e: e.tensor_tensor(qrot[:, :, 0:N], cT[:, 0:4, 0:N], cT[:, 4:8, 0:N], ALU.add), reads=["cT"], writes=["qrot"])
        S.op("dve", lambda e: e.tensor_tensor(kq[:, 0, 0:N], kq[:, 0, 0:N], rc[:, 0:N], ALU.mult), reads=["kq", "rc"], writes=["kq"])
        S.op("pool", lambda e: e.tensor_tensor(kq[:, 1, 0:N], kq[:, 1, 0:N], rs[:, 0:N], ALU.mult), reads=["kq", "rs"], writes=["kq"])
        S.op("dve", lambda e: e.tensor_tensor(kq[:, 0, 0:N], kq[:, 0, 0:N], kq[:, 1, 0:N], ALU.add), reads=["kq"], writes=["kq"])
        S.op("act", lambda e: e.activation(krot_b[:, 0:N], kq[:, 0, 0:N], AF.Copy), reads=["kq"], writes=["krot_b"])

    def t2(eng, out, a, b, op, reads, writes):
        S.op(eng, lambda e: e.tensor_tensor(out, a, b, op), reads=reads, writes=writes)

    def cmul(eng, o_re, o_im, a_re, a_im, b_re, b_im, tmp, names_r, names_w, shape):
        ta, tb = tmp
        t2(eng, ta, a_re, b_re, ALU.mult, names_r, ["cm_ta"])
        t2(eng, tb, a_im, b_im, ALU.mult, names_r, ["cm_tb"])
        t2(eng, o_re, ta, tb, ALU.subtract, ["cm_ta", "cm_tb"], names_w)
        t2(eng, ta, a_re, b_im, ALU.mult, names_r + ["cm_ta"], ["cm_ta"])
        t2(eng, tb, a_im, b_re, ALU.mult, names_r + ["cm_tb"], ["cm_tb"])
        t2(eng, o_im, ta, tb, ALU.add, ["cm_ta", "cm_tb"], names_w)

    def reduce_angle(dst, src, add):
        S.op("dve", lambda e: e.tensor_scalar(spc("t0"), src, add, None, ALU.add), reads=["sp"], writes=["sp"])
        S.op("dve", lambda e: e.tensor_scalar(spc("t1"), spc("t0"), 1.0 / TWO_PI, None, ALU.mult), reads=["sp"], writes=["sp"])
        S.op("dve", lambda e: e.tensor_copy(sti[:], spc("t1")), reads=["sp"], writes=["sti"])
        S.op("dve", lambda e: e.tensor_copy(spc("t1"), sti[:]), reads=["sti"], writes=["sp"])
        S.op("dve", lambda e: e.scalar_tensor_tensor(dst, spc("t1"), -TWO_PI, spc("t0"), ALU.mult, ALU.add), reads=["sp"], writes=["sp"])

    Xre = Cb[:, 0:2048].rearrange("p (j m) -> p j m", m=128)
    Xim = Cb[:, 2048:4096].rearrange("p (j m) -> p j m", m=128)
    Cpr = Cb[:, 4096:6144].rearrange("p (j m) -> p j m", m=128)
    Cpi = Cb[:, 6144:8192].rearrange("p (j m) -> p j m", m=128)
    Ctmp = [Cf[:, 4096 + i * 256:4096 + (i + 1) * 256] for i in range(4)]

    WE = R[:, 0:16384].rearrange("p (i c d m) -> p i c d m", i=4, c=2, d=16)
    UP = [Dd[:, i * 2048:(i + 1) * 2048] for i in range(2)]
    WO = R[:, 0:16384].rearrange("p (j c t m) -> p j c t m", j=16, c=2, t=16)
    KD = R[:, 16384:24576].rearrange("p (i d m) -> p i d m", i=4, d=16)

    def scat_ap(t, half, base_elems, rowlen):
        return bass.AP(t[:].tensor, half * 64 * rowlen + base_elems + half * 16, [[rowlen, 64], [512, 4], [160, 4], [1, 16]])

    def ssm_prep_params():
        pk = A[:, 0:1072].rearrange("p (j f) -> p j f", f=67)
        S.dma("sp", pk, w["ssm_pack"], "ssm_ld", writes=["hT"])
        for nm, col in (("ar", 0), ("ai", 1), ("ls", 2)):
            S.op("dve", lambda e, nm=nm, col=col: e.tensor_copy(spc(nm), pk[:, :, col]), reads=["hT"], writes=["sp"])
        for dstt, c0, nm in ((BBr, 3, "BBr"), (BBi, 19, "BBi"), (CRt, 35, "CRt"), (CIt, 51, "CIt")):
            S.op("dve", lambda e, dstt=dstt, c0=c0: e.tensor_copy(dstt[:], pk[:, :, c0:c0 + 16]), reads=["hT"], writes=[nm])
        S.op("act", lambda e: e.activation(spc("dt"), spc("ls"), AF.Exp), reads=["sp"], writes=["sp"])
        t2("dve", spc("ang"), spc("ai"), spc("dt"), ALU.mult, ["sp"], ["sp"])
        t2("dve", spc("t2"), spc("ar"), spc("dt"), ALU.mult, ["sp"], ["sp"])
        S.op("act", lambda e: e.activation(spc("mag"), spc("t2"), AF.Exp), reads=["sp"], writes=["sp"])
        reduce_angle(spc("t3"), spc("ang"), 0.0)
        S.op("act", lambda e: e.activation(spc("sn"), spc("t3"), AF.Sin), reads=["sp"], writes=["sp"])
        reduce_angle(spc("t3"), spc("ang"), math.pi / 2)
        S.op("act", lambda e: e.activation(spc("cs"), spc("t3"), AF.Sin), reads=["sp"], writes=["sp"])
        t2("dve", spc("lbr"), spc("mag"), spc("cs"), ALU.mult, ["sp"], ["sp"])
        t2("dve", spc("lbi"), spc("mag"), spc("sn"), ALU.mult, ["sp"], ["sp"])
        S.op("dve", lambda e: e.tensor_scalar(spc("t0"), spc("lbr"), -1.0, None, ALU.add), reads=["sp"], writes=["sp"])
        t2("dve", spc("t1"), spc("ar"), spc("ar"), ALU.mult, ["sp"], ["sp"])
        t2("dve", spc("t2"), spc("ai"), spc("ai"), ALU.mult, ["sp"], ["sp"])
        t2("dve", spc("t1"), spc("t1"), spc("t2"), ALU.add, ["sp"], ["sp"])
        S.op("dve", lambda e: e.reciprocal(spc("t1"), spc("t1")), reads=["sp"], writes=["sp"])
        t2("dve", spc("t2"), spc("t0"), spc("ar"), ALU.mult, ["sp"], ["sp"])
        t2("dve", spc("t3"), spc("lbi"), spc("ai"), ALU.mult, ["sp"], ["sp"])
        t2("dve", spc("t2"), spc("t2"), spc("t3"), ALU.add, ["sp"], ["sp"])
        t2("dve", spc("cre"), spc("t2"), spc("t1"), ALU.mult, ["sp"], ["sp"])
        t2("dve", spc("t2"), spc("lbi"), spc("ar"), ALU.mult, ["sp"], ["sp"])
        t2("dve", spc("t3"), spc("t0"), spc("ai"), ALU.mult, ["sp"], ["sp"])
        t2("dve", spc("t2"), spc("t2"), spc("t3"), ALU.subtract, ["sp"], ["sp"])
        t2("dve", spc("cim"), spc("t2"), spc("t1"), ALU.mult, ["sp"], ["sp"])
        cb = lambda nm: spc(nm).unsqueeze(2).to_broadcast([128, 16, 16])
        T = [Ctmp[i].rearrange("p (j h) -> p j h", h=16) for i in range(4)]
        t2("dve", T[0], BBr[:], cb("cre"), ALU.mult, ["BBr", "sp"], ["ct0"])
        t2("dve", T[1], BBi[:], cb("cim"), ALU.mult, ["BBi", "sp"], ["ct1"])
        t2("dve", T[2], BBi[:], cb("cre"), ALU.mult, ["BBi", "sp"], ["ct2"])
        t2("dve", T[3], BBr[:], cb("cim"), ALU.mult, ["BBr", "sp"], ["ct3"])
        t2("dve", BBr[:], T[0], T[1], ALU.subtract, ["ct0", "ct1"], ["BBr"])
        t2("dve", BBi[:], T[2], T[3], ALU.add, ["ct2", "ct3"], ["BBi"])
        S.op("dve", lambda e: e.memset(PWr[:, :, 0:1], 1.0), writes=["PW"])
        S.op("dve", lambda e: e.memset(PWi[:, :, 0:1], 0.0), writes=["PW"])
        S.op("dve", lambda e: e.tensor_copy(PWr[:, :, 1], spc("lbr")), reads=["sp"], writes=["PW"])
        S.op("dve", lambda e: e.tensor_copy(PWi[:, :, 1], spc("lbi")), reads=["sp"], writes=["PW"])
        n = 1
        while n < 16:
            tmpv = [Ctmp[i][:, 0:16 * n].rearrange("p (j d) -> p j d", d=n) for i in range(2)]
            br = PWr[:, :, n:n + 1].to_broadcast([128, 16, n])
            bi = PWi[:, :, n:n + 1].to_broadcast([128, 16, n])
            cmul("dve", PWr[:, :, n + 1:2 * n + 1], PWi[:, :, n + 1:2 * n + 1], PWr[:, :, 1:n + 1], PWi[:, :, 1:n + 1], br, bi,
                 tmpv, ["PW"], ["PW"], None)
            n *= 2
        S.op("dve", lambda e: e.tensor_copy(AS[:, 0, 0, :], PWr[:, :, 16]), reads=["PW"], writes=["AS"])
        S.op("dve", lambda e: e.tensor_copy(AS[:, 1, 0, :], PWi[:, :, 16]), reads=["PW"], writes=["AS"])
        S.op("dve", lambda e: e.tensor_scalar(AS[:, 2, 0, :], AS[:, 1, 0, :], -1.0, None, ALU.mult), reads=["AS"], writes=["AS"])
        for k in range(8):
            tv = [Ctmp[i][:, 0:16] for i in range(2)]
            cmul("dve", AS[:, 0, k + 1, :], AS[:, 1, k + 1, :], AS[:, 0, k, :], AS[:, 1, k, :], AS[:, 0, k, :], AS[:, 1, k, :],
                 tv, ["AS"], ["AS"], None)
            S.op("dve", lambda e, k=k: e.tensor_scalar(AS[:, 2, k + 1, :], AS[:, 1, k + 1, :], -1.0, None, ALU.mult), reads=["AS"], writes=["AS"])

    def build_xpad(d):
        T = [Ctmp[i].rearrange("p (j h) -> p j h", h=16) for i in range(4)]
        pr = PWr[:, :, d:d + 1].to_broadcast([128, 16, 16])
        pi_ = PWi[:, :, d:d + 1].to_broadcast([128, 16, 16])
        t2("dve", T[0], BBr[:], pr, ALU.mult, ["BBr", "PW"], ["ct0"])
        t2("dve", T[1], BBi[:], pi_, ALU.mult, ["BBi", "PW"], ["ct1"])
        t2("dve", T[2], BBi[:], pr, ALU.mult, ["BBi", "PW"], ["ct2"])
        t2("dve", T[3], BBr[:], pi_, ALU.mult, ["BBr", "PW"], ["ct3"])
        for g2 in range(2):
            hs = slice(g2 * 64, (g2 + 1) * 64)
            v4 = lambda t: t[hs].rearrange("p (a b) h -> p a b h", b=4)
            S.op("dve", lambda e, g2=g2, hs=hs: e.tensor_tensor(scat_ap(C, g2, 0, 11264), T[0][hs].rearrange("p (a b) h -> p a b h", b=4),
                                                                T[1][hs].rearrange("p (a b) h -> p a b h", b=4), ALU.subtract),
                 reads=["ct0", "ct1"], writes=["Xre"])
            S.op("dve", lambda e, g2=g2, hs=hs: e.tensor_tensor(scat_ap(C, g2, 2048, 11264), T[2][hs].rearrange("p (a b) h -> p a b h", b=4),
                                                                T[3][hs].rearrange("p (a b) h -> p a b h", b=4), ALU.add),
                 reads=["ct2", "ct3"], writes=["Xim"])

    def zero_pads():
        S.op("pool", lambda e: e.memset(Cb[:, 0:8192], 0.0), writes=["Xre", "Xim", "Cpr", "Cpi"])

    def build_cpad():
        for g2 in range(2):
            hs = slice(g2 * 64, (g2 + 1) * 64)
            S.op("dve", lambda e, g2=g2, hs=hs: e.tensor_copy(scat_ap(C, g2, 4096, 11264), CRt[hs].rearrange("p (a b) h -> p a b h", b=4)),
                 reads=["CRt"], writes=["Cpr"])
            S.op("dve", lambda e, g2=g2, hs=hs: e.tensor_scalar(scat_ap(C, g2, 6144, 11264), CIt[hs].rearrange("p (a b) h -> p a b h", b=4),
                                                                -1.0, None, ALU.mult),
                 reads=["CIt"], writes=["Cpi"])

    def build_WE():
        zero_pads()
        build_cpad()
        for d in range(16):
            build_xpad(d)
            for comp, X, xn_ in ((0, Xre, "Xre"), (1, Xim, "Xim")):
                pp, ppn = next_ps()
                for i in range(4):
                    for jj in range(4):
                        mm(pp[:, i * 128:(i + 1) * 128], X[:, 4 * i + jj, :], ident_bf[:], jj == 0, jj == 3, [xn_, "ident_bf"], [ppn])
                S.op("act", lambda e, pp=pp, comp=comp, d=d: e.activation(WE[:, :, comp, d, :], pp[:].rearrange("p (i m) -> p i m", m=128), AF.Copy),
                     reads=[ppn], writes=["R_we"])
            pp, ppn = next_ps()
            for i in range(4):
                n_ = 0
                for jj in range(4):
                    for X, Cp, xn_, cn_ in ((Xre, Cpr, "Xre", "Cpr"), (Xim, Cpi, "Xim", "Cpi")):
                        mm(pp[:, i * 128:(i + 1) * 128], X[:, 4 * i + jj, :], Cp[:, 4 * i + jj, :], n_ == 0, n_ == 7, [xn_, cn_], [ppn])
                        n_ += 1
            if d == 0:
                for i in range(4):
                    S.op("dve", lambda e, pp=pp, i=i: e.scalar_tensor_tensor(KD[:, i, 0, :], ident[:], gvs[:, GV["ssm_d"] + i:GV["ssm_d"] + i + 1],
                                                                             pp[:, i * 128:(i + 1) * 128], ALU.mult, ALU.add),
                         reads=[ppn, "ident", "gvs"], writes=["R_kd"])
            else:
                S.op("act", lambda e, pp=pp, d=d: e.activation(KD[:, :, d, :], pp[:].rearrange("p (i m) -> p i m", m=128), AF.Copy),
                     reads=[ppn], writes=["R_kd"])

    def build_KD():
        pass

    def build_WO():
        S.op("pool", lambda e: e.memset(R[:, 0:16384], 0.0), writes=["R_wo"])
        for jg in range(4):
            js = slice(jg * 4, (jg + 1) * 4)
            T = [Cf[:, i * 1024:(i + 1) * 1024].rearrange("p (j t h) -> p j t h", j=4, t=16) for i in range(4)]
            cr = CRt[:, js, :].unsqueeze(2).to_broadcast([128, 4, 16, 16])
            ci = CIt[:, js, :].unsqueeze(2).to_broadcast([128, 4, 16, 16])
            pr = PWr[:, js, 1:17].unsqueeze(3).to_broadcast([128, 4, 16, 16])
            pi_ = PWi[:, js, 1:17].unsqueeze(3).to_broadcast([128, 4, 16, 16])
            t2("dve", T[0], cr, pr, ALU.mult, ["CRt", "PW"], ["wt0"])
            t2("pool", T[1], ci, pi_, ALU.mult, ["CIt", "PW"], ["wt1"])
            t2("dve", T[2], cr, pi_, ALU.mult, ["CRt", "PW"], ["wt2"])
            t2("pool", T[3], ci, pr, ALU.mult, ["CIt", "PW"], ["wt3"])
            S.op("dve", lambda e, T=T: e.scalar_tensor_tensor(T[2], T[2], -1.0, T[3], ALU.mult, ALU.subtract), reads=["wt2", "wt3"], writes=["wt2"])
            for g2 in range(2):
                hs = slice(g2 * 64, (g2 + 1) * 64)
                base = g2 * 64 * 24576 + jg * 4 * 1024 + g2 * 16
                o_re = bass.AP(R[:].tensor, base, [[24576, 64], [1024, 4], [32, 16], [1, 16]])
                o_im = bass.AP(R[:].tensor, base + 512, [[24576, 64], [1024, 4], [32, 16], [1, 16]])
                S.op("dve", lambda e, o_re=o_re, T=T, hs=hs: e.tensor_tensor(o_re, T[0][hs], T[1][hs], ALU.subtract), reads=["wt0", "wt1"], writes=["R_wo"])
                S.op("dve", lambda e, o_im=o_im, T=T, hs=hs: e.tensor_copy(o_im, T[2][hs]), reads=["wt2"], writes=["R_wo"])

    Zre = A[:].rearrange("p (j c) -> p j c", j=16)
    Zim = B[:].rearrange("p (j c) -> p j c", j=16)
    Zp = [Cf[:, 0:2048].rearrange("p (j c) -> p j c", j=8), Cf[:, 2048:4096].rearrange("p (j c) -> p j c", j=8)]

    def compute_E(src_scr, srcname):
        nck = NCK
        hn = nck // 2
        ht = NTOK // 2
        cnt = 0
        for i in range(4):
            for hf in range(2):
                up = UP[cnt % 2]
                upn = "xn_a" if cnt % 2 == 0 else "xn_b"
                cnt += 1
                S.dma("sp", up[:, 0:ht], src_scr[:, i, hf * ht:(hf + 1) * ht], "UP%d" % (cnt % 2), reads=[srcname], writes=[upn])
                u3 = up[:, 0:ht].rearrange("p (c t) -> p c t", t=16)
                for jj in range(4):
                    j = 4 * i + jj
                    for comp, Z in ((0, Zre), (1, Zim)):
                        pp, ppn = next_ps()
                        for tau in range(16):
                            mm(pp[:, 0:hn], WE[jj * 32:(jj + 1) * 32, i, comp, 15 - tau, :], u3[jj * 32:(jj + 1) * 32, :, tau],
                               tau == 0, tau == 15, ["R_we", upn], [ppn], tile_position=(32 * jj, 0))
                        S.op("act", lambda e, pp=pp, Z=Z, j=j, hf=hf: e.activation(Z[:, j, hf * hn:(hf + 1) * hn], pp[:, 0:hn], AF.Copy),
                             reads=[ppn], writes=[f"Z{j}" + ("r" if comp == 0 else "i")])

    def prev_state():
        compute_E(uprev_scr, "uprev_scr")
        Zq = [Cf[:, 0:2048].rearrange("p (j c) -> p j c", j=16), Cf[:, 2048:4096].rearrange("p (j c) -> p j c", j=16)]
        n = NCK
        k = 0
        cur = 0
        while n > 1:
            h = n // 2
            for j in range(16):
                if cur == 0:
                    sre, sim, sn = Zre[:, j, 0:n], Zim[:, j, 0:n], f"Z{j}"
                    dre, dim, dn = Zq[0][:, j, 0:h], Zq[1][:, j, 0:h], f"Zq{j}"
                else:
                    sre, sim, sn = Zq[0][:, j, 0:n], Zq[1][:, j, 0:n], f"Zq{j}"
                    dre, dim, dn = Zre[:, j, 0:h], Zim[:, j, 0:h], f"Z{j}"
                ev_re, od_re = sre.rearrange("p (c two) -> p c two", two=2)[:, :, 0], sre.rearrange("p (c two) -> p c two", two=2)[:, :, 1]
                ev_im, od_im = sim.rearrange("p (c two) -> p c two", two=2)[:, :, 0], sim.rearrange("p (c two) -> p c two", two=2)[:, :, 1]
                are, aim, naim = AS[:, 0, k, j:j + 1], AS[:, 1, k, j:j + 1], AS[:, 2, k, j:j + 1]
                S.op("dve", lambda e, dre=dre, ev_re=ev_re, od_re=od_re, are=are: e.scalar_tensor_tensor(dre, ev_re, are, od_re, ALU.mult, ALU.add),
                     reads=[sn + "r", "AS"], writes=[dn + "r"])
                S.op("dve", lambda e, dim=dim, ev_im=ev_im, od_im=od_im, are=are: e.scalar_tensor_tensor(dim, ev_im, are, od_im, ALU.mult, ALU.add),
                     reads=[sn + "i", "AS"], writes=[dn + "i"])
                S.op("dve", lambda e, dre=dre, ev_im=ev_im, naim=naim: e.scalar_tensor_tensor(dre, ev_im, naim, dre, ALU.mult, ALU.add),
                     reads=[sn + "i", "AS", dn + "r"], writes=[dn + "r"])
                S.op("dve", lambda e, dim=dim, ev_re=ev_re, aim=aim: e.scalar_tensor_tensor(dim, ev_re, aim, dim, ALU.mult, ALU.add),
                     reads=[sn + "r", "AS", dn + "i"], writes=[dn + "i"])
            cur = 1 - cur
            n = h
            k += 1
        src = (Zre, Zim, "Z") if cur == 0 else (Zq[0], Zq[1], "Zq")
        allr = [f"{src[2]}{j}r" for j in range(16)] + [f"{src[2]}{j}i" for j in range(16)]
        S.op("dve", lambda e: e.tensor_copy(sin_f[:, 0, :], src[0][:, :, 0]), reads=allr, writes=["sin_f"])
        S.op("dve", lambda e: e.tensor_copy(sin_f[:, 1, :], src[1][:, :, 0]), reads=allr, writes=["sin_f"])

    def phase15():
        nck = NCK
        compute_E(u_scr, "u_scr")
        for half in range(2):
            k = 0
            s_ = 1
            cur = 0
            while s_ < nck:
                for j8 in range(8):
                    j = half * 8 + j8
                    zz = (Zre[:, j, 0:nck], Zim[:, j, 0:nck], f"Z{j}")
                    zp = (Zp[0][:, j8, 0:nck], Zp[1][:, j8, 0:nck], f"Zp{j8}")
                    (ore, oim, on_), (nre, nim, nn_) = (zz, zp) if cur == 0 else (zp, zz)
                    are = AS[:, 0, k, j:j + 1]
                    aim = AS[:, 1, k, j:j + 1]
                    naim = AS[:, 2, k, j:j + 1]
                    s = s_
                    S.op("dve", lambda e, nre=nre, ore=ore, are=are, s=s: e.scalar_tensor_tensor(nre[:, s:], ore[:, 0:nck - s], are, ore[:, s:], ALU.mult, ALU.add),
                         reads=[on_ + "r", "AS"], writes=[nn_ + "r"])
                    S.op("dve", lambda e, nim=nim, oim=oim, are=are, s=s: e.scalar_tensor_tensor(nim[:, s:], oim[:, 0:nck - s], are, oim[:, s:], ALU.mult, ALU.add),
                         reads=[on_ + "i", "AS"], writes=[nn_ + "i"])
                    S.op("dve", lambda e, nre=nre, oim=oim, naim=naim, s=s: e.scalar_tensor_tensor(nre[:, s:], oim[:, 0:nck - s], naim, nre[:, s:], ALU.mult, ALU.add),
                         reads=[on_ + "i", "AS", nn_ + "r"], writes=[nn_ + "r"])
                    S.op("dve", lambda e, nim=nim, ore=ore, aim=aim, s=s: e.scalar_tensor_tensor(nim[:, s:], ore[:, 0:nck - s], aim, nim[:, s:], ALU.mult, ALU.add),
                         reads=[on_ + "r", "AS", nn_ + "i"], writes=[nn_ + "i"])
                    S.op("pool", lambda e, nre=nre, ore=ore, s=s: e.tensor_copy(nre[:, 0:s], ore[:, 0:s]), reads=[on_ + "r", nn_ + "r"], writes=[nn_ + "r"])
                    S.op("pool", lambda e, nim=nim, oim=oim, s=s: e.tensor_copy(nim[:, 0:s], oim[:, 0:s]), reads=[on_ + "i", nn_ + "i"], writes=[nn_ + "i"])
                cur = 1 - cur
                s_ *= 2
                k += 1
            if cur == 1:
                for j8 in range(8):
                    j = half * 8 + j8
                    S.op("pool", lambda e, j=j, j8=j8: e.tensor_copy(Zre[:, j, 0:nck], Zp[0][:, j8, 0:nck]), reads=[f"Zp{j8}r"], writes=[f"Z{j}r"])
                    S.op("pool", lambda e, j=j, j8=j8: e.tensor_copy(Zim[:, j, 0:nck], Zp[1][:, j8, 0:nck]), reads=[f"Zp{j8}i"], writes=[f"Z{j}i"])

    sin_dev = kb.inp("sin_dev", [128, 32])
    fin_t = kb.sb("fin_t", [128, 32], F32)

    def custom_dma_like(q, fn, key, reads, writes):
        waits = S._deps(q, reads, writes)
        n = S.dcnt.get(key, 0) + 1
        S.dcnt[key] = n
        ev = ("d:" + key, 16 * n)
        S.ops[q].append((waits, fn, ("d:" + key, 16)))
        S._commit(ev, reads, writes)

    def exchange_and_correct():
        nck = NCK
        zall = [f"Z{j}{c}" for j in range(16) for c in "ri"]
        S.op("dve", lambda e: e.tensor_copy(fin_t[:, 0:16], Zre[:, :, nck - 1]), reads=zall, writes=["fin_t"])
        S.op("dve", lambda e: e.tensor_copy(fin_t[:, 16:32], Zim[:, :, nck - 1]), reads=zall, writes=["fin_t"])
        for half in range(2):
            for j8 in range(8):
                j = half * 8 + j8
                fr, fi = Zp[0][:, j8, 0:nck], Zp[1][:, j8, 0:nck]
                fn_ = f"Zp{j8}"
                are, aim, naim = AS[:, 0, 0, j:j + 1], AS[:, 1, 0, j:j + 1], AS[:, 2, 0, j:j + 1]
                sr, si = sin_f[:, 0, j:j + 1], sin_f[:, 1, j:j + 1]
                S.op("dve", lambda e, fr=fr, sr=sr, are=are: e.tensor_scalar(fr[:, 0:1], sr, are, None, ALU.mult), reads=["sin_f", "AS"], writes=[fn_ + "r"])
                S.op("dve", lambda e, fr=fr, si=si, naim=naim: e.scalar_tensor_tensor(fr[:, 0:1], si, naim, fr[:, 0:1], ALU.mult, ALU.add),
                     reads=["sin_f", "AS", fn_ + "r"], writes=[fn_ + "r"])
                S.op("dve", lambda e, fi=fi, si=si, are=are: e.tensor_scalar(fi[:, 0:1], si, are, None, ALU.mult), reads=["sin_f", "AS"], writes=[fn_ + "i"])
                S.op("dve", lambda e, fi=fi, sr=sr, aim=aim: e.scalar_tensor_tensor(fi[:, 0:1], sr, aim, fi[:, 0:1], ALU.mult, ALU.add),
                     reads=["sin_f", "AS", fn_ + "i"], writes=[fn_ + "i"])
                n = 1
                k = 0
                while n < nck:
                    m = min(n, nck - n)
                    are, aim, naim = AS[:, 0, k, j:j + 1], AS[:, 1, k, j:j + 1], AS[:, 2, k, j:j + 1]
                    S.op("dve", lambda e, fr=fr, are=are, n=n, m=m: e.tensor_scalar(fr[:, n:n + m], fr[:, 0:m], are, None, ALU.mult),
                         reads=[fn_ + "r", "AS"], writes=[fn_ + "r"])
                    S.op("dve", lambda e, fi=fi, are=are, n=n, m=m: e.tensor_scalar(fi[:, n:n + m], fi[:, 0:m], are, None, ALU.mult),
                         reads=[fn_ + "i", "AS"], writes=[fn_ + "i"])
                    S.op("dve", lambda e, fr=fr, fi=fi, naim=naim, n=n, m=m: e.scalar_tensor_tensor(fr[:, n:n + m], fi[:, 0:m], naim, fr[:, n:n + m], ALU.mult, ALU.add),
                         reads=[fn_ + "r", fn_ + "i", "AS"], writes=[fn_ + "r"])
                    S.op("dve", lambda e, fr=fr, fi=fi, aim=aim, n=n, m=m: e.scalar_tensor_tensor(fi[:, n:n + m], fr[:, 0:m], aim, fi[:, n:n + m], ALU.mult, ALU.add),
                         reads=[fn_ + "r", fn_ + "i", "AS"], writes=[fn_ + "i"])
                    n *= 2
                    k += 1
                S.op("pool", lambda e, j=j, fr=fr: e.tensor_tensor(Zre[:, j, 0:nck], Zre[:, j, 0:nck], fr, ALU.add), reads=[fn_ + "r", f"Z{j}r"], writes=[f"Z{j}r"])
                S.op("pool", lambda e, j=j, fi=fi: e.tensor_tensor(Zim[:, j, 0:nck], Zim[:, j, 0:nck], fi, ALU.add), reads=[fn_ + "i", f"Z{j}i"], writes=[f"Z{j}i"])
        S.op("dve", lambda e: e.tensor_copy(fin_t[:, 0:16], Zre[:, :, nck - 1]), reads=zall, writes=["fin_t"])
        S.op("dve", lambda e: e.tensor_copy(fin_t[:, 16:32], Zim[:, :, nck - 1]), reads=zall, writes=["fin_t"])
        ppf, ppfn = next_ps()
        S.op("pe", lambda e: e.transpose(ppf[0:32, 0:128], fin_t[:, 0:32], ident[:]), reads=["fin_t", "ident"], writes=[ppfn])
        S.op("act", lambda e: e.activation(xtok[0][0:32, 0:128], ppf[0:32, 0:128], AF.Copy), reads=[ppfn], writes=["xtok0"])
        S.dma("sp", o_psre.rearrange("(j g) p -> j (g p)", g=2), xtok[0][0:16, 0:128], "ystore0", reads=["xtok0"], writes=["o_ps"])
        S.dma("sp", o_psim.rearrange("(j g) p -> j (g p)", g=2), xtok[0][16:32, 0:128], "ystore0", reads=["xtok0"], writes=["o_ps"])
        hst = Cb[:, 0:32 * nck].rearrange("p (j a c) -> p j a c", j=16, a=2)
        zp_all = [f"Zp{j8}{c}" for j8 in range(8) for c in "ri"]
        if nck > 1:
            S.op("dve", lambda e: e.tensor_copy(hst[:, :, 0, 1:nck], Zre[:, :, 0:nck - 1]), reads=zall, writes=["hst"] + zp_all)
            S.op("pool", lambda e: e.tensor_copy(hst[:, :, 1, 1:nck], Zim[:, :, 0:nck - 1]), reads=zall, writes=["hst"] + zp_all)
        S.op("dve", lambda e: e.tensor_copy(hst[:, :, 0, 0], sin_f[:, 0, :]), reads=["sin_f"], writes=["hst"] + zp_all)
        S.op("dve", lambda e: e.tensor_copy(hst[:, :, 1, 0], sin_f[:, 1, :]), reads=["sin_f"], writes=["hst"] + zp_all)
        S.dma("sp", hb_scr, hst, "hb_st", reads=["hst"], writes=["hb_scr"])

    def memory_kv():
        import os
        MK = int(os.environ.get("MK", "9"))
        load_block_T(w["mem_p"], 2)
        if MK < 2:
            return
        rms_stats([hT[:, kt, 0:256] for kt in range(8)], "hT", 256)
        norm_to_bf(xn[:, :, 0:256], "xn", hT[:, :, 0:256], "hT", 8, 256)
        if MK < 3:
            return
        for which, spec, outd in ((0, specs["w_mem_k"], o_pmk), (1, specs["w_mem_v"], o_pmv)):
            if which == 1 and os.environ.get("MKV", "1") == "0":
                continue
            def ev(vmt, pp, ppn, slot, sname, ml, which=which):
                if pp is not None:
                    S.op("act", lambda e: e.activation(mkT[:, vmt, :], pp[:, 0:256], AF.Copy), reads=[ppn], writes=["mkT"])
                    return False
                ppt, pptn = next_ps()
                for s in range(2):
                    for kt in range(8):
                        mm(ppt[:, s * 128:(s + 1) * 128], xn[:, kt, s * 128:(s + 1) * 128], slot[:, (ml * 8 + kt) * 128:(ml * 8 + kt + 1) * 128],
                           kt == 0, kt == 7, [sname, "xn"], [pptn])
                for s in range(2):
                    S.op("act", lambda e, s=s: e.activation(xtok[s][:, vmt * 128:(vmt + 1) * 128], ppt[:, s * 128:(s + 1) * 128], AF.Copy),
                         reads=[pptn], writes=[f"xtok{s}"])
                if which == 1:
                    for s in range(2):
                        S.op("pool", lambda e, s=s: e.tensor_copy(mv_b[:, s, vmt * 128:(vmt + 1) * 128], xtok[s][:, vmt * 128:(vmt + 1) * 128]),
                             reads=[f"xtok{s}"], writes=["mv_b"])
                return which == 1
            linear(spec, lambda kt: (xn[:, kt, 0:256], 128), 8, 256, ev)
            if MK < 4:
                return
            for s in range(2):
                S.dma("sp", outd[s * 128:(s + 1) * 128, :], xtok[s][:, :], "o_pm%d" % s, reads=[f"xtok{s}"], writes=["o_pm"])

    def p1_block(b):
        halo = b < 0
        N, nsub = (128, 1) if halo else (512, 4)
        row0 = 0 if halo else 128 + b * 512
        nxt = None
        if halo:
            nxt = (xprev[0:512, :], 4, 128, "prev0")
        elif b < NB - 1:
            nxt = (xp[128 + (b + 1) * 512:128 + (b + 2) * 512, :], 4, 128, f"own{b + 1}")
        load_block_T(xp[row0:row0 + N, :], nsub, key=None if halo else f"own{b}", nxt=nxt)
        if wstate.get("act_stores") is not None:
            wstate["act_stores"]()
            wstate["act_stores"] = None
        ffn("ffn1", GV["ffn1_post"], N)
        if not halo:
            S.dma("sp", h1_scr[b], A[:], "st_h1", reads=["hT"], writes=["h1_scr"])
        in_proj(N, nsub, 128, ropeC_d[:, row0:row0 + N], ropeS_d[:, row0:row0 + N], need_q=not halo)
        vs0 = 0 if halo else 1 + 4 * b

        def stores(row0=row0, N=N, vs0=vs0, nsub=nsub, b=b, halo=halo):
            S.dma("sp", k_scr[:, row0:row0 + N], krot_b[:, 0:N], "st_k", reads=["krot_b"], writes=["k_scr"])
            S.dma("sp", v_scr[vs0:vs0 + nsub].rearrange("s p m -> p s m"), v_bf[:, 1:1 + nsub, :], "st_v", reads=["v_bf"], writes=["v_scr"])
            if not halo:
                S.dma("sp", q_scr[b], qrot[:].rearrange("p t n -> p (t n)"), "st_q", reads=["qrot"], writes=["q_scr"])
                S.dma("sp", u_scr[:, :, b * 512:(b + 1) * 512], u_bf[:], "st_u", reads=["u_bf"], writes=["u_scr"])
        if (not halo) and b < NB - 1:
            wstate["act_stores"] = stores
        else:
            stores()
            if b == NB - 1:
                pp, ppn = next_ps()
                S.op("pe", lambda e: e.transpose(pp[:, 0:128], kq[:, 0, 384:512], ident[:]), reads=["kq", "ident"], writes=[ppn])
                S.op("act", lambda e: e.activation(xtok[0][:, 0:128], pp[:, 0:128], AF.Copy), reads=[ppn], writes=["xtok0"])
                S.dma("sp", o_pwk, xtok[0][:, 0:128], "ystore0", reads=["xtok0"], writes=["o_pwk"])
                S.dma("sp", o_pwv, v_f[:], "o_vf", reads=["v_f"], writes=["o_pwv"])

    def p1_prev_block(pb):
        N = 512
        nxt = (xprev[(pb + 1) * 512:(pb + 2) * 512, :], 4, 128, f"prev{pb + 1}") if pb < NB - 1 else (xp[128:128 + 512, :], 4, 128, "own0")
        load_block_T(xprev[pb * 512:(pb + 1) * 512, :], 4, key=f"prev{pb}", nxt=nxt)
        ffn("ffn1", GV["ffn1_post"], N)
        rms_stats([hT[:, kt, 0:N] for kt in range(8)], [f"hT{kt}" for kt in range(8)], N)
        norm_to_bf(xn[:, :, 0:N], "xn", hT[:, :, 0:N], "hT", 8, N)

        def ev(vmt, pp, ppn, slot, sname, ml):
            if pp is None:
                return False
            S.op("act", lambda e: e.activation(u_bf[:, vmt, 0:N], pp[:, 0:N], AF.Copy), reads=[ppn], writes=["u_bf"])
            return False

        linear(specs["w_in"], lambda kt: (xn[:, kt, 0:N], 128), 8, N, ev, vm_sel=lambda v: v < 4)
        S.dma("sp", uprev_scr[:, :, pb * 512:(pb + 1) * 512], u_bf[:], "st_u", reads=["u_bf"], writes=["uprev_scr"])

    gf = Cf[:, 0:2048].rearrange("p (k n) -> p k n", n=512)
    gb = Cb[:, 4096:6144].rearrange("p (k n) -> p k n", n=512)
    ysn = Cb[:, 6144:8192].rearrange("p (k n) -> p k n", n=512)
    qmT = Cb[:, 0:4096].rearrange("p (k n) -> p k n", n=512)
    Pm = Cb[:, 4096:8192].rearrange("p (k n) -> p k n", n=512)
    o_all = B[0:64, :].rearrange("p (k n) -> p k n", n=512)

    def ssm_out(N, u3_fn, y3_fn, h_fn, ntau, nlag):
        for i in range(4):
            pp, ppn = next_ps()
            y3 = y3_fn(pp)
            for d in range(nlag):
                mm(y3[:, :, d:ntau], KD[:, i, d, :], u3_fn(i)[:, :, 0:ntau - d], d == 0, False, ["R_kd", "u_bf"], [ppn])
            cnt = 0
            for jj in range(4):
                j = 4 * i + jj
                for comp in range(2):
                    for tau in range(ntau):
                        cnt += 1
                        mm(y3[jj * 32:(jj + 1) * 32, :, tau], WO[:, j, comp, tau, :], h_fn(j, comp), False, cnt == 8 * ntau,
                           ["R_wo", "hbk"], [ppn], tile_position=(0, 32 * jj))
            S.op("act", lambda e, pp=pp, i=i: e.activation(gf[:, i, 0:N], pp[:, 0:N], AF.Gelu_apprx_tanh), reads=[ppn], writes=["gf"])
        S.op("pool", lambda e: e.tensor_copy(gb[:, 0:1, 0:N], gf[:, 0:1, 0:N]), reads=["gf"], writes=["gb"])
        S.op("dve", lambda e: e.tensor_copy(gb[:, 1:4, 0:N], gf[:, 1:4, 0:N]), reads=["gf"], writes=["gb"])

        def ev(vmt, pp, ppn, slot, sname, ml):
            if pp is None:
                return False
            S.op("act", lambda e: e.activation(rc[:, 0:N], pp[:, 0:N], AF.Sigmoid, bias=gvs[:, GV["b_glu"] + vmt:GV["b_glu"] + vmt + 1]),
                 reads=[ppn, "gvs"], writes=["rc"])
            S.op("dve", lambda e: e.tensor_tensor(gf[:, vmt, 0:N], gf[:, vmt, 0:N], rc[:, 0:N], ALU.mult), reads=["gf", "rc", "gb"], writes=["gf"])
            return False

        linear(specs["w_glu"], lambda kt: (gb[:, kt, 0:N], 128), 4, N, ev, rnames=("gb",))
        rms_stats([gf[:, i, 0:N] for i in range(4)], "gf", N)
        norm_to_bf(ysn[:, :, 0:N], "ysn", gf[:, :, 0:N], "gf", 4, N)

    def attn_norm_outproj(N):
        rms_stats([o_all[0:64, h, 0:N] for h in range(8)], "cT", N, P=64, nfeat=512)
        norm_to_bf(xn[0:64, :, 0:N], "xn", o_all[0:64, :, 0:N], "cT", 8, N, P=64)

        def ev(vmt, pp, ppn, slot, sname, ml):
            if pp is None:
                return False
            S.op("act", lambda e: e.activation(cT[:, vmt, 0:N], pp[:, 0:N], AF.Copy), reads=[ppn], writes=[f"cT{vmt}"])
            return False

        linear(specs["w_out"], lambda kt: (ysn[:, kt, 0:N], 128) if kt < 4 else (xn[0:64, kt - 4, 0:N], 64), 12, N, ev, rnames=("xn", "ysn"))
        post_residual("hT", GV["mix_post"], N, False)

    def cross_attn(N, score_fn):
        rms_stats([hT[:, kt, 0:N] for kt in range(8)], [f"hT{kt}" for kt in range(8)], N)
        norm_to_bf(xn[:, :, 0:N], "xn", hT[:, :, 0:N], "hT", 8, N)

        def evq(vmt, pp, ppn, slot, sname, ml):
            if pp is None:
                return False
            S.op("act", lambda e: e.activation(qmT[:, vmt, 0:N], pp[:, 0:N], AF.Copy), reads=[ppn], writes=["qmT"])
            return False

        linear(specs["w_mem_q"], lambda kt: (xn[:, kt, 0:N], 128), 8, N, evq)
        score_fn()

        def evo(vmt, pp, ppn, slot, sname, ml):
            if pp is None:
                return False
            S.op("act", lambda e: e.activation(cT[:, vmt, 0:N], pp[:, 0:N], AF.Copy), reads=[ppn], writes=[f"cT{vmt}"])
            return False

        linear(specs["w_mem_o"], lambda kt: (xn[:, kt, 0:N], 128), 8, N, evo)
        post_residual("hT", GV["xa_post"], N, False)

    def prompt_mem_scores():
        N = 512
        for hm in range(4):
            for mt in range(2):
                pp, ppn = next_ps()
                for half in range(2):
                    mm(pp[:, 0:N], mkT[:, hm * 2 + half, mt * 128:(mt + 1) * 128], qmT[:, hm * 2 + half, 0:N], half == 0, half == 1,
                       ["mkT", "qmT"], [ppn])
                S.op("act", lambda e, pp=pp, hm=hm, mt=mt: e.activation(Pm[:, hm * 2 + mt, 0:N], pp[:, 0:N], AF.Exp, scale=1.0 / 16.0),
                     reads=[ppn], writes=["Pm"])
            pd, pdn = next_ps()
            for mt in range(2):
                mm(pd[:, 0:N], ones_bf[:, :], Pm[:, hm * 2 + mt, 0:N], mt == 0, mt == 1, ["ones", "Pm"], [pdn])
            S.op("dve", lambda e, pd=pd: e.reciprocal(rc[:, 0:N], pd[:, 0:N]), reads=[pdn], writes=["rc"])
            for half in range(2):
                po, pon = next_ps()
                for mt in range(2):
                    mm(po[:, 0:N], mv_b[:, mt, (hm * 2 + half) * 128:(hm * 2 + half + 1) * 128], Pm[:, hm * 2 + mt, 0:N], mt == 0, mt == 1,
                       ["mv_b", "Pm"], [pon])
                S.op("dve", lambda e, po=po, hm=hm, half=half: e.tensor_tensor(xn[:, hm * 2 + half, 0:N], po[:, 0:N], rc[:, 0:N], ALU.mult),
                     reads=[pon, "rc"], writes=["xn"])

    def prompt_attention(b):
        for s in range(4):
            for hh in range(2):
                hs = slice(hh * 64, (hh + 1) * 64)
                pa, pan = next_ps()
                pb, pbn = next_ps()
                for t in range(4):
                    mm(pa[:, t * 128:(t + 1) * 128], kblk[hs, 128 + s * 128:128 + (s + 1) * 128], qrot[hs, t, s * 128:(s + 1) * 128], True, True,
                       ["kblk", "qrot"], [pan])
                    mm(pb[:, t * 128:(t + 1) * 128], kblk[hs, s * 128:(s + 1) * 128], qrot[hs, t, s * 128:(s + 1) * 128], True, True,
                       ["kblk", "qrot"], [pbn])
                S.op("act", lambda e, pa=pa, hh=hh: e.activation(pt[:, hh * 2, :], pa[:, :], AF.Exp, scale=0.125), reads=[pan], writes=[f"pt{hh}c"])
                S.op("act", lambda e, pb=pb, hh=hh: e.activation(pt[:, hh * 2 + 1, :], pb[:, :], AF.Exp, scale=0.125), reads=[pbn], writes=[f"pt{hh}p"])
                mprev = 2 if (b == 0 and s == 0) else 1
                S.op("pool", lambda e, hh=hh: e.tensor_tensor(pt[:, hh * 2, :].rearrange("p (t q) -> p t q", q=128), pt[:, hh * 2, :].rearrange("p (t q) -> p t q", q=128),
                                                              masks[:, 0, :].unsqueeze(1).to_broadcast([128, 4, 128]), ALU.mult),
                     reads=[f"pt{hh}c", "masks"], writes=[f"pt{hh}c"])
                S.op("pool", lambda e, hh=hh, mprev=mprev: e.tensor_tensor(pt[:, hh * 2 + 1, :].rearrange("p (t q) -> p t q", q=128), pt[:, hh * 2 + 1, :].rearrange("p (t q) -> p t q", q=128),
                                                                           masks[:, mprev, :].unsqueeze(1).to_broadcast([128, 4, 128]), ALU.mult),
                     reads=[f"pt{hh}p", "masks"], writes=[f"pt{hh}p"])
                po, pon = next_ps()
                pd, pdn = next_ps()
                mm(po[0:64, :], v_bf[:, 1 + s, hs], pt[:, hh * 2, :], True, False, ["v_bf", f"pt{hh}c"], [pon])
                mm(po[0:64, :], v_bf[:, s, hs], pt[:, hh * 2 + 1, :], False, True, ["v_bf", f"pt{hh}p"], [pon])
                mm(pd[0:64, :], ones_bf[:, 0:64], pt[:, hh * 2, :], True, False, ["ones", f"pt{hh}c"], [pdn])
                mm(pd[0:64, :], ones_bf[:, 0:64], pt[:, hh * 2 + 1, :], False, True, ["ones", f"pt{hh}p"], [pdn])
                S.op("dve", lambda e, pd=pd, hh=hh: e.tensor_tensor(att_t[0:64, :].rearrange("p (t q) -> p t q", q=128), pd[0:64, :].rearrange("p (t q) -> p t q", q=128),
                                                                    sinkexp[0:64, hh * 4:(hh + 1) * 4].unsqueeze(2).to_broadcast([64, 4, 128]), ALU.add),
                     reads=[pdn, "sinkexp"], writes=["att_t"])
                S.op("dve", lambda e: e.reciprocal(att_t[0:64, :], att_t[0:64, :]), reads=["att_t"], writes=["att_t"])
                S.op("dve", lambda e, po=po, hh=hh, s=s: e.tensor_tensor(o_all[0:64, hh * 4:(hh + 1) * 4, s * 128:(s + 1) * 128], po[0:64, :].rearrange("p (t q) -> p t q", q=128),
                                                                         att_t[0:64, :].rearrange("p (t q) -> p t q", q=128), ALU.mult),
                     reads=[pon, "att_t"], writes=["cT"])

    def p2_loads(b):
        S.dma("sp", qrot[:].rearrange("p t n -> p (t n)"), q_scr[b], "ld_q", reads=["q_scr"], writes=["qrot"])
        S.dma("sp", kblk[:, :], k_scr[:, b * 512:b * 512 + 640], "ld_k", reads=["k_scr"], writes=["kblk"])
        S.dma("sp", v_bf[:, :, :], v_scr[4 * b:4 * b + 5].rearrange("s p m -> p s m"), "ld_v", reads=["v_scr"], writes=["v_bf"])
        S.dma("sp", u_bf[:], u_scr[:, :, b * 512:(b + 1) * 512], "ld_u", reads=["u_scr"], writes=["u_bf"])
        S.dma("sp", hbk[:], hb_scr[:, :, :, b * 32:(b + 1) * 32], "ld_hb", reads=["hb_scr"], writes=["hbk"])

    def p2_block(b):
        N = 512
        if b == 0:
            p2_loads(0)
        S.dma("sp", A[:], h1_scr[b], "ld_h1", reads=["h1_scr"], writes=["hT"])
        ssm_out(N, lambda i: u_bf[:, i, :].rearrange("p (c t) -> p c t", t=16), lambda pp: pp[:].rearrange("p (c t) -> p c t", t=16),
                lambda j, comp: hbk[:, j, comp, :], 16, 16)
        if stage == "dbg_ssm":
            S.dma("sp", dbg_o[:, 0:4, :], gf[:, :, :], "dbg", reads=["gf"], writes=["dbg"])
            return
        prompt_attention(b)
        if stage == "dbg_attn":
            S.dma("sp", dbg_o[0:64, :, :], o_all[:, :, :], "dbg", reads=["cT"], writes=["dbg"])
            return
        attn_norm_outproj(N)
        if b + 1 < NB and not stage.startswith("dbg"):
            p2_loads(b + 1)
        if stage == "dbg_mix":
            S.dma("sp", dbg_o[:, :, :], hT[:, :, :], "dbg", reads=["hT"], writes=["dbg"])
            return
        cross_attn(N, prompt_mem_scores)
        if stage == "dbg_xa":
            S.dma("sp", dbg_o[:, :, :], hT[:, :, :], "dbg", reads=["hT"], writes=["dbg"])
            return
        ffn("ffn2", GV["ffn2_post"], N)
        if stage == "dbg_ffn2":
            S.dma("sp", dbg_o[:, :, :], hT[:, :, :], "dbg", reads=["hT"], writes=["dbg"])
        store_block_T(y_out[b * 512:(b + 1) * 512, :], 4)


    def sample_part_a():
        import os
        SA = int(os.environ.get("SA", "9"))
        N = 64
        load_block_T(xs, 1, W=64)
        if SA < 2:
            return
        ffn("ffn1", GV["ffn1_post"], N)
        if SA < 3:
            return
        S.op("pool", lambda e: e.tensor_copy(h1s[:], hT[:, :, 0:64]), reads=["hT"], writes=["h1s"])
        in_proj(N, 1, 64, ropeCs_d, ropeSs_d, need_q=True)
        if SA < 4:
            return
        S.op("pool", lambda e: e.tensor_copy(us_b[:], u_bf[:, :, 0:64]), reads=["u_bf"], writes=["us_b"])
        S.op("pool", lambda e: e.tensor_copy(qs_b[:], qrot[:, :, 0:64]), reads=["qrot"], writes=["qs_b"])
        S.op("pool", lambda e: e.tensor_copy(ks_b[:], krot_b[:, 0:64]), reads=["krot_b"], writes=["ks_b"])
        S.op("pool", lambda e: e.tensor_copy(vs_b[:], v_bf[0:64, 1, :]), reads=["v_bf"], writes=["vs_b"])
        if SA < 5:
            return
        S.dma("sp", o_swk[:, 0:124, :], w["swa_k"][:, 4:128, :], "o_sw", writes=["o_swk"])
        S.dma("sp", o_swv[:, 0:124, :], w["swa_v"][:, 4:128, :], "o_sw", writes=["o_swv"])
        if SA < 6:
            return
        pp, ppn = next_ps()
        S.op("pe", lambda e: e.transpose(pp[0:64, 0:128], kq[:, 0, 0:64], ident[:]), reads=["kq", "ident"], writes=[ppn])
        S.op("act", lambda e: e.activation(xtok[0][0:64, 0:128], pp[0:64, 0:128], AF.Copy), reads=[ppn], writes=["xtok0"])
        for b_ in range(16):
            S.dma("sp", o_swk[b_, 124:128, :], xtok[0][4 * b_:4 * b_ + 4, 0:128], "ystore0", reads=["xtok0"], writes=["o_swk"])
            S.dma("sp", o_swv[b_, 124:128, :], v_f[4 * b_:4 * b_ + 4, :], "o_vf", reads=["v_f"], writes=["o_swv"])

    def sample_state():
        hs_f = rc[:, :].rearrange("p (c j b) -> p c j b", c=2, j=16)
        fin = rs[:, :].rearrange("p (c j b) -> p c j b", c=2, j=16)
        for comp, src_ in ((0, w["st_re"]), (1, w["st_im"])):
            xt, xtn = xtok[comp], f"xtok{comp}"
            S.dma("sp", xt[0:16, 0:1024], src_.rearrange("b g p -> b (g p)")[:, 0:1024], xtn, writes=[xtn])
            pp, ppn = next_ps()
            for j in range(8):
                S.op("pe", lambda e, pp=pp, j=j, xt=xt: e.transpose(pp[:, j * 16:(j + 1) * 16], xt[0:16, j * 128:(j + 1) * 128], ident[0:16, 0:16]),
                     reads=[xtn, "ident"], writes=[ppn])
            S.op("act", lambda e, pp=pp, comp=comp: e.activation(hs_f[:, comp, 0:8, :], pp[:, 0:128].rearrange("p (j b) -> p j b", b=16), AF.Copy),
                 reads=[ppn], writes=["rc"])
            S.dma("sp", xt[0:16, 0:1024], src_.rearrange("b g p -> b (g p)")[:, 1024:2048], xtn, reads=[ppn], writes=[xtn])
            pp, ppn = next_ps()
            for j in range(8):
                S.op("pe", lambda e, pp=pp, j=j, xt=xt: e.transpose(pp[:, j * 16:(j + 1) * 16], xt[0:16, j * 128:(j + 1) * 128], ident[0:16, 0:16]),
                     reads=[xtn, "ident"], writes=[ppn])
            S.op("act", lambda e, pp=pp, comp=comp: e.activation(hs_f[:, comp, 8:16, :], pp[:, 0:128].rearrange("p (j b) -> p j b", b=16), AF.Copy),
                 reads=[ppn], writes=["rc"])
        import os
        SS = int(os.environ.get("SS", "9"))
        if SS < 2:
            return
        S.op("pool", lambda e: e.tensor_copy(hs_b[:].rearrange("p j c b -> p c j b"), hs_f), reads=["rc"], writes=["hs_b"])
        if SS < 3:
            return
        pes = [next_ps() for _ in range(4)]
        u4 = us_b[:].rearrange("p i (b t) -> p i b t", t=4)
        for i in range(4):
            for jj in range(4):
                pe_, pen = pes[jj]
                for comp in range(2):
                    col = (comp * 4 + i) * 16
                    for tau in range(4):
                        mm(pe_[:, col:col + 16], WE[jj * 32:(jj + 1) * 32, i, comp, 3 - tau, :], u4[jj * 32:(jj + 1) * 32, i, :, tau],
                           tau == 0, tau == 3, ["R_we", "us_b"], [pen], tile_position=(32 * jj, 0))
        if SS < 4:
            return
        p4r = PWr[:, :, 4:5].to_broadcast([128, 16, 16])
        p4i = PWi[:, :, 4:5].to_broadcast([128, 16, 16])
        T = [Ctmp[i].rearrange("p (j b) -> p j b", b=16) for i in range(4)]
        t2("dve", T[0], hs_f[:, 0], p4r, ALU.mult, ["rc", "PW"], ["ct0"])
        t2("dve", T[1], hs_f[:, 1], p4i, ALU.mult, ["rc", "PW"], ["ct1"])
        t2("dve", T[2], hs_f[:, 0], p4i, ALU.mult, ["rc", "PW"], ["ct2"])
        t2("dve", T[3], hs_f[:, 1], p4r, ALU.mult, ["rc", "PW"], ["ct3"])
        t2("dve", T[0], T[0], T[1], ALU.subtract, ["ct0", "ct1"], ["ct0"])
        t2("dve", T[2], T[2], T[3], ALU.add, ["ct2", "ct3"], ["ct2"])
        for jj in range(4):
            pe_, pen = pes[jj]
            pe3 = pe_[:, 0:128].rearrange("p (c i b) -> p c i b", c=2, i=4)
            for comp, Tc, tn in ((0, T[0], "ct0"), (1, T[2], "ct2")):
                S.op("dve", lambda e, jj=jj, comp=comp, Tc=Tc, pe3=pe3: e.tensor_tensor(
                    fin[:, comp].rearrange("p (i q) b -> p i q b", q=4)[:, :, jj, :], Tc.rearrange("p (i q) b -> p i q b", q=4)[:, :, jj, :],
                    pe3[:, comp], ALU.add), reads=[tn, pen], writes=["rs"])
        if SS < 5:
            return
        for comp, dst_ in ((0, o_ssre), (1, o_ssim)):
            xt, xtn = xtok[comp], f"xtok{comp}"
            for jq in range(4):
                pp, ppn = next_ps()
                for j4 in range(4):
                    j = jq * 4 + j4
                    S.op("pe", lambda e, pp=pp, j4=j4, j=j, comp=comp: e.transpose(pp[0:16, j4 * 128:(j4 + 1) * 128], fin[:, comp, j, :], ident[:]),
                         reads=["rs", "ident"], writes=[ppn])
                S.op("act", lambda e, pp=pp, jq=jq, xt=xt: e.activation(xt[0:16, (jq % 2) * 512:(jq % 2) * 512 + 512], pp[0:16, :], AF.Copy),
                     reads=[ppn], writes=[xtn])
                if jq % 2 == 1:
                    S.dma("sp", dst_.rearrange("b g p -> b (g p)")[:, (jq // 2) * 1024:(jq // 2 + 1) * 1024], xt[0:16, 0:1024], "ystore%d" % comp,
                          reads=[xtn], writes=["o_ss"])

    def sample_attention():
        kcT = pt[:].rearrange("p a n -> p (a n)").rearrange("p (b k) -> p b k", k=128)
        vc = kq[:].rearrange("p a n -> p (a n)").bitcast(BF16).rearrange("p (b c) -> p b c", c=128)
        vnew = Cb[0:4, 8192:10240].rearrange("p (b c) -> p b c", c=128)
        for hf in range(2):
            xt, xtn = xtok[hf], f"xtok{hf}"
            S.dma("sp", xt[:, :].rearrange("p (b c) -> p b c", c=128), w["swa_k"][8 * hf:8 * hf + 8].rearrange("b p c -> p b c"), xtn, writes=[xtn])
            for bq in range(2):
                pp, ppn = next_ps()
                for b4 in range(4):
                    b8 = bq * 4 + b4
                    S.op("pe", lambda e, pp=pp, b4=b4, b8=b8, xt=xt: e.transpose(pp[:, b4 * 128:(b4 + 1) * 128], xt[:, b8 * 128:(b8 + 1) * 128], ident[:]),
                         reads=[xtn, "ident"], writes=[ppn])
                S.op("act", lambda e, pp=pp, hf=hf, bq=bq: e.activation(kcT[:, hf * 8 + bq * 4:hf * 8 + bq * 4 + 4, :], pp[:].rearrange("p (b k) -> p b k", k=128), AF.Copy),
                     reads=[ppn], writes=["pt0c"])
        for hf in range(2):
            xt, xtn = xtok[hf], f"xtok{hf}"
            S.dma("sp", xt[:, :].rearrange("p (b c) -> p b c", c=128), w["swa_v"][8 * hf:8 * hf + 8].rearrange("b p c -> p b c"), xtn, writes=[xtn])
            S.op("pool", lambda e, hf=hf, xt=xt: e.tensor_copy(vc[:, hf * 8:(hf + 1) * 8, :], xt[:, :].rearrange("p (b c) -> p b c", c=128)),
                 reads=[xtn], writes=["kq"])
        for b_ in range(16):
            S.dma("sp", vnew[0:4, b_, :], vs_b[4 * b_:4 * b_ + 4, :], "vnew", reads=["vs_b"], writes=["C2"])
        psc, pscn = next_ps()
        psn, psnn = next_ps()
        for b_ in range(16):
            for hh in range(2):
                hs = slice(hh * 64, (hh + 1) * 64)
                for t in range(4):
                    col = ((b_ * 2 + hh) * 4 + t) * 4
                    mm(psc[:, col:col + 4], kcT[hs, b_, :], qs_b[hs, t, 4 * b_:4 * b_ + 4], True, True, ["pt0c", "qs_b"], [pscn])
                    mm(psn[0:4, col:col + 4], ks_b[hs, 4 * b_:4 * b_ + 4], qs_b[hs, t, 4 * b_:4 * b_ + 4], True, True, ["ks_b", "qs_b"], [psnn])
        Pc, Pn = sg[0], sg[1]
        S.op("act", lambda e: e.activation(Pc[:, :], psc[:, :], AF.Exp, scale=0.125), reads=[pscn], writes=["sg0"])
        S.op("act", lambda e: e.activation(Pn[0:4, :], psn[0:4, :], AF.Exp, scale=0.125), reads=[psnn], writes=["sg1"])
        S.op("pool", lambda e: e.tensor_tensor(Pc[:, :].rearrange("p (a q) -> p a q", q=4), Pc[:, :].rearrange("p (a q) -> p a q", q=4),
                                               masks[:, 3, 0:4].unsqueeze(1).to_broadcast([128, 128, 4]), ALU.mult), reads=["sg0", "masks"], writes=["sg0"])
        S.op("pool", lambda e: e.tensor_tensor(Pn[0:4, :].rearrange("p (a q) -> p a q", q=4), Pn[0:4, :].rearrange("p (a q) -> p a q", q=4),
                                               masks[0:4, 4, 0:4].unsqueeze(1).to_broadcast([4, 128, 4]), ALU.mult), reads=["sg1", "masks"], writes=["sg1"])
        po, pon = next_ps()
        pd, pdn = next_ps()
        for b_ in range(16):
            for hh in range(2):
                hs = slice(hh * 64, (hh + 1) * 64)
                col = (b_ * 2 + hh) * 16
                mm(po[0:64, col:col + 16], vc[:, b_, hs], Pc[:, col:col + 16], True, False, ["kq", "sg0"], [pon])
                mm(po[0:64, col:col + 16], vnew[0:4, b_, hs], Pn[0:4, col:col + 16], False, True, ["C2", "sg1"], [pon])
                mm(pd[0:64, col:col + 16], ones_bf[:, 0:64], Pc[:, col:col + 16], True, False, ["ones", "sg0"], [pdn])
                mm(pd[0:64, col:col + 16], ones_bf[0:4, 0:64], Pn[0:4, col:col + 16], False, True, ["ones", "sg1"], [pdn])
        S.op("dve", lambda e: e.tensor_tensor(att_t[0:64, :].rearrange("p (b h q) -> p b h q", h=8, q=4), pd[0:64, :].rearrange("p (b h q) -> p b h q", h=8, q=4),
                                              sinkexp[0:64, :].unsqueeze(1).unsqueeze(3).to_broadcast([64, 16, 8, 4]), ALU.add),
             reads=[pdn, "sinkexp"], writes=["att_t"])
        S.op("dve", lambda e: e.reciprocal(att_t[0:64, :], att_t[0:64, :]), reads=["att_t"], writes=["att_t"])
        S.op("dve", lambda e: e.tensor_tensor(o_all[0:64, :, 0:64].rearrange("p h (b q) -> p b h q", q=4), po[0:64, :].rearrange("p (b h q) -> p b h q", h=8, q=4),
                                              att_t[0:64, :].rearrange("p (b h q) -> p b h q", h=8, q=4), ALU.mult),
             reads=[pon, "att_t"], writes=["cT"])

    def sample_mem_scores():
        KTb = mkT
        Vb = mv_b
        vst = Cf[:, 4096:5120]
        Ps0, Ps1 = sg[0], sg[1]
        pC, pCn = PS[6], "ps6"
        pD, pDn = PS[7], "ps7"
        for b_ in range(16):
            for mt in range(2):
                xt, xtn = xtok[mt], f"xtok{mt}"
                S.dma("sp", xt[:, :], w["memk"][b_, mt * 128:(mt + 1) * 128, :], xtn, writes=[xtn])
                for kq_ in range(2):
                    pp, ppn = next_ps()
                    for k4 in range(4):
                        kt = kq_ * 4 + k4
                        S.op("pe", lambda e, pp=pp, k4=k4, kt=kt, xt=xt: e.transpose(pp[:, k4 * 128:(k4 + 1) * 128], xt[:, kt * 128:(kt + 1) * 128], ident[:]),
                             reads=[xtn, "ident"], writes=[ppn])
                    S.op("act", lambda e, pp=pp, kq_=kq_, mt=mt: e.activation(KTb[:, kq_ * 4:(kq_ + 1) * 4, mt * 128:(mt + 1) * 128],
                                                                             pp[:].rearrange("p (k n) -> p k n", n=128), AF.Copy),
                         reads=[ppn], writes=["mkT"])
                S.dma("sp", vst, w["memv"][b_, mt * 128:(mt + 1) * 128, :], "vst", writes=["C2"])
                S.op("pool", lambda e, mt=mt: e.tensor_copy(Vb[:, mt, :], vst), reads=["C2"], writes=["mv_b"])
            psS = []
            for mt in range(2):
                pp, ppn = next_ps()
                psS.append((pp, ppn))
                for hm in range(4):
                    for half in range(2):
                        mm(pp[:, hm * 4:(hm + 1) * 4], KTb[:, hm * 2 + half, mt * 128:(mt + 1) * 128], qmT[:, hm * 2 + half, 4 * b_:4 * b_ + 4],
                           half == 0, half == 1, ["mkT", "qmT"], [ppn])
            for mt, Pt, pn_ in ((0, Ps0, "sg0"), (1, Ps1, "sg1")):
                pp, ppn = psS[mt]
                S.op("act", lambda e, pp=pp, Pt=Pt: e.activation(Pt[:, 0:16], pp[:, 0:16], AF.Exp, scale=1.0 / 16.0), reads=[ppn], writes=[pn_])
            for hm in range(4):
                for half in range(2):
                    col = ((b_ * 4 + hm) * 2 + half) * 4
                    for mt, Pt, pn_ in ((0, Ps0, "sg0"), (1, Ps1, "sg1")):
                        mm(pC[:, col:col + 4], Vb[:, mt, (hm * 2 + half) * 128:(hm * 2 + half + 1) * 128], Pt[:, hm * 4:(hm + 1) * 4],
                           mt == 0, mt == 1, ["mv_b", pn_], [pCn])
                cold = (b_ * 4 + hm) * 4
                for mt, Pt, pn_ in ((0, Ps0, "sg0"), (1, Ps1, "sg1")):
                    mm(pD[:, cold:cold + 4], ones_bf[:, :], Pt[:, hm * 4:(hm + 1) * 4], mt == 0, mt == 1, ["ones", pn_], [pDn])
        S.op("dve", lambda e: e.reciprocal(rc[:, 0:256], pD[:, 0:256]), reads=[pDn], writes=["rc"])
        for half in range(2):
            S.op("dve", lambda e, half=half: e.tensor_tensor(
                xn[:, :, 0:64].rearrange("p (hm hf) (b q) -> p hf b hm q", hf=2, q=4)[:, half],
                pC[:, :].rearrange("p (b hm hf q) -> p hf b hm q", hm=4, hf=2, q=4)[:, half],
                rc[:, 0:256].rearrange("p (b hm q) -> p b hm q", hm=4, q=4), ALU.mult),
                reads=[pCn, "rc"], writes=["xn"])

    def sample_part_b():
        N = 64
        S.barrier()
        S.op("pool", lambda e: e.tensor_copy(hT[:, :, 0:64], h1s[:]), reads=["h1s"], writes=["hT"])
        S.op("pool", lambda e: e.tensor_copy(u_bf[:, :, 0:64], us_b[:]), reads=["us_b"], writes=["u_bf"])
        S.op("pool", lambda e: e.tensor_copy(hbk[:, :, :, 0:16], hs_b[:]), reads=["hs_b"], writes=["hbk"])
        ssm_out(N, lambda i: u_bf[:, i, 0:64].rearrange("p (b t) -> p b t", t=4), lambda pp: pp[:, 0:64].rearrange("p (b t) -> p b t", t=4),
                lambda j, comp: hbk[:, j, comp, 0:16], 4, 4)
        sample_attention()
        attn_norm_outproj(N)
        cross_attn(N, sample_mem_scores)
        ffn("ffn2", GV["ffn2_post"], N)
        store_block_T(ys_out, 1, W=64)

    kb._ns = dict(locals())
    return kb


def assemble(cfg):
    kb = build(cfg)
    ns = kb._ns
    S = kb.S
    NB = cfg["NB"]
    S.alias.update({"hmid": ["C0", "C1", "C2"], "gf": ["C0"], "gb": ["C1"], "ysn": ["C1"], "qmT": ["C0"], "Pm": ["C1"],
                    "Xre": ["C0"], "Xim": ["C0"], "Cpr": ["C1"], "Cpi": ["C1"], "cm_ta": ["C2"], "cm_tb": ["C2"],
                    "ct0": ["C2"], "ct1": ["C2"], "ct2": ["C2"], "ct3": ["C2"],
                    "wt0": ["C0"], "wt1": ["C0"], "wt2": ["C1"], "wt3": ["C1"], "hst": ["C0", "C1"]})
    S.alias.update({"xn": ["xn_a", "xn_b"], "ysn_a": ["C1"], "ysn_b": ["C1"],
                    "cT": [f"cT{k}" for k in range(8)], "cT_a": [f"cT{k}" for k in range(6)], "cT_b": [f"cT{k}" for k in range(6, 8)],
                    "hT": [f"hT{k}" for k in range(8)]})
    for j in range(16):
        S.alias[f"Zq{j}r"] = ["C0"]
        S.alias[f"Zq{j}i"] = ["C1"]
    for j8 in range(8):
        S.alias[f"Zp{j8}r"] = ["C0"]
        S.alias[f"Zp{j8}i"] = ["C1"]
    stage = cfg.get("stage", "full")
    stop = cfg.get("stop", 99)
    steps = [lambda: ns["ssm_prep_params"](), lambda: ns["build_WE"](), lambda: ns["memory_kv"](), lambda: ns["p1_block"](-1),
             lambda: [ns["p1_prev_block"](b) for b in range(NB)] + [ns["p1_block"](b) for b in range(NB)],
             lambda: (ns["sample_part_a"]() if cfg.get("sample", True) else None),
             lambda: (S.barrier(), ns["prev_state"](), S.barrier(), ns["phase15"]()),
             lambda: (ns["sample_state"]() if cfg.get("sample", True) else None),
             lambda: (ns["precast_phase2_weights"]() if cfg.get("precast", False) else None, ns["exchange_and_correct"](), S.barrier()),
             lambda: ns["build_KD"](), lambda: (ns["build_WO"](), S.barrier()),
             lambda: [ns["p2_block"](b) for b in range(NB)],
             lambda: (ns["sample_part_b"]() if cfg.get("sample", True) else None)]
    for i, st_ in enumerate(steps):
        if i >= stop:
            break
        st_()
        ns["flush_pending"]()
    S.barrier()
    kb.nsem = S.emit(kb.st)
    kb.st.close()
    return kb


def _rope_tables(pos):
    half = 8
    inv = (500000.0 ** (-np.arange(half, dtype=np.float32) * (2.0 / 16))).astype(np.float32)
    ang = pos.astype(np.float32)[None, :] * inv[:, None]
    cos, sin = np.cos(ang).astype(np.float32), np.sin(ang).astype(np.float32)
    T = pos.shape[0]
    Cc = np.ones((64, T), np.float32)
    Ss = np.zeros((64, T), np.float32)
    Cc[0:8] = cos
    Cc[8:16] = cos
    Ss[0:8] = -sin
    Ss[8:16] = sin
    return np.concatenate([Cc, Cc], 0), np.concatenate([Ss, Ss], 0)


def _perm_w_in(w_in):
    u = w_in[:, 0:512]
    q = w_in[:, 512:1024].reshape(1024, 8, 64)
    k = w_in[:, 1024:1152].reshape(1024, 2, 64)
    v = w_in[:, 1152:1280]
    swap = np.arange(64)
    swap[0:8] = np.arange(8, 16)
    swap[8:16] = np.arange(0, 8)
    qt = np.stack([np.concatenate([q[:, t, :], q[:, 4 + t, :]], axis=1) for t in range(4)], axis=1).reshape(1024, 512)
    qs = q[:, :, swap]
    qst = np.stack([np.concatenate([qs[:, t, :], qs[:, 4 + t, :]], axis=1) for t in range(4)], axis=1).reshape(1024, 512)
    ks = k[:, :, swap]
    return np.ascontiguousarray(np.concatenate([u, qt, k.reshape(1024, 128), qst, ks.reshape(1024, 128), v], axis=1))


def _gvecs(inp):
    gv = np.zeros((128, GVN), np.float32)
    for nm, key in [("ffn1_pre_g", "ffn1_pre"), ("ffn1_post_g", "ffn1_post"), ("mix_pre_g", "mix_pre"), ("mix_post_g", "mix_post"),
                    ("xa_pre_g", "xa_pre"), ("xa_post_g", "xa_post"), ("ffn2_pre_g", "ffn2_pre"), ("ffn2_post_g", "ffn2_post"),
                    ("mem_norm_g", "mem_norm")]:
        gv[:, GV[key]:GV[key] + 8] = inp[nm].reshape(8, 128).T
    gv[:, GV["out_g"]:GV["out_g"] + 4] = inp["ssm_out_g"].reshape(4, 128).T
    gv[0:64, GV["out_g"] + 4:GV["out_g"] + 12] = inp["attn_out_g"].reshape(8, 64).T
    gv[:, GV["ssm_d"]:GV["ssm_d"] + 4] = inp["ssm_d"].reshape(4, 128).T
    gv[:, GV["b_glu"]:GV["b_glu"] + 4] = inp["ssm_b_glu"].reshape(4, 128).T
    gv[:, GV["sinks"]:GV["sinks"] + 8] = np.broadcast_to(inp["attn_sinks"].reshape(1, 8), (128, 8))
    return gv


def _masks(has_prev):
    m = np.zeros((128, 5, 128), np.float32)
    k = np.arange(128)[:, None]
    q = np.arange(128)[None, :]
    m[:, 0, :] = (k <= q)
    m[:, 1, :] = (k >= q)
    m[:, 2, :] = (k >= q) * (1.0 if has_prev else 0.0)
    m[:, 3, 0:4] = (np.arange(128)[:, None] >= np.arange(4)[None, :])
    m[0:4, 4, 0:4] = (np.arange(4)[:, None] <= np.arange(4)[None, :])
    return m


_WNAMES = ["ffn1_w_gate", "ffn1_w_up", "ffn1_w_down", "ffn2_w_gate", "ffn2_w_up", "ffn2_w_down", "ssm_w_glu", "w_out",
           "w_mem_q", "w_mem_k", "w_mem_v", "w_mem_o"]


def _ssm_pack(inp):
    pk = np.zeros((128, 16, 67), np.float32)
    def gp(a):
        a = a.reshape((16, 2, 64) + a.shape[2:])
        return np.moveaxis(a, 0, 2).reshape((128, 16) + a.shape[3:])
    pk[:, :, 0] = gp(inp["ssm_a_re"])
    pk[:, :, 1] = gp(inp["ssm_a_im"])
    pk[:, :, 2] = gp(np.broadcast_to(inp["ssm_log_step"][:, None], (32, 64)))
    pk[:, :, 3:19] = gp(inp["ssm_b_re"])
    pk[:, :, 19:35] = gp(inp["ssm_b_im"])
    pk[:, :, 35:51] = gp(np.transpose(inp["ssm_c_re"], (0, 2, 1)))
    pk[:, :, 51:67] = gp(np.transpose(inp["ssm_c_im"], (0, 2, 1)))
    return pk


def make_in_maps(inp, NB, ncores, core_list=None):
    ntok = NB * 512
    shared = {k: np.ascontiguousarray(inp[k], dtype=np.float32) for k in _WNAMES}
    shared["w_in_p"] = _perm_w_in(np.asarray(inp["w_in"], np.float32))
    shared["ident"] = np.eye(128, dtype=np.float32)
    shared["gvecs"] = _gvecs(inp)
    shared["ssm_pack"] = _ssm_pack(inp)
    pos_s = 16384.0 + np.tile(np.arange(4, dtype=np.float32), 16)
    shared["ropeCs"], shared["ropeSs"] = _rope_tables(pos_s)
    shared["sin_dev"] = np.zeros((128, 32), np.float32)
    maps = []
    for c in (core_list if core_list is not None else range(ncores)):
        b, half = c // 2, c % 2
        m = dict(shared)
        start = half * ntok
        xp = np.zeros((128 + ntok, 1024), np.float32)
        if half == 1:
            xp[0:128] = inp["x_prompt"][b, start - 128:start]
        xp[128:] = inp["x_prompt"][b, start:start + ntok]
        m["xp"] = xp
        m["xprev"] = np.ascontiguousarray(inp["x_prompt"][b, 0:ntok]) if half == 1 else np.zeros((ntok, 1024), np.float32)
        pos = np.arange(start - 128, start + ntok, dtype=np.float32)
        m["ropeC"], m["ropeS"] = _rope_tables(pos)
        m["masks"] = _masks(half == 1)
        sel = np.zeros((128, 8), np.float32)
        if half == 1:
            sel[:, c - 1] = 1.0
        m["sel"] = sel
        sb = slice(16 * c, 16 * c + 16)
        m["xs"] = np.ascontiguousarray(inp["x_sample"][sb].reshape(64, 1024))
        m["st_re"] = np.ascontiguousarray(inp["state_ssm_re"][sb])
        m["st_im"] = np.ascontiguousarray(inp["state_ssm_im"][sb])
        m["swa_k"] = np.ascontiguousarray(inp["cache_swa_k"][sb].reshape(16, 128, 128))
        m["swa_v"] = np.ascontiguousarray(inp["cache_swa_v"][sb].reshape(16, 128, 128))
        m["memk"] = np.ascontiguousarray(inp["cache_mem_k"][sb].reshape(16, 256, 1024))
        m["memv"] = np.ascontiguousarray(inp["cache_mem_v"][sb].reshape(16, 256, 1024))
        m["mem_p"] = np.ascontiguousarray(inp["mem_prompt"][b])
        maps.append(m)
    return maps


_CACHE = {}


def kernel(**inputs):
    inp = {k: np.asarray(v) for k, v in inputs.items()}
    NB = 8
    if "kb" not in _CACHE:
        _CACHE["kb"] = assemble({"NB": NB})
    kb = _CACHE["kb"]
    maps = make_in_maps(inp, NB, NCORES)
    res = run_bass_kernel_spmd(kb.nc, maps, core_ids=list(range(NCORES))).results
    f32 = np.float32
    y_p = np.stack([np.concatenate([res[2 * b]["y_p"], res[2 * b + 1]["y_p"]], axis=0) for b in range(4)]).astype(f32)
    y_s = np.concatenate([r["y_s"].reshape(16, 4, 1024) for r in res], axis=0).astype(f32)
    p_sre = np.stack([res[2 * b + 1]["p_sre"] for b in range(4)]).astype(f32)
    p_sim = np.stack([res[2 * b + 1]["p_sim"] for b in range(4)]).astype(f32)
    p_wk = np.stack([res[2 * b + 1]["p_wk"].reshape(128, 2, 64) for b in range(4)]).astype(f32)
    p_wv = np.stack([res[2 * b + 1]["p_wv"].reshape(128, 2, 64) for b in range(4)]).astype(f32)
    pm_k = np.stack([res[2 * b]["pm_k"].reshape(256, 4, 256) for b in range(4)]).astype(f32)
    pm_v = np.stack([res[2 * b]["pm_v"].reshape(256, 4, 256) for b in range(4)]).astype(f32)
    s_sre = np.concatenate([r["s_sre"] for r in res], axis=0).astype(f32)
    s_sim = np.concatenate([r["s_sim"] for r in res], axis=0).astype(f32)
    s_wk = np.concatenate([r["s_wk"].reshape(16, 128, 2, 64) for r in res], axis=0).astype(f32)
    s_wv = np.concatenate([r["s_wv"].reshape(16, 128, 2, 64) for r in res], axis=0).astype(f32)
    return (y_p, y_s, p_sre, p_sim, p_wk, p_wv, pm_k, pm_v, s_sre, s_sim, s_wk, s_wv)
```
